# Optimizing a Trainium2 kernel written in Bass

```python
import math
import jax, jax.numpy as jnp
from jax import lax
import numpy as np

D_MODEL = 1024
BATCH = 4
SEQ = 8192
DEPTH = 2
DEC_BATCH = 128
DEC_SEQ = 4
PAST_LEN = 16384
PAGE_SIZE = 128

HEAD_DIM = 64
MIX_WIDTH = D_MODEL
SSD_HEADS = 8
SSD_INNER = SSD_HEADS * HEAD_DIM
SSD_GROUPS = 2
SSD_STATE = 128
SSD_CONV = 4
SSD_CHUNK = 128
SSD_CONV_DIM = SSD_INNER + 2 * SSD_GROUPS * SSD_STATE
ATT_HEADS = 8
ATT_KV_HEADS = 2
ATT_GROUP = ATT_HEADS // ATT_KV_HEADS
ATT_WIDTH = ATT_HEADS * HEAD_DIM
KV_WIDTH = ATT_KV_HEADS * HEAD_DIM
WINDOW = 128
ATT_BLOCK = 128
D_FF = 2816
FFN_CONV = 3
SPLITS = [SSD_INNER,
          SSD_INNER + SSD_CONV_DIM,
          SSD_INNER + SSD_CONV_DIM + SSD_HEADS,
          SSD_INNER + SSD_CONV_DIM + SSD_HEADS + ATT_WIDTH,
          SSD_INNER + SSD_CONV_DIM + SSD_HEADS + ATT_WIDTH + KV_WIDTH]
IN_DIM = SSD_INNER + SSD_CONV_DIM + SSD_HEADS + ATT_WIDTH + 2 * KV_WIDTH
EPS = 1e-6

kernel_name = 'hybrid_ssd_swa_sink_convffn_step'


def rmsnorm(x, w):
    xf = x.astype(jnp.float32)
    y = xf * lax.rsqrt(jnp.mean(xf * xf, axis=-1, keepdims=True) + EPS)
    return (y * w.astype(jnp.float32)).astype(x.dtype)


def causal_dwconv(h, prev, w, b):
    k = w.shape[0]
    length = h.shape[1]
    hp = jnp.concatenate([prev, h], axis=1)
    out = hp[:, 0:length] * w[0]
    for i in range(1, k):
        out = out + hp[:, i:i + length] * w[i]
    return out + b, hp[:, length:]


def alibi_slopes():
    s = 2.0 ** (-8.0 * np.arange(1, ATT_HEADS + 1) / ATT_HEADS)
    return jnp.asarray(s.reshape(ATT_KV_HEADS, ATT_GROUP), dtype=jnp.float32)


def sink_probs(scores, rel, valid, slopes, sink):
    s = scores.astype(jnp.float32) - slopes[:, :, None, None] * rel.astype(jnp.float32)
    s = jnp.where(valid, s, -jnp.inf)
    sk = sink.astype(jnp.float32)[:, :, None, None]
    m = jnp.maximum(jnp.max(s, axis=-1, keepdims=True), sk)
    p = jnp.exp(s - m)
    return p / (jnp.sum(p, axis=-1, keepdims=True) + jnp.exp(sk - m))


def swa_prompt(q, k, v, slopes, sink):
    b, s = q.shape[:2]
    nq = ATT_BLOCK
    nb = s // nq
    qb = q.reshape(b, nb, nq, ATT_KV_HEADS, ATT_GROUP, HEAD_DIM)
    kb = k.reshape(b, nb, nq, ATT_KV_HEADS, HEAD_DIM)
    vb = v.reshape(b, nb, nq, ATT_KV_HEADS, HEAD_DIM)

    def with_prev(t):
        prev = jnp.concatenate([jnp.zeros_like(t[:, :1]), t[:, :-1]], axis=1)
        return jnp.concatenate([prev, t], axis=2)

    kk, vv = with_prev(kb), with_prev(vb)
    scores = jnp.einsum('bnqkgd,bnskd->bnkgqs', qb, kk) * (HEAD_DIM ** -0.5)
    qi = jnp.arange(nq)[:, None]
    sj = jnp.arange(2 * nq)[None, :]
    rel = qi + nq - sj
    key_pos = jnp.arange(nb)[:, None, None] * nq + sj[None] - nq
    valid = (rel >= 0) & (rel < WINDOW) & (key_pos >= 0)
    probs = sink_probs(scores, rel, valid[:, None, None], slopes, sink).astype(v.dtype)
    out = jnp.einsum('bnkgqs,bnskd->bnqkgd', probs, vv)
    return out.reshape(b, s, ATT_WIDTH)


def swa_sample(q, k, v, k_buf, v_buf, slopes, sink):
    b, t = q.shape[:2]
    w = k_buf.shape[1]
    kk = jnp.concatenate([k_buf, k], axis=1)
    vv = jnp.concatenate([v_buf, v], axis=1)
    qg = q.reshape(b, t, ATT_KV_HEADS, ATT_GROUP, HEAD_DIM)
    scores = jnp.einsum('btkgd,bskd->bkgts', qg, kk) * (HEAD_DIM ** -0.5)
    rel = jnp.arange(t)[:, None] + w - jnp.arange(w + t)[None, :]
    valid = (rel >= 0) & (rel < WINDOW)
    probs = sink_probs(scores, rel, valid, slopes, sink).astype(v.dtype)
    out = jnp.einsum('bkgts,bskd->btkgd', probs, vv).reshape(b, t, ATT_WIDTH)
    return out, kk[:, t:], vv[:, t:]


def ssd_scan(x, dt, a, bm, cm, h0, chunk):
    b, l, h, p = x.shape
    g, n = bm.shape[2:]
    r = h // g
    nc = l // chunk
    f32 = jnp.float32
    xc = x.reshape(b, nc, chunk, h, p)
    dtc = dt.reshape(b, nc, chunk, h)
    bc = bm.reshape(b, nc, chunk, g, n)
    cc = cm.reshape(b, nc, chunk, g, n)
    acum = jnp.cumsum(dtc * a, axis=2)
    acum_h = jnp.moveaxis(acum, 3, 2)
    seg = acum_h[..., :, None] - acum_h[..., None, :]
    causal = jnp.tril(jnp.ones((chunk, chunk), dtype=bool))
    decay = jnp.exp(jnp.where(causal, seg, -jnp.inf))
    cb = jnp.repeat(jnp.einsum('bclgn,bcsgn->bcgls', cc, bc), r, axis=2)
    xdt = xc * dtc[..., None].astype(x.dtype)
    y_diag = jnp.einsum('bchls,bcshp->bclhp', (cb * decay).astype(x.dtype), xdt)
    tail = jnp.exp(acum[:, :, -1:, :] - acum)
    bh = jnp.repeat(bc, r, axis=3)
    states = jnp.einsum('bclhn,bclhp->bchpn', bh * (tail * dtc)[..., None].astype(x.dtype), xc)
    chunk_decay = jnp.exp(acum[:, :, -1, :])

    def step(carry, inp):
        st, dc = inp
        return carry * dc[:, :, None, None] + st, carry

    final, prev = lax.scan(step, h0.astype(f32),
                           (jnp.moveaxis(states.astype(f32), 1, 0), jnp.moveaxis(chunk_decay, 1, 0)))
    prev = jnp.moveaxis(prev, 0, 1)
    ch = jnp.repeat(cc, r, axis=3)
    y_off = jnp.einsum('bclhn,bchpn->bclhp', ch.astype(f32), prev) * jnp.exp(acum)[..., None]
    y = (y_diag.astype(f32) + y_off).astype(x.dtype).reshape(b, l, h, p)
    return y, final.astype(h0.dtype)


def ssd_mixer(z, xbc, dt_raw, conv_prev, h0, lp):
    b, l = z.shape[:2]
    xbc, conv_new = causal_dwconv(xbc, conv_prev, lp['ssd_conv_w'], lp['ssd_conv_b'])
    xbc = jax.nn.silu(xbc)
    xs = xbc[..., :SSD_INNER].reshape(b, l, SSD_HEADS, HEAD_DIM)
    bm = xbc[..., SSD_INNER:SSD_INNER + SSD_GROUPS * SSD_STATE].reshape(b, l, SSD_GROUPS, SSD_STATE)
    cm = xbc[..., SSD_INNER + SSD_GROUPS * SSD_STATE:].reshape(b, l, SSD_GROUPS, SSD_STATE)
    dt = jax.nn.softplus((dt_raw + lp['dt_bias']).astype(jnp.float32))
    a = -jnp.exp(lp['a_log'].astype(jnp.float32))
    chunk = SSD_CHUNK if l % SSD_CHUNK == 0 else l
    y, h_new = ssd_scan(xs, dt, a, bm, cm, h0, chunk)
    y = y + xs * lp['d_skip'][:, None]
    y = y.reshape(b, l, SSD_INNER) * jax.nn.silu(z)
    gs = SSD_INNER // SSD_GROUPS
    y = rmsnorm(y.reshape(b, l, SSD_GROUPS, gs), lp['ssd_norm'].reshape(SSD_GROUPS, gs)).reshape(b, l, SSD_INNER)
    return y, conv_new, h_new


def hybrid_layer(x, lp, ssd_conv_prev, ssm_prev, ffn_conv_prev, kv_buf):
    b, l, _ = x.shape
    hn = rmsnorm(x, lp['norm1'])
    proj = hn @ lp['w_in']
    z, xbc, dt_raw, q, k, v = jnp.split(proj, SPLITS, axis=-1)
    y_ssd, ssd_conv_new, ssm_new = ssd_mixer(z, xbc, dt_raw, ssd_conv_prev, ssm_prev, lp)
    q = rmsnorm(q.reshape(b, l, ATT_HEADS, HEAD_DIM), lp['q_norm'])
    k = rmsnorm(k.reshape(b, l, ATT_KV_HEADS, HEAD_DIM), lp['k_norm'])
    v = v.reshape(b, l, ATT_KV_HEADS, HEAD_DIM)
    slopes = alibi_slopes()
    sink = lp['sinks'].reshape(ATT_KV_HEADS, ATT_GROUP)
    if kv_buf is None:
        y_att = swa_prompt(q, k, v, slopes, sink)
        k_new, v_new = k[:, -WINDOW:], v[:, -WINDOW:]
    else:
        y_att, k_new, v_new = swa_sample(q, k, v, kv_buf[0], kv_buf[1], slopes, sink)
    x = x + jnp.concatenate([y_ssd, y_att], axis=-1) @ lp['w_out']
    hn = rmsnorm(x, lp['norm2'])
    u, ffn_conv_new = causal_dwconv(hn @ lp['w_up'], ffn_conv_prev, lp['ffn_conv_w'], lp['ffn_conv_b'])
    gate, val = jnp.split(u, 2, axis=-1)
    x = x + (jax.nn.silu(gate) * val) @ lp['w_down']
    return x, (ssm_new, ssd_conv_new, k_new, v_new, ffn_conv_new)


def setup_inputs(seed: int = 0) -> dict:
    key = jax.random.key(seed)
    ks = iter(jax.random.split(key, 40))
    f32 = jnp.float32

    def nrm(shape, scale):
        return jax.random.normal(next(ks), shape, f32) * scale

    w_buf = min(WINDOW, PAST_LEN)
    dt0 = jnp.exp(jax.random.uniform(next(ks), (DEPTH, SSD_HEADS), f32, math.log(1e-3), math.log(1e-1)))
    return {
        'x_prompt': nrm((BATCH, SEQ, D_MODEL), 1.0),
        'x_sample': nrm((DEC_BATCH, DEC_SEQ, D_MODEL), 1.0),
        'state_ssm': nrm((DEPTH, DEC_BATCH, SSD_HEADS, HEAD_DIM, SSD_STATE), 0.1),
        'state_ssd_conv': nrm((DEPTH, DEC_BATCH, SSD_CONV - 1, SSD_CONV_DIM), 1.0),
        'cache_swa_k': nrm((DEPTH, DEC_BATCH, w_buf, ATT_KV_HEADS, HEAD_DIM), 1.0),
        'cache_swa_v': nrm((DEPTH, DEC_BATCH, w_buf, ATT_KV_HEADS, HEAD_DIM), 1.0),
        'state_ffn_conv': nrm((DEPTH, DEC_BATCH, FFN_CONV - 1, 2 * D_FF), 1.0),
        'norm1_w': 1.0 + nrm((DEPTH, D_MODEL), 0.01),
        'w_in': nrm((DEPTH, D_MODEL, IN_DIM), D_MODEL ** -0.5),
        'ssd_conv_w': nrm((DEPTH, SSD_CONV, SSD_CONV_DIM), SSD_CONV ** -0.5),
        'ssd_conv_b': nrm((DEPTH, SSD_CONV_DIM), 0.02),
        'dt_bias': dt0 + jnp.log(-jnp.expm1(-dt0)),
        'a_log': jnp.log(jax.random.uniform(next(ks), (DEPTH, SSD_HEADS), f32, 1.0, 16.0)),
        'd_skip': 1.0 + nrm((DEPTH, SSD_HEADS), 0.1),
        'ssd_norm_w': 1.0 + nrm((DEPTH, SSD_INNER), 0.01),
        'q_norm_w': 1.0 + nrm((DEPTH, HEAD_DIM), 0.01),
        'k_norm_w': 1.0 + nrm((DEPTH, HEAD_DIM), 0.01),
        'attn_sinks': nrm((DEPTH, ATT_HEADS), 0.5),
        'w_out': nrm((DEPTH, MIX_WIDTH, D_MODEL), MIX_WIDTH ** -0.5),
        'norm2_w': 1.0 + nrm((DEPTH, D_MODEL), 0.01),
        'w_up': nrm((DEPTH, D_MODEL, 2 * D_FF), D_MODEL ** -0.5),
        'ffn_conv_w': nrm((DEPTH, FFN_CONV, 2 * D_FF), FFN_CONV ** -0.5),
        'ffn_conv_b': nrm((DEPTH, 2 * D_FF), 0.02),
        'w_down': nrm((DEPTH, D_FF, D_MODEL), D_FF ** -0.5),
    }


def reference(x_prompt, x_sample, state_ssm, state_ssd_conv, cache_swa_k, cache_swa_v, state_ffn_conv,
              norm1_w, w_in, ssd_conv_w, ssd_conv_b, dt_bias, a_log, d_skip, ssd_norm_w,
              q_norm_w, k_norm_w, attn_sinks, w_out, norm2_w, w_up, ffn_conv_w, ffn_conv_b, w_down):
    def layer_params(i):
        return dict(norm1=norm1_w[i], w_in=w_in[i], ssd_conv_w=ssd_conv_w[i], ssd_conv_b=ssd_conv_b[i],
                    dt_bias=dt_bias[i], a_log=a_log[i], d_skip=d_skip[i], ssd_norm=ssd_norm_w[i],
                    q_norm=q_norm_w[i], k_norm=k_norm_w[i], sinks=attn_sinks[i], w_out=w_out[i],
                    norm2=norm2_w[i], w_up=w_up[i], ffn_conv_w=ffn_conv_w[i], ffn_conv_b=ffn_conv_b[i],
                    w_down=w_down[i])

    b = x_prompt.shape[0]
    dtype = x_prompt.dtype
    xp, xs = x_prompt, x_sample
    p_states, s_states = [], []
    for i in range(DEPTH):
        lp = layer_params(i)
        xp, st_p = hybrid_layer(xp, lp,
                                jnp.zeros((b, SSD_CONV - 1, SSD_CONV_DIM), dtype),
                                jnp.zeros((b, SSD_HEADS, HEAD_DIM, SSD_STATE), dtype),
                                jnp.zeros((b, FFN_CONV - 1, 2 * D_FF), dtype),
                                None)
        xs, st_s = hybrid_layer(xs, lp, state_ssd_conv[i], state_ssm[i], state_ffn_conv[i],
                                (cache_swa_k[i], cache_swa_v[i]))
        p_states.append(st_p)
        s_states.append(st_s)

    def stacked(states, j):
        return jnp.stack([st[j] for st in states])

    return (xp, xs,
            stacked(p_states, 0), stacked(p_states, 1), stacked(p_states, 2), stacked(p_states, 3), stacked(p_states, 4),
            stacked(s_states, 0), stacked(s_states, 1), stacked(s_states, 2), stacked(s_states, 3), stacked(s_states, 4))
```

```python
import contextlib
import numpy as np
import concourse.bass as bass
import concourse.mybir as mybir
from concourse.bass_utils import run_bass_kernel_spmd

F32 = mybir.dt.float32
BF16 = mybir.dt.bfloat16
ALU = mybir.AluOpType
AF = mybir.ActivationFunctionType

D_MODEL = 1024
DEPTH = 2
NCORES = 8
NSQ = 16
LS = 4
TS = NSQ * LS
TT = 256
D_FF = 2816
NPAIR = 22
EPS = 1e-6
Z0, XBC0, Q0, K0, V0, DT0, IN_DIM = 0, 512, 1536, 2048, 2176, 2304, 2312
P_N1, P_N2, P_CW, P_CB, P_DTB, P_ALOG, P_DSK, P_SNW, P_QN, P_KN, P_SINK, P_FW, P_FB, NPP = \
    0, 8, 16, 48, 56, 64, 72, 76, 80, 81, 82, 86, 218, 262
C_TRI, C_ONES, C_E, NCF = 0, 128, 256, 256 + 2048
B_ID, B_ONES, B_BD, NCB = 0, 128, 256, 384

import os
POOL_ENG = os.environ.get("KPOOL", "pool")
MIXER_IMPL = os.environ.get("KMIXER", "r2")
ENGINES = ("pe", "act", "dve", "pool", "sp")
_UID = [0]


def _uname(name):
    return f"{name}_u{_UID[0]}"

SEM_CHUNK = 2000


class Op:
    __slots__ = ("eng", "fn", "reads", "writes", "dma_key", "group", "deps", "marked",
                 "sem_idx", "sem_val", "id")

    def __init__(self, eng, fn, reads, writes, dma_key, group):
        self.eng, self.fn, self.reads, self.writes = eng, fn, reads, writes
        self.dma_key, self.group = dma_key, group
        self.deps = set()
        self.marked = False
        self.sem_idx = self.sem_val = None


class _Rec:
    def __init__(self, sched, eng, r, w, key, group):
        self._a = (sched, eng, r, w, key, group)

    def __getattr__(self, name):
        sched, eng, r, w, key, group = self._a

        def rec(*args, **kwargs):
            return sched.op(eng, (name, args, kwargs), r, w, dma_key=key, group=group)
        return rec


class Sched:
    def __init__(self):
        self.ops = []

    def op(self, eng, fn, reads=(), writes=(), dma_key=None, group=False):
        self.nrec = getattr(self, "nrec", 0) + 1
        cut = int(os.environ.get("KCUT", "0"))
        if cut and self.nrec > cut:
            return None
        o = Op(eng, fn, tuple(reads), tuple(writes), dma_key, group)
        o.id = len(self.ops)
        self.ops.append(o)
        return o

    def pe(self, r=(), w=()): return _Rec(self, "pe", r, w, None, False)
    def act(self, r=(), w=()): return _Rec(self, "act", r, w, None, False)
    def dve(self, r=(), w=()): return _Rec(self, "dve", r, w, None, False)
    def pool(self, r=(), w=()): return _Rec(self, POOL_ENG, r, w, None, False)

    def dma(self, key, r=(), w=(), eng="sp", group=False):
        return _Rec(self, eng, r, w, key, group)

    def analyze(self):
        last_w, readers, last_dma = {}, {}, {}
        ops = self.ops
        for o in ops:
            deps = set()
            for r in o.reads:
                if r in last_w:
                    deps.add(last_w[r])
                if isinstance(r, tuple) and r and r[0] == "PB":
                    deps.update(d for d in readers.get(r, ()) if ops[d].eng != o.eng)
            for w in o.writes:
                if w in last_w:
                    po = ops[last_w[w]]
                    if not (o.group and po.group and o.dma_key is not None and po.dma_key == o.dma_key):
                        deps.add(po.id)
                deps.update(readers.get(w, ()))
            if o.dma_key is not None:
                if not o.group and o.dma_key in last_dma:
                    deps.add(last_dma[o.dma_key])
                last_dma[o.dma_key] = o.id
            deps.discard(o.id)
            if o.eng == "pe" and o.dma_key is None:
                deps = {d for d in deps if not (ops[d].eng == "pe" and ops[d].dma_key is None)}
            o.deps = deps
            for r in o.reads:
                readers.setdefault(r, set()).add(o.id)
            for w in o.writes:
                last_w[w] = o.id
                readers[w] = set()
        for o in ops:
            for d in o.deps:
                ops[d].marked = True
        cnt = {e: 0 for e in ENGINES}
        dma_cum = {}
        for o in ops:
            if o.dma_key is not None:
                dma_cum[o.dma_key] = dma_cum.get(o.dma_key, 0) + 16
                o.sem_idx = ("dma", o.dma_key)
                o.sem_val = dma_cum[o.dma_key]
            elif o.marked:
                k = cnt[o.eng]
                o.sem_idx = (o.eng, k // SEM_CHUNK)
                o.sem_val = (k % SEM_CHUNK) + 1
                cnt[o.eng] = k + 1
        self.dma_final = dma_cum
        self.n_eng_sems = {e: (cnt[e] + SEM_CHUNK - 1) // SEM_CHUNK for e in ENGINES}
        return self

    def emit(self, nc):
        ops = self.ops
        handles = {}
        for e in ENGINES:
            for i in range(self.n_eng_sems[e]):
                handles[(e, i)] = nc.alloc_semaphore(name=_uname(f"s_{e}_{i}"))
        for j, k in enumerate(self.dma_final.keys()):
            handles[("dma", k)] = nc.alloc_semaphore(name=_uname(f"d_{j}"))
        by_eng = {e: [o for o in ops if o.eng == e] for e in ENGINES}
        dma_final = self.dma_final

        def make(engname):
            def body(eng):
                waited = {}
                for o in by_eng[engname]:
                    need = {}
                    for d in o.deps:
                        po = ops[d]
                        if need.get(po.sem_idx, 0) < po.sem_val:
                            need[po.sem_idx] = po.sem_val
                    for key, v in need.items():
                        if waited.get(key, 0) < v:
                            eng.wait_ge(handles[key], v)
                            waited[key] = v
                    name, args, kwargs = o.fn
                    inst = getattr(eng, name)(*args, **kwargs)
                    if o.dma_key is not None:
                        inst.then_inc(handles[o.sem_idx], 16)
                    elif o.marked:
                        inst.then_inc(handles[o.sem_idx], 1)
                if engname == "sp":
                    for k, v in dma_final.items():
                        eng.wait_ge(handles[("dma", k)], v)
            return body

        with nc.Block() as block:
            block.tensor(make("pe"))
            block.scalar(make("act"))
            block.vector(make("dve"))
            block.gpsimd(make("pool"))
            block.sync(make("sp"))
        nc.clear_and_free_semaphores(list(handles.values()))
        nc.all_engine_barrier()


def tile_list(TP):
    tiles = []
    for i in range(TP // TT):
        tiles.append(dict(col0=i * TT, nseq=1, L=TT, sample=False, first=(i == 0),
                          last=(i == TP // TT - 1)))
    tiles.append(dict(col0=TP, nseq=NSQ, L=LS, sample=True, first=False, last=False))
    return tiles


def build_nc(TP, n_stages=4, debug=False):
    TTOT = TP + TS
    nc = bass.Bass("TRN2", target_bir_lowering=False)

    def din(name, shape):
        return nc.dram_tensor(name, list(shape), F32, kind="ExternalInput").ap()

    def dout(name, shape):
        return nc.dram_tensor(name, list(shape), F32, kind="ExternalOutput").ap()

    D = {}
    D["xT"] = din("xT", [D_MODEL, TTOT])
    D["w_in"] = din("w_in", [DEPTH, D_MODEL, IN_DIM])
    D["w_out"] = din("w_out", [DEPTH, D_MODEL, D_MODEL])
    D["w_up"] = din("w_up", [DEPTH, D_MODEL, 2 * D_FF])
    D["w_down"] = din("w_down", [DEPTH, D_FF, D_MODEL])
    D["pp"] = din("pp", [DEPTH, 128, NPP])
    D["cstf"] = din("cstf", [128, NCF])
    D["cstb"] = din("cstb", [128, NCB])
    D["s_ssm"] = din("s_ssm", [DEPTH, NSQ, 128, 512])
    D["s_conv"] = din("s_conv", [DEPTH, 128, 8, NSQ, 3])
    D["s_kT"] = din("s_kT", [DEPTH, NSQ, 128, 128])
    D["s_v"] = din("s_v", [DEPTH, NSQ, 128, 128])
    D["s_ffn"] = din("s_ffn", [DEPTH, 128, 44, NSQ, 2])
    D["yT"] = dout("yT", [D_MODEL, TTOT])
    D["p_ssm"] = dout("p_ssm", [DEPTH, 128, 512])
    D["p_conv"] = dout("p_conv", [DEPTH, 128, 8, 3])
    D["p_kT"] = dout("p_kT", [DEPTH, 128, 128])
    D["p_v"] = dout("p_v", [DEPTH, 128, 128])
    D["p_ffn"] = dout("p_ffn", [DEPTH, 128, 44, 2])
    D["o_ssm"] = dout("o_ssm", [DEPTH, NSQ, 128, 512])
    D["o_conv"] = dout("o_conv", [DEPTH, 128, 8, NSQ, 3])
    D["o_kT"] = dout("o_kT", [DEPTH, NSQ, 128, 128])
    D["o_v"] = dout("o_v", [DEPTH, NSQ, 128, 128])
    D["o_ffn"] = dout("o_ffn", [DEPTH, 128, 44, NSQ, 2])
    if debug:
        D["xmid"] = dout("xmid", [DEPTH, D_MODEL, TTOT])
        D["x1"] = dout("x1", [D_MODEL, TTOT])
    else:
        D["xmid"] = nc.dram_tensor("xmid", [DEPTH, D_MODEL, TTOT], F32, kind="Internal").ap()
        D["x1"] = nc.dram_tensor("x1", [D_MODEL, TTOT], F32, kind="Internal").ap()

    tiles = tile_list(TP)
    stage = 0
    for l in range(DEPTH):
        xsrc = D["xT"] if l == 0 else D["x1"]
        if stage < n_stages:
            (mixer_stage_r1 if MIXER_IMPL == "r1" else mixer_stage)(nc, l, tiles, xsrc, D["xmid"][l], D)
        stage += 1
        xdst = D["x1"] if l == 0 else D["yT"]
        if stage < n_stages:
            ffn_stage(nc, l, tiles, D["xmid"][l], xdst, D)
        stage += 1
    return nc


def kp(ap):
    return ap.rearrange("(k p) t -> p k t", p=128)


def load_common(nc, S, st, l, D, want_bf=True):
    sb = lambda name, shape, dt: st.enter_context(nc.sbuf_tensor(_uname(name), shape, dt))
    C = {}
    C["pp"] = sb("pp", [128, NPP], F32)
    C["cb"] = sb("cb", [128, NCB], BF16)
    S.dma("pp", w=["pp"]).dma_start(out=C["pp"][:], in_=D["pp"][l])
    S.dma("cb", w=["cb"], eng="pool").dma_start(out=C["cb"][:], in_=D["cstb"])
    return C


def rmsnorm_tile(nc, S, xt_ap, xkey, W, Cst, B, PB, pkey, ncol, hn, sq, std, rstd, TTt):
    ones = Cst["cb"][:, B_ONES:B_ONES + 128]
    S.act(r=[xkey], w=["sq"]).activation(out=sq[:, :, 0:TTt], in_=xt_ap, func=AF.Square)
    for k in range(8):
        S.pe(r=["sq", "cb"], w=[pkey]).matmul(PB[:, 0:TTt], lhsT=ones, rhs=sq[:, k, 0:TTt], start=(k == 0), stop=(k == 7))
    S.act(r=[pkey, "eps"], w=["std"]).activation(out=std[:, 0:TTt], in_=PB[:, 0:TTt], func=AF.Sqrt, bias=Cst["eps"][:, 0:1],
                                 scale=1.0 / D_MODEL)
    S.dve(r=["std"], w=["rstd"]).reciprocal(out=rstd[:, 0:TTt], in_=std[:, 0:TTt])
    for k in range(8):
        S.dve(r=[xkey, "rstd", "pp"], w=[("hn", k)]).scalar_tensor_tensor(out=hn[:, k, 0:TTt], in0=xt_ap[:, k, :],
                                                    scalar=Cst["pp"][:, ncol + k:ncol + k + 1], in1=rstd[:, 0:TTt],
                                                    op0=ALU.mult, op1=ALU.mult)


def mixer_stage_r1(nc, l, tiles, xsrc, xdst, D):
    _UID[0] += 1
    with contextlib.ExitStack() as st:
        sb = lambda name, shape, dt: st.enter_context(nc.sbuf_tensor(_uname(name), shape, dt))
        S = Sched()
        C = load_common(nc, S, st, l, D)
        pp, cb = C["pp"], C["cb"]
        cf = sb("cf", [128, NCF], F32)
        S.dma("cf", w=["cf"]).dma_start(out=cf[:], in_=D["cstf"])
        TRI = cf[:, C_TRI:C_TRI + 128]
        ONESF = cf[:, C_ONES:C_ONES + 128]
        ETAB = cf[:, C_E:C_E + 2048].rearrange("p (h b q) -> p h b q", h=8, b=2)
        IDB = cb[:, B_ID:B_ID + 128]
        ONESB = cb[:, B_ONES:B_ONES + 128]
        BDB = cb[:, B_BD:B_BD + 128]
        eps_t = sb("eps_t", [128, 1], F32)
        S.dve(w=["eps"]).memset(eps_t[:], EPS)
        C["eps"] = eps_t

        Win = sb("Win", [128, 8, IN_DIM], BF16)
        Wout = sb("Wout", [128, 8, D_MODEL], BF16)
        for k in range(8):
            S.dma("win", w=["Win"], eng="pool", group=True).dma_start(out=Win[:, k, :], in_=D["w_in"][l, k * 128:(k + 1) * 128, :])
        for k in range(8):
            S.dma("wout", w=["Wout"], eng="pool", group=True).dma_start(out=Wout[:, k, :], in_=D["w_out"][l, k * 128:(k + 1) * 128, :])

        a_row = sb("a_row", [128, 8], F32)
        esink = sb("esink", [128, 4], F32)
        wq8 = sb("wq8", [128, 1], F32)
        S.act(r=["pp"], w=["a_row"]).activation(out=a_row[:], in_=pp[:, P_ALOG:P_ALOG + 8], func=AF.Exp)
        S.dve(r=["a_row"], w=["a_row"]).tensor_scalar(out=a_row[:], in0=a_row[:], scalar1=-1.0, scalar2=None, op0=ALU.mult)
        S.act(r=["pp"], w=["esink"]).activation(out=esink[:], in_=pp[:, P_SINK:P_SINK + 4], func=AF.Exp)
        S.dve(r=["pp"], w=["wq8"]).tensor_scalar(out=wq8[:], in0=pp[:, P_QN:P_QN + 1], scalar1=0.125, scalar2=None,
                                        op0=ALU.mult)

        xt = [sb(f"xt{i}", [128, 8, TT], F32) for i in range(2)]
        sq = sb("sq", [128, 8, TT], BF16)
        hn = sb("hn", [128, 8, TT], BF16)
        std = sb("std", [128, TT], F32)
        rstd = sb("rstd", [128, TT], F32)
        sz = sb("sz", [128, 4, TT], F32)
        pcb = sb("pcb", [128, 8, TT + 4], F32)
        pcar = sb("pcar", [128, 8, 3], F32)
        cv = [sb(f"cv{i}", [128, TT], F32) for i in range(2)]
        xbc = sb("xbc", [128, 8, TT], BF16)
        qk32 = sb("qk32", [128, 5, TT], F32)
        qn = sb("qn", [128, 4, TT], BF16)
        kn32 = sb("kn32", [128, TT], F32)
        kbuf = sb("kbuf", [128, 128 + TT], BF16)
        kcar = sb("kcar", [128, 128], BF16)
        kcache = [sb(f"kcache{i}", [128, 128], BF16) for i in range(2)]
        vcache = [sb(f"vcache{i}", [128, 128], BF16) for i in range(2)]
        yg = sb("yg", [128, 4, TT], F32)
        cssd = sb("cssd", [128, 4, TT], BF16)
        catt = sb("catt", [128, 4, TT], BF16)
        dtb = sb("dtb", [128, 8], F32)
        e1 = sb("e1", [128, 8], F32)
        dt = sb("dt", [128, 8], F32)
        dA = sb("dA", [128, 8], F32)
        acum = sb("acum", [128, 8], F32)
        diff = sb("diff", [128, 8], F32)
        tail = sb("tail", [128, 8], F32)
        cd = sb("cd", [128, 8], F32)
        dtt = sb("dtt", [128, 8], F32)
        dA_rep = sb("dA_rep", [128, 8, 128], F32)
        xtm = sb("xtm", [128, 768], BF16)
        xdt = sb("xdt", [128, 8, 64], BF16)
        xst = sb("xst", [128, 8, 64], BF16)
        cbm = sb("cbm", [128, 2, 128], F32)
        seg = sb("seg", [128, 8, 128], F32)
        MT = sb("MT", [128, 8, 128], BF16)
        eb = sb("eb", [128, 8, 128], F32)
        Cp = sb("Cp", [128, 8, 128], BF16)
        ygt = sb("ygt", [128, 4, 128], F32)
        Sst = [sb(f"Sst{i}", [128, 8, 64], F32) for i in range(2)]
        Sbf = sb("Sbf", [128, 8, 64], BF16)
        vb = [sb(f"vb{i}", [128, 128], BF16) for i in range(3)]
        v32 = sb("v32", [128, 128], F32)
        pexp = sb("pexp", [128, 4, 2, 128], F32)
        PT = sb("PT", [128, 4, 2, 128], BF16)
        den = sb("den", [128, 4, 128], F32)
        rden = sb("rden", [128, 4, 128], F32)

        PB = [st.enter_context(nc.psum_tensor(_uname(f"pb{i}"), [128, 512], F32)) for i in range(8)]
        small = PB[2]
        tr_ps = PB[3][:].bitcast(BF16)
        ns_ps = PB[3]
        acb_ps = [PB[4], PB[5]]
        sc_ps = [PB[4], PB[5]]
        y_ps = PB[6]
        o_ps = PB[6]
        den_ps = PB[7]
        pbrot = [0]

        def next_pb():
            i = pbrot[0] % 2
            pbrot[0] += 1
            return PB[i], ("PB", i)

        S.dve(w=["pcar"]).memset(pcar[:], 0.0)
        S.dve(w=[("S", 0)]).memset(Sst[0][:], 0.0)
        S.pool(w=["Sbf"]).memset(Sbf[:], 0.0)
        S.pool(w=["kcar"]).memset(kcar[:], 0.0)

        gchunk = [0]

        for ti, tl in enumerate(tiles):
            col0, nseq, L, sample = tl["col0"], tl["nseq"], tl["L"], tl["sample"]
            TTt = nseq * L
            Lc = min(L, 128)
            nj = L // Lc
            slot = ti % 2
            X = xt[slot]
            xkey = ("xt", slot)
            xap = X[:, :, 0:TTt]
            S.dma(("xld", slot), w=[xkey]).dma_start(out=X[:, :, 0:TTt], in_=kp(xsrc)[:, :, col0:col0 + TTt])
            pbt, pbk = next_pb()
            rmsnorm_tile(nc, S, xap, xkey, None, C, None, pbt, pbk, P_N1, hn, sq, std, rstd, TTt)
            hnk = [("hn", k) for k in range(8)]
            pcv = pcb[:, :, 0:nseq * (3 + L)].rearrange("p c (s t) -> p c s t", s=nseq)
            if sample:
                for c in range(8):
                    S.dma("sconv", r=[], w=["pcb_prev"], group=True).dma_start(out=pcv[:, c, :, 0:3], in_=D["s_conv"][l, :, c])
            else:
                S.pool(r=["pcar"], w=["pcb_prev"]).tensor_copy(out=pcv[:, :, 0, 0:3], in_=pcar[:])
            chunks = [("z", c, Z0 + c * 128) for c in range(4)] + [("x", c, XBC0 + c * 128) for c in range(8)] + \
                     [("q", c, Q0 + c * 128) for c in range(4)] + [("q", 4, K0)]
            for kind, c, wc in chunks:
                pbt, pbk = next_pb()
                for k in range(8):
                    S.pe(r=["Win"] + hnk, w=[pbk]).matmul(pbt[:, 0:TTt], lhsT=Win[:, k, wc:wc + 128],
                                                                 rhs=hn[:, k, 0:TTt], start=(k == 0), stop=(k == 7))
                if kind == "z":
                    S.act(r=[pbk], w=[("sz", c)]).activation(out=sz[:, c, 0:TTt], in_=pbt[:, 0:TTt], func=AF.Silu)
                elif kind == "x":
                    S.act(r=[pbk], w=[("pcb", c)]).activation(
                        out=pcv[:, c, :, 3:3 + L], in_=pbt[:, 0:TTt].rearrange("p (s t) -> p s t", s=nseq), func=AF.Copy)
                else:
                    S.dve(r=[pbk], w=[("qk32", c)]).tensor_copy(out=qk32[:, c, 0:TTt], in_=pbt[:, 0:TTt])
            for c in range(8):
                t = cv[c % 2]
                tk = ("cv", c % 2)
                t3 = t[:, 0:TTt].rearrange("p (s t) -> p s t", s=nseq)
                cw = lambda j, c=c: pp[:, P_CW + c * 4 + j:P_CW + c * 4 + j + 1]
                S.dve(r=[("pcb", c), "pp"], w=[tk]).tensor_scalar(
                    out=t3, in0=pcv[:, c, :, 3:3 + L], scalar1=cw(3), scalar2=pp[:, P_CB + c:P_CB + c + 1],
                    op0=ALU.mult, op1=ALU.add)
                for j in (2, 1, 0):
                    S.dve(r=[("pcb", c), "pcb_prev", "pp", tk], w=[tk]).scalar_tensor_tensor(
                        out=t3, in0=pcv[:, c, :, j:j + L], scalar=cw(j), in1=t3, op0=ALU.mult, op1=ALU.add)
                S.act(r=[tk], w=[("xbc", c)]).activation(out=xbc[:, c, 0:TTt], in_=t[:, 0:TTt], func=AF.Silu)
            allpcb = [("pcb", c) for c in range(8)]
            if sample:
                for c in range(8):
                    S.dma("oconv", r=allpcb + ["pcb_prev"], w=["oconv"], group=True).dma_start(out=D["o_conv"][l, :, c], in_=pcv[:, c, :, L:L + 3])
            else:
                S.pool(r=allpcb + ["pcb_prev"],
                       w=["pcar"]).tensor_copy(out=pcar[:], in_=pcv[:, :, 0, L:L + 3])
                if tl["last"]:
                    S.dma("pconv", r=["pcar"]).dma_start(out=D["p_conv"][l], in_=pcar[:])
            if not sample:
                S.pool(r=["kcar"], w=["kbuf_prev"]).tensor_copy(out=kbuf[:, 0:128], in_=kcar[:])
            for c in range(5):
                pbt, pbk = next_pb()
                S.act(r=[("qk32", c)], w=["sq"]).activation(out=sq[:, 0, 0:TTt], in_=qk32[:, c, 0:TTt], func=AF.Square)
                S.pe(r=["sq", "cb"], w=[pbk]).matmul(pbt[:, 0:TTt], lhsT=BDB, rhs=sq[:, 0, 0:TTt], start=True, stop=True)
                S.act(r=[pbk, "eps"], w=["std"]).activation(out=std[:, 0:TTt], in_=pbt[:, 0:TTt], func=AF.Sqrt,
                                                      bias=eps_t[:, 0:1], scale=1.0 / 64)
                S.dve(r=["std"], w=["rstd"]).reciprocal(out=rstd[:, 0:TTt], in_=std[:, 0:TTt])
                if c < 4:
                    S.dve(r=[("qk32", c), "wq8", "rstd"], w=[("qn", c)]).scalar_tensor_tensor(out=qn[:, c, 0:TTt], in0=qk32[:, c, 0:TTt], scalar=wq8[:, 0:1],
                                                                in1=rstd[:, 0:TTt], op0=ALU.mult, op1=ALU.mult)
                else:
                    S.dve(r=[("qk32", 4), "pp", "rstd"], w=["kn32"]).scalar_tensor_tensor(out=kn32[:, 0:TTt], in0=qk32[:, 4, 0:TTt],
                                                           scalar=pp[:, P_KN:P_KN + 1], in1=rstd[:, 0:TTt],
                                                           op0=ALU.mult, op1=ALU.mult)
                    S.act(r=["kn32"], w=["kbuf"]).activation(out=kbuf[:, 128:128 + TTt], in_=kn32[:, 0:TTt], func=AF.Copy)
            if not sample:
                S.pool(r=["kbuf", "kbuf_prev"], w=["kcar"]).tensor_copy(out=kcar[:], in_=kbuf[:, TTt:TTt + 128])
                if tl["last"]:
                    S.dma("pk", r=["kn32"]).dma_start(out=D["p_kT"][l], in_=kn32[:, TTt - 128:TTt])
            else:
                for s in range(nseq):
                    S.dma("okc", group=True, w=["okc"]).dma_start(out=D["o_kT"][l, s, :, 0:124], in_=D["s_kT"][l, s, :, 4:128])
                    S.dma("okn", group=True, r=["kn32"], w=["okn"]).dma_start(out=D["o_kT"][l, s, :, 124:128], in_=kn32[:, s * LS:(s + 1) * LS])
                    S.dma("ovc", group=True, w=["ovc"]).dma_start(out=D["o_v"][l, s, 0:124, :], in_=D["s_v"][l, s, 4:128, :])

            for s in range(nseq):
                if sample:
                    sslot = s % 2
                    Scur, skey = Sst[sslot], ("S", sslot)
                    S.dma(("sld", sslot), w=[skey]).dma_start(out=Scur[:].rearrange("p h d -> p (h d)"),
                                                                in_=D["s_ssm"][l, s])
                    S.act(r=[skey], w=["Sbf"]).activation(out=Sbf[:], in_=Scur[:], func=AF.Copy)
                    kc, vc = kcache[sslot], vcache[sslot]
                    S.dma(("kcl", sslot),
                          w=[("kcache", sslot)], eng="pool").dma_start(out=kc[:], in_=D["s_kT"][l, s])
                    S.dma(("vcl", sslot),
                          w=[("vcache", sslot)], eng="pool").dma_start(out=vc[:], in_=D["s_v"][l, s])
                else:
                    Scur, skey = Sst[0], ("S", 0)
                for j in range(nj):
                    g = gchunk[0]
                    gchunk[0] += 1
                    c0 = s * L + j * Lc
                    cs = slice(c0, c0 + Lc)
                    has_prev = sample or not (tl["first"] and j == 0)
                    for k in range(8):
                        S.pe(r=["Win"] + hnk, w=[("PB", 2)]).matmul(small[0:Lc, 0:8], lhsT=hn[:, k, cs], rhs=Win[:, k, DT0:DT0 + 8],
                                                            start=(k == 0), stop=(k == 7))
                    for k in range(8):
                        S.pe(r=["Win"] + hnk, w=[("PB", 7)]).matmul(PB[7][0:Lc, 0:128], lhsT=hn[:, k, cs], rhs=Win[:, k, V0:V0 + 128],
                                                            start=(k == 0), stop=(k == 7))
                    S.dve(r=[("PB", 2), "pp"], w=["dtb"]).tensor_tensor(out=dtb[0:Lc, :], in0=small[0:Lc, 0:8], in1=pp[0:Lc, P_DTB:P_DTB + 8],
                                                    op=ALU.add)
                    S.act(r=["dtb"], w=["e1"]).activation(out=e1[0:Lc, :], in_=dtb[0:Lc, :], func=AF.Exp)
                    S.act(r=["e1"], w=["dt"]).activation(out=dt[0:Lc, :], in_=e1[0:Lc, :], func=AF.Ln, bias=1.0)
                    S.dve(r=["dt", "a_row"], w=["dA"]).tensor_tensor(out=dA[0:Lc, :], in0=dt[0:Lc, :], in1=a_row[0:Lc, :], op=ALU.mult)
                    S.pe(r=["dA", "cf"], w=[("PB", 2)]).matmul(small[0:Lc, 144:152], lhsT=TRI[0:Lc, 0:Lc], rhs=dA[0:Lc, :], start=True, stop=True)
                    S.pe(r=["dA", "cf"], w=[("PB", 2)]).matmul(small[:, 152:160], lhsT=ONESF[0:Lc, :], rhs=dA[0:Lc, :], start=True, stop=True)
                    S.dve(r=[("PB", 2)], w=["acum"]).tensor_copy(out=acum[0:Lc, :], in_=small[0:Lc, 144:152])
                    S.dve(r=[("PB", 2), "acum"], w=["diff"]).tensor_tensor(out=diff[0:Lc, :], in0=small[0:Lc, 152:160], in1=acum[0:Lc, :],
                                                    op=ALU.subtract)
                    S.act(r=["diff"], w=["tail"]).activation(out=tail[0:Lc, :], in_=diff[0:Lc, :], func=AF.Exp)
                    S.act(r=[("PB", 2)], w=["cd"]).activation(out=cd[:, :], in_=small[:, 152:160], func=AF.Exp)
                    S.dve(r=["dt", "tail"], w=["dtt"]).tensor_tensor(out=dtt[0:Lc, :], in0=dt[0:Lc, :], in1=tail[0:Lc, :], op=ALU.mult)
                    S.dve(r=["dA"], w=["dA_rep"]).tensor_copy(out=dA_rep[0:Lc, :, :], in_=dA[0:Lc, :].unsqueeze(2).to_broadcast([Lc, 8, 128]))
                    for h in range(8):
                        S.pe(r=["dA_rep", "cf"], w=[("PB", 4 + h // 4)]).matmul(acb_ps[h // 4][:, (h % 4) * 128:(h % 4) * 128 + Lc],
                                                     lhsT=dA_rep[0:Lc, h, :], rhs=TRI[0:Lc, 0:Lc], start=True, stop=True)
                    for i in range(6):
                        S.pe(r=[("xbc", i), "cb"], w=[("PB", 3)]).transpose(tr_ps[0:Lc, i * 128:(i + 1) * 128], xbc[:, i, cs], IDB)
                    S.act(r=[("PB", 3)], w=["xtm"]).activation(out=xtm[0:Lc, :], in_=tr_ps[0:Lc, 0:768], func=AF.Copy)
                    xtm3 = xtm[0:Lc, 0:512].rearrange("p (h d) -> p h d", h=8)
                    S.dve(r=["xtm", "dt"], w=["xdt"]).tensor_tensor(out=xdt[0:Lc], in0=xtm3,
                                                               in1=dt[0:Lc, :].unsqueeze(2).to_broadcast([Lc, 8, 64]), op=ALU.mult)
                    S.dve(r=["xtm", "dtt"], w=["xst"]).tensor_tensor(out=xst[0:Lc], in0=xtm3,
                                                               in1=dtt[0:Lc, :].unsqueeze(2).to_broadcast([Lc, 8, 64]), op=ALU.mult)
                    for gidx in range(2):
                        S.pe(r=[("xbc", 4 + gidx), ("xbc", 6 + gidx)], w=[("PB", 2)]).matmul(small[0:Lc, 160 + gidx * 128:160 + gidx * 128 + Lc],
                                                                  lhsT=xbc[:, 4 + gidx, cs], rhs=xbc[:, 6 + gidx, cs],
                                                                  start=True, stop=True)
                    S.dve(r=[("PB", 2), "cf"], w=["cbm"]).tensor_tensor(out=cbm[0:Lc, :, 0:Lc],
                                                    in0=small[0:Lc, 160:416].rearrange("p (g q) -> p g q", g=2)[:, :, 0:Lc],
                                                    in1=TRI[0:Lc, 0:Lc].unsqueeze(1).to_broadcast([Lc, 2, Lc]), op=ALU.mult)
                    for h in range(8):
                        S.dve(r=[("PB", 4 + h // 4), "acum"], w=["seg"]).tensor_scalar(out=seg[0:Lc, h, 0:Lc],
                                                             in0=acb_ps[h // 4][0:Lc, (h % 4) * 128:(h % 4) * 128 + Lc],
                                                             scalar1=acum[0:Lc, h:h + 1], scalar2=0.0,
                                                             op0=ALU.subtract, op1=ALU.min)
                    S.act(r=["seg"], w=["seg"]).activation(out=seg[0:Lc, :, 0:Lc], in_=seg[0:Lc, :, 0:Lc], func=AF.Exp)
                    S.dve(r=["seg", "cbm"], w=["MT"]).tensor_tensor(out=MT[0:Lc, :, 0:Lc].rearrange("p (g r) q -> p g r q", g=2),
                                                    in0=seg[0:Lc, :, 0:Lc].rearrange("p (g r) q -> p g r q", g=2),
                                                    in1=cbm[0:Lc, :, 0:Lc].unsqueeze(2).to_broadcast([Lc, 2, 4, Lc]), op=ALU.mult)
                    for hh in range(2):
                        S.act(r=[("PB", 4 + hh)], w=[("eb", hh)]).activation(
                            out=eb[:, hh * 4:(hh + 1) * 4, 0:Lc],
                            in_=acb_ps[hh][:, :].rearrange("p (r q) -> p r q", r=4)[:, :, 0:Lc], func=AF.Exp)
                        S.dve(r=[("eb", hh), ("xbc", 6 + hh)], w=[("Cp", hh)]).tensor_tensor(
                            out=Cp[:, hh * 4:(hh + 1) * 4, 0:Lc], in0=eb[:, hh * 4:(hh + 1) * 4, 0:Lc],
                            in1=xbc[:, 6 + hh, cs].unsqueeze(1).to_broadcast([128, 4, Lc]), op=ALU.mult)
                    for h in range(8):
                        pr = (h % 2) * 64
                        yo = y_ps[pr:pr + 64, (h // 2) * 128:(h // 2) * 128 + Lc]
                        S.pe(r=["xdt", "MT"], w=[("PB", 6)]).matmul(yo, lhsT=xdt[0:Lc, h, :], rhs=MT[0:Lc, h, 0:Lc], start=True, stop=False)
                        S.pe(r=["Sbf", ("Cp", h // 4)], w=[("PB", 6)]).matmul(yo, lhsT=Sbf[:, h, :], rhs=Cp[:, h, 0:Lc], start=False, stop=True)
                    for c in range(4):
                        S.dve(r=[("xbc", c), "pp", ("PB", 6)], w=["ygt"]).scalar_tensor_tensor(
                            out=ygt[:, c, 0:Lc], in0=xbc[:, c, cs], scalar=pp[:, P_DSK + c:P_DSK + c + 1],
                            in1=y_ps[:, c * 128:c * 128 + Lc], op0=ALU.mult, op1=ALU.add)
                    S.dve(r=["ygt"] + [("sz", c) for c in range(4)], w=[("yg", g)]).tensor_tensor(out=yg[:, :, cs], in0=ygt[:, :, 0:Lc], in1=sz[:, :, cs], op=ALU.mult)
                    for gidx in range(2):
                        S.pe(r=["xtm", "xst"], w=[("PB", 3)]).matmul(ns_ps[:, gidx * 256:(gidx + 1) * 256],
                                                           lhsT=xtm[0:Lc, 512 + gidx * 128:512 + (gidx + 1) * 128],
                                                           rhs=xst[0:Lc, gidx * 4:(gidx + 1) * 4, :], start=True, stop=True)
                    S.dve(r=[skey, "cd"], w=[skey]).tensor_tensor(out=Scur[:], in0=Scur[:],
                                                               in1=cd[:, :].unsqueeze(2).to_broadcast([128, 8, 64]), op=ALU.mult)
                    S.dve(r=[skey, ("PB", 3)], w=[skey]).tensor_tensor(out=Scur[:], in0=Scur[:],
                                                               in1=ns_ps[:, :].rearrange("p (h d) -> p h d", h=8), op=ALU.add)
                    if sample:
                        S.dma(("sst", s % 2), r=[skey]).dma_start(out=D["o_ssm"][l, s], in_=Scur[:].rearrange("p h d -> p (h d)"))
                    else:
                        S.act(r=[skey], w=["Sbf"]).activation(out=Sbf[:], in_=Scur[:], func=AF.Copy)
                        if tl["last"] and j == nj - 1:
                            S.dma("pssm", r=[skey]).dma_start(out=D["p_ssm"][l], in_=Scur[:].rearrange("p h d -> p (h d)"))

                    vown = vb[g % 3]
                    vkey = ("vb", g % 3)
                    S.act(r=[("PB", 7)], w=[vkey]).activation(out=vown[0:Lc, :], in_=PB[7][0:Lc, 0:128], func=AF.Copy)
                    if sample:
                        S.dve(r=[("PB", 7)], w=["v32"]).tensor_copy(out=v32[0:Lc, :], in_=PB[7][0:Lc, 0:128])
                        S.dma("ovn", r=["v32"]).dma_start(out=D["o_v"][l, s, 124:128, :], in_=v32[0:Lc, :])
                        kprev, kpkey = kcache[s % 2], ("kcache", s % 2)
                        vprev, vpkey = vcache[s % 2], ("vcache", s % 2)
                        kown = kbuf[:, 128 + c0:128 + c0 + Lc]
                        kpre = kprev[:, :]
                    else:
                        if tl["last"] and j == nj - 1:
                            S.dve(r=[("PB", 7)], w=["v32"]).tensor_copy(out=v32[0:Lc, :], in_=PB[7][0:Lc, 0:128])
                            S.dma("pv", r=["v32"]).dma_start(out=D["p_v"][l], in_=v32[:, :])
                        kpre, kpkey = kbuf[:, j * 128:(j + 1) * 128], "kbuf_prev" if j == 0 else "kbuf"
                        kown = kbuf[:, 128 + j * 128:128 + (j + 1) * 128]
                        vprev, vpkey = vb[(g - 1) % 3], ("vb", (g - 1) % 3)
                    for hg in range(2):
                        pr = hg * 64
                        for i in range(4):
                            bank = sc_ps[i // 2]
                            bk = ("PB", 4 + i // 2)
                            base = (i % 2) * 256
                            qh = qn[pr:pr + 64, i, cs]
                            if has_prev:
                                S.pe(r=[kpkey, ("qn", i)], w=[bk]).matmul(
                                    bank[:, base:base + Lc], lhsT=kpre[pr:pr + 64, :], rhs=qh, start=True, stop=True)
                            S.pe(r=["kbuf", ("qn", i)], w=[bk]).matmul(
                                bank[0:Lc, base + 128:base + 128 + Lc], lhsT=kown[pr:pr + 64, :], rhs=qh, start=True, stop=True)
                        for bi in range(2):
                            scv = sc_ps[bi][:, :].rearrange("p (i b q) -> p i b q", i=2, b=2)
                            if has_prev:
                                S.act(r=[("PB", 4 + bi)], w=[("pexp", bi)]).activation(out=pexp[:, 2 * bi:2 * bi + 2, 0, 0:Lc],
                                                                             in_=scv[:, :, 0, 0:Lc], func=AF.Exp)
                            S.act(r=[("PB", 4 + bi)], w=[("pexp", bi)]).activation(out=pexp[0:Lc, 2 * bi:2 * bi + 2, 1, 0:Lc],
                                                                         in_=scv[0:Lc, :, 1, 0:Lc], func=AF.Exp)
                        pkeys = [("pexp", 0), ("pexp", 1)]
                        if has_prev:
                            S.dve(r=pkeys + ["cf"], w=["PT"]).tensor_tensor(out=PT[:, :, 0, 0:Lc], in0=pexp[:, :, 0, 0:Lc],
                                                                   in1=ETAB[:, 4 * hg:4 * hg + 4, 0, 0:Lc], op=ALU.mult)
                        S.dve(r=pkeys + ["cf"], w=["PT"]).tensor_tensor(out=PT[0:Lc, :, 1, 0:Lc], in0=pexp[0:Lc, :, 1, 0:Lc],
                                                               in1=ETAB[0:Lc, 4 * hg:4 * hg + 4, 1, 0:Lc], op=ALU.mult)
                        for i in range(4):
                            oo = o_ps[pr:pr + 64, i * 128:i * 128 + Lc]
                            dd = den_ps[pr:pr + 64, i * 128:i * 128 + Lc]
                            if has_prev:
                                S.pe(r=[vpkey, "PT"], w=[("PB", 6)]).matmul(
                                    oo, lhsT=vprev[:, pr:pr + 64], rhs=PT[:, i, 0, 0:Lc], start=True, stop=False)
                            S.pe(r=[vkey, "PT"], w=[("PB", 6)]).matmul(
                                oo, lhsT=vown[0:Lc, pr:pr + 64], rhs=PT[0:Lc, i, 1, 0:Lc], start=(not has_prev), stop=True)
                            if has_prev:
                                S.pe(r=["cb", "PT"], w=[("PB", 7)]).matmul(dd, lhsT=ONESB[:, 0:64], rhs=PT[:, i, 0, 0:Lc],
                                                                    start=True, stop=False)
                            S.pe(r=["cb", "PT"], w=[("PB", 7)]).matmul(dd, lhsT=ONESB[0:Lc, 0:64], rhs=PT[0:Lc, i, 1, 0:Lc],
                                                                start=(not has_prev), stop=True)
                    S.dve(r=[("PB", 7), "esink"], w=["den"]).tensor_tensor(out=den[:, :, 0:Lc],
                                                    in0=den_ps[:, :].rearrange("p (i q) -> p i q", i=4)[:, :, 0:Lc],
                                                    in1=esink[:, :].unsqueeze(2).to_broadcast([128, 4, Lc]), op=ALU.add)
                    S.dve(r=["den"], w=["rden"]).reciprocal(out=rden[:, :, 0:Lc], in_=den[:, :, 0:Lc])
                    S.dve(r=[("PB", 6), "rden"], w=[("catt", g)]).tensor_tensor(out=catt[:, :, cs],
                                                           in0=o_ps[:, :].rearrange("p (i q) -> p i q", i=4)[:, :, 0:Lc],
                                                           in1=rden[:, :, 0:Lc], op=ALU.mult)
            ng = nseq * nj
            gl = [gchunk[0] - ng + i for i in range(ng)]
            ygk = [("yg", g) for g in gl]
            cak = [("catt", g) for g in gl]
            S.act(r=ygk, w=["sq"]).activation(out=sq[:, 0:4, 0:TTt], in_=yg[:, :, 0:TTt], func=AF.Square)
            for gi in range(2):
                pbt, pbk = next_pb()
                for cc in range(2):
                    S.pe(r=["sq", "cb"], w=[pbk]).matmul(pbt[:, 0:TTt], lhsT=ONESB, rhs=sq[:, 2 * gi + cc, 0:TTt],
                                                                   start=(cc == 0), stop=(cc == 1))
                S.act(r=[pbk, "eps"], w=["std"]).activation(out=std[:, 0:TTt], in_=pbt[:, 0:TTt], func=AF.Sqrt, bias=eps_t[:, 0:1],
                                                      scale=1.0 / 256)
                S.dve(r=["std"], w=["rstd"]).reciprocal(out=rstd[:, 0:TTt], in_=std[:, 0:TTt])
                for cc in range(2):
                    c = 2 * gi + cc
                    S.dve(r=ygk + ["pp", "rstd"], w=[("cssd", c)]).scalar_tensor_tensor(out=cssd[:, c, 0:TTt], in0=yg[:, c, 0:TTt],
                                                                scalar=pp[:, P_SNW + c:P_SNW + c + 1], in1=rstd[:, 0:TTt],
                                                                op0=ALU.mult, op1=ALU.mult)
            for m in range(8):
                pbt, pbk = next_pb()
                for c in range(8):
                    src = cssd[:, c, 0:TTt] if c < 4 else catt[:, c - 4, 0:TTt]
                    rk = [("cssd", c)] if c < 4 else cak
                    S.pe(r=["Wout"] + rk, w=[pbk]).matmul(pbt[:, 0:TTt], lhsT=Wout[:, c, m * 128:(m + 1) * 128],
                                                                        rhs=src, start=(c == 0), stop=(c == 7))
                S.dve(r=[pbk, xkey], w=[xkey]).tensor_tensor(out=X[:, m, 0:TTt], in0=pbt[:, 0:TTt], in1=X[:, m, 0:TTt],
                                                                   op=ALU.add)
            S.dma(("xst", slot), r=[xkey]).dma_start(out=kp(xdst)[:, :, col0:col0 + TTt], in_=X[:, :, 0:TTt])
        S.analyze().emit(nc)


def interleave(*gens):
    gens = [g for g in gens if g is not None]
    if os.environ.get("KSEQ"):
        for g in gens:
            for _ in g:
                pass
        return
    while gens:
        for g in list(gens):
            try:
                next(g)
            except StopIteration:
                gens.remove(g)


def rms_rstd(S, eps_t, ps_ap, pkey, ln_ap, lnkey, out_ap, okey, inv_n):
    if os.environ.get("KRSQ"):
        S.act(r=[pkey, "eps"], w=[lnkey]).activation(out=ln_ap, in_=ps_ap, func=AF.Sqrt, bias=eps_t[:, 0:1], scale=inv_n)
        S.dve(r=[lnkey], w=[okey]).reciprocal(out=out_ap, in_=ln_ap)
        return
    S.act(r=[pkey, "eps"], w=[lnkey]).activation(out=ln_ap, in_=ps_ap, func=AF.Ln, bias=eps_t[:, 0:1], scale=inv_n)
    S.act(r=[lnkey], w=[okey]).activation(out=out_ap, in_=ln_ap, func=AF.Exp, scale=-0.5)


def mixer_stage(nc, l, tiles, xsrc, xdst, D):
    _UID[0] += 1
    with contextlib.ExitStack() as st:
        sb = lambda name, shape, dt: st.enter_context(nc.sbuf_tensor(_uname(name), shape, dt))
        S = Sched()
        C = load_common(nc, S, st, l, D)
        pp, cb = C["pp"], C["cb"]
        cf = sb("cf", [128, NCF], F32)
        S.dma("cf", w=["cf"]).dma_start(out=cf[:], in_=D["cstf"])
        TRI = cf[:, C_TRI:C_TRI + 128]
        ONESF = cf[:, C_ONES:C_ONES + 128]
        ETAB = cf[:, C_E:C_E + 2048].rearrange("p (h b q) -> p h b q", h=8, b=2)
        IDB = cb[:, B_ID:B_ID + 128]
        ONESB = cb[:, B_ONES:B_ONES + 128]
        BDB = cb[:, B_BD:B_BD + 128]
        eps_t = sb("eps_t", [128, 1], F32)
        S.dve(w=["eps"]).memset(eps_t[:], EPS)

        Win = sb("Win", [128, 8, IN_DIM], BF16)
        Wout = sb("Wout", [128, 8, D_MODEL], BF16)
        for k in range(8):
            S.dma("win", w=["Win"], eng="pool", group=True).dma_start(out=Win[:, k, :], in_=D["w_in"][l, k * 128:(k + 1) * 128, :])
        for k in range(8):
            S.dma("wout", w=["Wout"], eng="pool", group=True).dma_start(out=Wout[:, k, :], in_=D["w_out"][l, k * 128:(k + 1) * 128, :])

        a_row = sb("a_row", [128, 8], F32)
        esink = sb("esink", [128, 4], F32)
        wq8 = sb("wq8", [128, 1], F32)
        S.act(r=["pp"], w=["a_row"]).activation(out=a_row[:], in_=pp[:, P_ALOG:P_ALOG + 8], func=AF.Exp)
        S.dve(r=["a_row"], w=["a_row"]).tensor_scalar(out=a_row[:], in0=a_row[:], scalar1=-1.0, scalar2=None, op0=ALU.mult)
        S.act(r=["pp"], w=["esink"]).activation(out=esink[:], in_=pp[:, P_SINK:P_SINK + 4], func=AF.Exp)
        S.dve(r=["pp"], w=["wq8"]).tensor_scalar(out=wq8[:], in0=pp[:, P_QN:P_QN + 1], scalar1=0.125, scalar2=None, op0=ALU.mult)

        xt = [sb(f"xt{i}", [128, 8, TT], F32) for i in range(2)]
        sq = sb("sq", [128, 8, TT], BF16)
        hn = sb("hn", [128, 8, TT], BF16)
        lnb0 = sb("lnb0", [128, TT], F32)
        rstd0 = sb("rstd0", [128, TT], F32)
        lnb = sb("lnb", [128, 2 * TT], F32)
        rstd = sb("rstd", [128, 2 * TT], F32)
        sq2 = sb("sq2", [128, 4, TT], BF16)
        lnb2 = sb("lnb2", [128, TT], F32)
        rstd2 = sb("rstd2", [128, TT], F32)
        sz = sb("sz", [128, 4, TT], F32)
        pcb = sb("pcb", [128, 8, TT + 4], F32)
        pcar = sb("pcar", [128, 8, 3], F32)
        cv = [sb(f"cv{i}", [128, TT], F32) for i in range(2)]
        xbc = sb("xbc", [128, 8, TT], BF16)
        qk32 = sb("qk32", [128, 5, TT], F32)
        qn = sb("qn", [128, 4, TT], BF16)
        kn32 = sb("kn32", [128, TT], F32)
        kbuf = sb("kbuf", [128, 128 + TT], BF16)
        kcar = sb("kcar", [128, 128], BF16)
        kcache = [sb(f"kcache{i}", [128, 128], BF16) for i in range(2)]
        vcache = [sb(f"vcache{i}", [128, 128], BF16) for i in range(2)]
        yg = sb("yg", [128, 4, TT], F32)
        cssd = sb("cssd", [128, 4, TT], BF16)
        catt = sb("catt", [128, 4, TT], BF16)
        dtb = sb("dtb", [128, 8], F32)
        e1 = sb("e1", [128, 8], F32)
        dt = sb("dt", [128, 8], F32)
        dA = sb("dA", [128, 8], F32)
        acum = sb("acum", [128, 8], F32)
        diff = sb("diff", [128, 8], F32)
        tail = sb("tail", [128, 8], F32)
        cd = sb("cd", [128, 8], F32)
        dtt = sb("dtt", [128, 8], F32)
        dA_rep = sb("dA_rep", [128, 8, 128], F32)
        xtm = sb("xtm", [128, 768], BF16)
        xdt = sb("xdt", [128, 8, 64], BF16)
        xst = sb("xst", [128, 8, 64], BF16)
        cbm = sb("cbm", [128, 2, 128], F32)
        seg = sb("seg", [128, 8, 128], F32)
        MT = sb("MT", [128, 8, 128], BF16)
        eb = sb("eb", [128, 8, 128], F32)
        Cp = sb("Cp", [128, 8, 128], BF16)
        ygt = sb("ygt", [128, 4, 128], F32)
        Sst = [sb(f"Sst{i}", [128, 8, 64], F32) for i in range(2)]
        Sbf = sb("Sbf", [128, 8, 64], BF16)
        vb = [sb(f"vb{i}", [128, 128], BF16) for i in range(3)]
        v32 = sb("v32", [128, 128], F32)
        pexp = [sb(f"pexp{i}", [128, 2, 2, 128], F32) for i in range(2)]
        PT = [sb(f"PT{i}", [128, 2, 2, 128], BF16) for i in range(2)]
        den = sb("den", [128, 4, 128], F32)
        lnd = sb("lnd", [128, 4, 128], F32)
        rden = sb("rden", [128, 4, 128], F32)

        PB = [st.enter_context(nc.psum_tensor(_uname(f"pb{i}"), [128, 512], F32)) for i in range(8)]
        small = PB[2]
        tr_ps = PB[3][:].bitcast(BF16)
        ns_ps = PB[3]
        y_ps = PB[3]
        acb_ps = PB[4]
        sc_ps = PB[5]
        o_ps = PB[6]
        den_ps = PB[7]
        K3, K4, K5, K6, K7 = ("PB", 3), ("PB", 4), ("PB", 5), ("PB", 6), ("PB", 7)
        pbrot = [0]

        def next_pb():
            i = pbrot[0] % 2
            pbrot[0] += 1
            return PB[i], ("PB", i)

        S.dve(w=["pcar"]).memset(pcar[:], 0.0)
        S.dve(w=[("S", 0)]).memset(Sst[0][:], 0.0)
        S.pool(w=["Sbf"]).memset(Sbf[:], 0.0)
        S.pool(w=["kcar"]).memset(kcar[:], 0.0)

        gchunk = [0]
        hnk = [("hn", k) for k in range(8)]
        rndc = [0]

        def front(ti):
            tl = tiles[ti]
            col0, nseq, L, sample = tl["col0"], tl["nseq"], tl["L"], tl["sample"]
            TTt = nseq * L
            slot = ti % 2
            X = xt[slot]
            xkey = ("xt", slot)
            xap = X[:, :, 0:TTt]
            S.dma(("xld", slot), w=[xkey]).dma_start(out=X[:, :, 0:TTt], in_=kp(xsrc)[:, :, col0:col0 + TTt])
            yield
            S.pool(r=[xkey], w=["sq"]).tensor_tensor(out=sq[:, :, 0:TTt], in0=xap, in1=xap, op=ALU.mult)
            yield
            pbt, pbk = next_pb()
            for k in range(8):
                S.pe(r=["sq", "cb"], w=[pbk]).matmul(pbt[:, 0:TTt], lhsT=ONESB, rhs=sq[:, k, 0:TTt], start=(k == 0), stop=(k == 7))
            yield
            rms_rstd(S, eps_t, pbt[:, 0:TTt], pbk, lnb0[:, 0:TTt], "lnb0", rstd0[:, 0:TTt], "rstd0", 1.0 / D_MODEL)
            yield
            for k in range(8):
                S.dve(r=[xkey, "rstd0", "pp"], w=[("hn", k)]).scalar_tensor_tensor(
                    out=hn[:, k, 0:TTt], in0=xap[:, k, :], scalar=pp[:, P_N1 + k:P_N1 + k + 1], in1=rstd0[:, 0:TTt],
                    op0=ALU.mult, op1=ALU.mult)
                if k % 2 == 1:
                    yield
            pcv = pcb[:, :, 0:nseq * (3 + L)].rearrange("p c (s t) -> p c s t", s=nseq)
            if sample:
                for c in range(8):
                    S.dma("sconv", r=[], w=["pcb_prev"], group=True).dma_start(out=pcv[:, c, :, 0:3], in_=D["s_conv"][l, :, c])
            else:
                S.pool(r=["pcar"], w=["pcb_prev"]).tensor_copy(out=pcv[:, :, 0, 0:3], in_=pcar[:])
                S.pool(r=["kcar"], w=["kbuf_prev"]).tensor_copy(out=kbuf[:, 0:128], in_=kcar[:])
            yield

            def proj(kind, c, wc):
                pbt, pbk = next_pb()
                for k in range(8):
                    S.pe(r=["Win"] + hnk, w=[pbk]).matmul(pbt[:, 0:TTt], lhsT=Win[:, k, wc:wc + 128], rhs=hn[:, k, 0:TTt],
                                                          start=(k == 0), stop=(k == 7))
                if kind == "z":
                    S.act(r=[pbk], w=[("sz", c)]).activation(out=sz[:, c, 0:TTt], in_=pbt[:, 0:TTt], func=AF.Silu)
                elif kind == "x":
                    S.act(r=[pbk], w=[("pcb", c)]).activation(out=pcv[:, c, :, 3:3 + L],
                                                             in_=pbt[:, 0:TTt].rearrange("p (s t) -> p s t", s=nseq), func=AF.Copy)
                else:
                    S.dve(r=[pbk], w=[("qk32", c)]).tensor_copy(out=qk32[:, c, 0:TTt], in_=pbt[:, 0:TTt])

            for c in range(4):
                proj("q", c, Q0 + c * 128)
                yield
            proj("q", 4, K0)
            yield
            S.pool(r=[("qk32", c) for c in range(5)], w=["sq"]).tensor_tensor(out=sq[:, 0:5, 0:TTt], in0=qk32[:, :, 0:TTt],
                                                                             in1=qk32[:, :, 0:TTt], op=ALU.mult)
            for c in range(8):
                proj("x", c, XBC0 + c * 128)
                yield
            for grp in [(0, 1), (2, 3), (4,)]:
                pbt, pbk = next_pb()
                n = len(grp)
                for i, c in enumerate(grp):
                    S.pe(r=["sq", "cb"], w=[pbk]).matmul(pbt[:, i * TTt:(i + 1) * TTt], lhsT=BDB, rhs=sq[:, c, 0:TTt], start=True, stop=True)
                rms_rstd(S, eps_t, pbt[:, 0:n * TTt], pbk, lnb[:, 0:n * TTt], "lnb", rstd[:, 0:n * TTt], "rstd", 1.0 / 64)
                for i, c in enumerate(grp):
                    rs = rstd[:, i * TTt:(i + 1) * TTt]
                    if c < 4:
                        S.dve(r=[("qk32", c), "wq8", "rstd"], w=[("qn", c)]).scalar_tensor_tensor(
                            out=qn[:, c, 0:TTt], in0=qk32[:, c, 0:TTt], scalar=wq8[:, 0:1], in1=rs, op0=ALU.mult, op1=ALU.mult)
                    else:
                        S.dve(r=[("qk32", 4), "pp", "rstd"], w=["kn32"]).scalar_tensor_tensor(
                            out=kn32[:, 0:TTt], in0=qk32[:, 4, 0:TTt], scalar=pp[:, P_KN:P_KN + 1], in1=rs, op0=ALU.mult, op1=ALU.mult)
                        S.act(r=["kn32"], w=["kbuf"]).activation(out=kbuf[:, 128:128 + TTt], in_=kn32[:, 0:TTt], func=AF.Copy)
                yield
            if not sample:
                S.pool(r=["kbuf", "kbuf_prev"], w=["kcar"]).tensor_copy(out=kcar[:], in_=kbuf[:, TTt:TTt + 128])
                if tl["last"]:
                    S.dma("pk", r=["kn32"]).dma_start(out=D["p_kT"][l], in_=kn32[:, TTt - 128:TTt])
            else:
                for s in range(nseq):
                    S.dma("okc", group=True, w=["okc"]).dma_start(out=D["o_kT"][l, s, :, 0:124], in_=D["s_kT"][l, s, :, 4:128])
                    S.dma("okn", group=True, r=["kn32"], w=["okn"]).dma_start(out=D["o_kT"][l, s, :, 124:128], in_=kn32[:, s * LS:(s + 1) * LS])
                    S.dma("ovc", group=True, w=["ovc"]).dma_start(out=D["o_v"][l, s, 0:124, :], in_=D["s_v"][l, s, 4:128, :])
            yield
            for c in range(4):
                proj("z", c, Z0 + c * 128)
                yield
            for c in range(8):
                t = cv[c % 2]
                tk = ("cv", c % 2)
                t3 = t[:, 0:TTt].rearrange("p (s t) -> p s t", s=nseq)
                cw = lambda j, c=c: pp[:, P_CW + c * 4 + j:P_CW + c * 4 + j + 1]
                S.pool(r=[("pcb", c), "pp"], w=[tk]).tensor_scalar(out=t3, in0=pcv[:, c, :, 3:3 + L], scalar1=cw(3),
                                                                  scalar2=pp[:, P_CB + c:P_CB + c + 1], op0=ALU.mult, op1=ALU.add)
                for j in (2, 1, 0):
                    S.dve(r=[("pcb", c), "pcb_prev", "pp", tk], w=[tk]).scalar_tensor_tensor(
                        out=t3, in0=pcv[:, c, :, j:j + L], scalar=cw(j), in1=t3, op0=ALU.mult, op1=ALU.add)
                S.act(r=[tk], w=[("xbc", c)]).activation(out=xbc[:, c, 0:TTt], in_=t[:, 0:TTt], func=AF.Silu)
                yield
            allpcb = [("pcb", c) for c in range(8)]
            if sample:
                for c in range(8):
                    S.dma("oconv", r=allpcb + ["pcb_prev"], w=["oconv"], group=True).dma_start(out=D["o_conv"][l, :, c], in_=pcv[:, c, :, L:L + 3])
            else:
                S.pool(r=allpcb + ["pcb_prev"], w=["pcar"]).tensor_copy(out=pcar[:], in_=pcv[:, :, 0, L:L + 3])
                if tl["last"]:
                    S.dma("pconv", r=["pcar"]).dma_start(out=D["p_conv"][l], in_=pcar[:])
            yield

        def ssd_chunk(tl, s, j, g, Scur, skey):
            nseq, L, sample = tl["nseq"], tl["L"], tl["sample"]
            Lc = min(L, 128)
            nj = L // Lc
            c0 = s * L + j * Lc
            cs = slice(c0, c0 + Lc)
            for k in range(8):
                S.pe(r=["Win"] + hnk, w=[("PB", 2)]).matmul(small[0:Lc, 0:8], lhsT=hn[:, k, cs], rhs=Win[:, k, DT0:DT0 + 8],
                                                            start=(k == 0), stop=(k == 7))
            for gidx in range(2):
                S.pe(r=[("xbc", 4 + gidx), ("xbc", 6 + gidx)], w=[("PB", 2)]).matmul(
                    small[0:Lc, 160 + gidx * 128:160 + gidx * 128 + Lc], lhsT=xbc[:, 4 + gidx, cs], rhs=xbc[:, 6 + gidx, cs], start=True, stop=True)
            yield
            S.dve(r=[("PB", 2), "pp"], w=["dtb"]).tensor_tensor(out=dtb[0:Lc, :], in0=small[0:Lc, 0:8], in1=pp[0:Lc, P_DTB:P_DTB + 8], op=ALU.add)
            S.dve(r=[("PB", 2), "cf"], w=["cbm"]).tensor_tensor(out=cbm[0:Lc, :, 0:Lc],
                                                               in0=small[0:Lc, 160:416].rearrange("p (g q) -> p g q", g=2)[:, :, 0:Lc],
                                                               in1=TRI[0:Lc, 0:Lc].unsqueeze(1).to_broadcast([Lc, 2, Lc]), op=ALU.mult)
            yield
            S.act(r=["dtb"], w=["e1"]).activation(out=e1[0:Lc, :], in_=dtb[0:Lc, :], func=AF.Exp)
            S.act(r=["e1"], w=["dt"]).activation(out=dt[0:Lc, :], in_=e1[0:Lc, :], func=AF.Ln, bias=1.0)
            yield
            S.dve(r=["dt", "a_row"], w=["dA"]).tensor_tensor(out=dA[0:Lc, :], in0=dt[0:Lc, :], in1=a_row[0:Lc, :], op=ALU.mult)
            yield
            S.pe(r=["dA", "cf"], w=[("PB", 2)]).matmul(small[0:Lc, 144:152], lhsT=TRI[0:Lc, 0:Lc], rhs=dA[0:Lc, :], start=True, stop=True)
            S.pe(r=["dA", "cf"], w=[("PB", 2)]).matmul(small[:, 152:160], lhsT=ONESF[0:Lc, :], rhs=dA[0:Lc, :], start=True, stop=True)
            S.pool(r=["dA"], w=["dA_rep"]).tensor_copy(out=dA_rep[0:Lc, :, :], in_=dA[0:Lc, :].unsqueeze(2).to_broadcast([Lc, 8, 128]))
            yield
            S.dve(r=[("PB", 2)], w=["acum"]).tensor_copy(out=acum[0:Lc, :], in_=small[0:Lc, 144:152])
            S.dve(r=[("PB", 2), "acum"], w=["diff"]).tensor_tensor(out=diff[0:Lc, :], in0=small[0:Lc, 152:160], in1=acum[0:Lc, :], op=ALU.subtract)
            S.act(r=["diff"], w=["tail"]).activation(out=tail[0:Lc, :], in_=diff[0:Lc, :], func=AF.Exp)
            S.act(r=[("PB", 2)], w=["cd"]).activation(out=cd[:, :], in_=small[:, 152:160], func=AF.Exp)
            yield
            S.dve(r=["dt", "tail"], w=["dtt"]).tensor_tensor(out=dtt[0:Lc, :], in0=dt[0:Lc, :], in1=tail[0:Lc, :], op=ALU.mult)
            yield
            for hh in range(2):
                for r_ in range(4):
                    h = hh * 4 + r_
                    S.pe(r=["dA_rep", "cf", "acum", "diff", "cd", "tail", "dtt"], w=[K4]).matmul(
                        acb_ps[:, r_ * 128:r_ * 128 + Lc], lhsT=dA_rep[0:Lc, h, :], rhs=TRI[0:Lc, 0:Lc], start=True, stop=True)
                yield
                for r_ in range(4):
                    h = hh * 4 + r_
                    S.dve(r=[K4, "acum"], w=[("seg", hh)]).tensor_scalar(out=seg[0:Lc, h, 0:Lc], in0=acb_ps[0:Lc, r_ * 128:r_ * 128 + Lc],
                                                                         scalar1=acum[0:Lc, h:h + 1], scalar2=0.0, op0=ALU.subtract, op1=ALU.min)
                S.act(r=[K4], w=[("eb", hh)]).activation(out=eb[:, hh * 4:(hh + 1) * 4, 0:Lc],
                                                         in_=acb_ps[:, :].rearrange("p (r q) -> p r q", r=4)[:, :, 0:Lc], func=AF.Exp)
                yield
                S.act(r=[("seg", hh)], w=[("seg", hh)]).activation(out=seg[0:Lc, hh * 4:(hh + 1) * 4, 0:Lc], in_=seg[0:Lc, hh * 4:(hh + 1) * 4, 0:Lc], func=AF.Exp)
                S.pool(r=[("eb", hh), ("xbc", 6 + hh)], w=[("Cp", hh)]).tensor_tensor(
                    out=Cp[:, hh * 4:(hh + 1) * 4, 0:Lc], in0=eb[:, hh * 4:(hh + 1) * 4, 0:Lc],
                    in1=xbc[:, 6 + hh, cs].unsqueeze(1).to_broadcast([128, 4, Lc]), op=ALU.mult)
                yield
                S.dve(r=[("seg", hh), "cbm"], w=[("MT", hh)]).tensor_tensor(
                    out=MT[0:Lc, hh * 4:(hh + 1) * 4, 0:Lc], in0=seg[0:Lc, hh * 4:(hh + 1) * 4, 0:Lc],
                    in1=cbm[0:Lc, hh, 0:Lc].unsqueeze(1).to_broadcast([Lc, 4, Lc]), op=ALU.mult)
                yield
            for i in range(6):
                S.pe(r=[("xbc", i), "cb"], w=[K3]).transpose(tr_ps[0:Lc, i * 128:(i + 1) * 128], xbc[:, i, cs], IDB)
            yield
            S.act(r=[K3], w=["xtm"]).activation(out=xtm[0:Lc, :], in_=tr_ps[0:Lc, 0:768], func=AF.Copy)
            yield
            xtm3 = xtm[0:Lc, 0:512].rearrange("p (h d) -> p h d", h=8)
            S.pool(r=["xtm", "dt"], w=["xdt"]).tensor_tensor(out=xdt[0:Lc], in0=xtm3, in1=dt[0:Lc, :].unsqueeze(2).to_broadcast([Lc, 8, 64]), op=ALU.mult)
            S.pool(r=["xtm", "dtt"], w=["xst"]).tensor_tensor(out=xst[0:Lc], in0=xtm3, in1=dtt[0:Lc, :].unsqueeze(2).to_broadcast([Lc, 8, 64]), op=ALU.mult)
            yield
            for gidx in range(2):
                S.pe(r=["xtm", "xst"], w=[K3]).matmul(ns_ps[:, gidx * 256:(gidx + 1) * 256], lhsT=xtm[0:Lc, 512 + gidx * 128:512 + (gidx + 1) * 128],
                                                      rhs=xst[0:Lc, gidx * 4:(gidx + 1) * 4, :], start=True, stop=True)
            yield
            S.pool(r=[skey, "cd"], w=[skey]).tensor_tensor(out=Scur[:], in0=Scur[:], in1=cd[:, :].unsqueeze(2).to_broadcast([128, 8, 64]), op=ALU.mult)
            S.dve(r=[skey, K3], w=[skey]).tensor_tensor(out=Scur[:], in0=Scur[:], in1=ns_ps[:, :].rearrange("p (h d) -> p h d", h=8), op=ALU.add)
            yield
            for h in range(8):
                pr = (h % 2) * 64
                yo = y_ps[pr:pr + 64, (h // 2) * 128:(h // 2) * 128 + Lc]
                S.pe(r=["xdt", ("MT", h // 4)], w=[K3]).matmul(yo, lhsT=xdt[0:Lc, h, :], rhs=MT[0:Lc, h, 0:Lc], start=True, stop=False)
                S.pe(r=["Sbf", ("Cp", h // 4)], w=[K3]).matmul(yo, lhsT=Sbf[:, h, :], rhs=Cp[:, h, 0:Lc], start=False, stop=True)
                if h % 4 == 3:
                    yield
            for c in range(4):
                S.dve(r=[("xbc", c), "pp", K3], w=["ygt"]).scalar_tensor_tensor(
                    out=ygt[:, c, 0:Lc], in0=xbc[:, c, cs], scalar=pp[:, P_DSK + c:P_DSK + c + 1], in1=y_ps[:, c * 128:c * 128 + Lc],
                    op0=ALU.mult, op1=ALU.add)
            yield
            S.pool(r=["ygt"] + [("sz", c) for c in range(4)], w=[("yg", g)]).tensor_tensor(out=yg[:, :, cs], in0=ygt[:, :, 0:Lc], in1=sz[:, :, cs], op=ALU.mult)
            if sample:
                S.dma(("sst", s % 2), r=[skey]).dma_start(out=D["o_ssm"][l, s], in_=Scur[:].rearrange("p h d -> p (h d)"))
            else:
                S.act(r=[skey], w=["Sbf"]).activation(out=Sbf[:], in_=Scur[:], func=AF.Copy)
                if tl["last"] and j == nj - 1:
                    S.dma("pssm", r=[skey]).dma_start(out=D["p_ssm"][l], in_=Scur[:].rearrange("p h d -> p (h d)"))
            yield

        def att_chunk(tl, s, j, g):
            nseq, L, sample = tl["nseq"], tl["L"], tl["sample"]
            Lc = min(L, 128)
            nj = L // Lc
            c0 = s * L + j * Lc
            cs = slice(c0, c0 + Lc)
            has_prev = sample or not (tl["first"] and j == 0)
            for k in range(8):
                S.pe(r=["Win"] + hnk, w=[K7]).matmul(PB[7][0:Lc, 0:128], lhsT=hn[:, k, cs], rhs=Win[:, k, V0:V0 + 128],
                                                     start=(k == 0), stop=(k == 7))
            yield
            vown = vb[g % 3]
            vkey = ("vb", g % 3)
            S.act(r=[K7], w=[vkey]).activation(out=vown[0:Lc, :], in_=PB[7][0:Lc, 0:128], func=AF.Copy)
            if sample:
                S.dve(r=[K7], w=["v32"]).tensor_copy(out=v32[0:Lc, :], in_=PB[7][0:Lc, 0:128])
                S.dma("ovn", r=["v32"]).dma_start(out=D["o_v"][l, s, 124:128, :], in_=v32[0:Lc, :])
                kpre, kpkey = kcache[s % 2][:, :], ("kcache", s % 2)
                vprev, vpkey = vcache[s % 2], ("vcache", s % 2)
                kown = kbuf[:, 128 + c0:128 + c0 + Lc]
            else:
                if tl["last"] and j == nj - 1:
                    S.dve(r=[K7], w=["v32"]).tensor_copy(out=v32[0:Lc, :], in_=PB[7][0:Lc, 0:128])
                    S.dma("pv", r=["v32"]).dma_start(out=D["p_v"][l], in_=v32[:, :])
                kpre, kpkey = kbuf[:, j * 128:(j + 1) * 128], ("kbuf_prev" if j == 0 else "kbuf")
                kown = kbuf[:, 128 + j * 128:128 + (j + 1) * 128]
                vprev, vpkey = vb[(g - 1) % 3], ("vb", (g - 1) % 3)
            yield
            for hg in range(2):
                pr = hg * 64
                for rnd in range(2):
                    sl_ = rndc[0] % 2
                    rndc[0] += 1
                    PX, pxk = pexp[sl_], ("pexp", sl_)
                    PTt, ptk = PT[sl_], ("PT", sl_)
                    for ii in range(2):
                        i = 2 * rnd + ii
                        base = ii * 256
                        qh = qn[pr:pr + 64, i, cs]
                        if has_prev:
                            S.pe(r=[kpkey, ("qn", i)], w=[K5]).matmul(sc_ps[:, base:base + Lc], lhsT=kpre[pr:pr + 64, :], rhs=qh, start=True, stop=True)
                        S.pe(r=["kbuf", ("qn", i)], w=[K5]).matmul(sc_ps[0:Lc, base + 128:base + 128 + Lc], lhsT=kown[pr:pr + 64, :], rhs=qh,
                                                                   start=True, stop=True)
                    yield
                    scv = sc_ps[:, :].rearrange("p (i b q) -> p i b q", i=2, b=2)
                    h0 = 4 * hg + 2 * rnd
                    if has_prev:
                        S.act(r=[K5], w=[pxk]).activation(out=PX[:, :, 0, 0:Lc], in_=scv[:, :, 0, 0:Lc], func=AF.Exp)
                    S.act(r=[K5], w=[pxk]).activation(out=PX[0:Lc, :, 1, 0:Lc], in_=scv[0:Lc, :, 1, 0:Lc], func=AF.Exp)
                    yield
                    if has_prev:
                        S.dve(r=[pxk, "cf"], w=[ptk]).tensor_tensor(out=PTt[:, :, 0, 0:Lc], in0=PX[:, :, 0, 0:Lc],
                                                                    in1=ETAB[:, h0:h0 + 2, 0, 0:Lc], op=ALU.mult)
                    S.dve(r=[pxk, "cf"], w=[ptk]).tensor_tensor(out=PTt[0:Lc, :, 1, 0:Lc], in0=PX[0:Lc, :, 1, 0:Lc],
                                                                in1=ETAB[0:Lc, h0:h0 + 2, 1, 0:Lc], op=ALU.mult)
                    yield
                    for ii in range(2):
                        i = 2 * rnd + ii
                        oo = o_ps[pr:pr + 64, i * 128:i * 128 + Lc]
                        dd = den_ps[pr:pr + 64, i * 128:i * 128 + Lc]
                        if has_prev:
                            S.pe(r=[vpkey, ptk], w=[K6]).matmul(oo, lhsT=vprev[:, pr:pr + 64], rhs=PTt[:, ii, 0, 0:Lc], start=True, stop=False)
                        S.pe(r=[vkey, ptk], w=[K6]).matmul(oo, lhsT=vown[0:Lc, pr:pr + 64], rhs=PTt[0:Lc, ii, 1, 0:Lc], start=(not has_prev), stop=True)
                        if has_prev:
                            S.pe(r=["cb", ptk], w=[K7]).matmul(dd, lhsT=ONESB[:, 0:64], rhs=PTt[:, ii, 0, 0:Lc], start=True, stop=False)
                        S.pe(r=["cb", ptk], w=[K7]).matmul(dd, lhsT=ONESB[0:Lc, 0:64], rhs=PTt[0:Lc, ii, 1, 0:Lc], start=(not has_prev), stop=True)
                    yield
            S.dve(r=[K7, "esink"], w=["den"]).tensor_tensor(out=den[:, :, 0:Lc], in0=den_ps[:, :].rearrange("p (i q) -> p i q", i=4)[:, :, 0:Lc],
                                                            in1=esink[:, :].unsqueeze(2).to_broadcast([128, 4, Lc]), op=ALU.add)
            yield
            if os.environ.get("KRSQ"):
                S.dve(r=["den"], w=["rden"]).reciprocal(out=rden[:, :, 0:Lc], in_=den[:, :, 0:Lc])
            else:
                S.act(r=["den"], w=["lnd"]).activation(out=lnd[:, :, 0:Lc], in_=den[:, :, 0:Lc], func=AF.Ln)
                S.act(r=["lnd"], w=["rden"]).activation(out=rden[:, :, 0:Lc], in_=lnd[:, :, 0:Lc], func=AF.Exp, scale=-1.0)
            yield
            S.dve(r=[K6, "rden"], w=[("catt", g)]).tensor_tensor(out=catt[:, :, cs], in0=o_ps[:, :].rearrange("p (i q) -> p i q", i=4)[:, :, 0:Lc],
                                                                 in1=rden[:, :, 0:Lc], op=ALU.mult)
            yield

        def back(ti, gl):
            tl = tiles[ti]
            col0, nseq, L = tl["col0"], tl["nseq"], tl["L"]
            TTt = nseq * L
            slot = ti % 2
            X = xt[slot]
            xkey = ("xt", slot)
            ygk = [("yg", g) for g in gl]
            cak = [("catt", g) for g in gl]
            S.pool(r=ygk, w=["sq2"]).tensor_tensor(out=sq2[:, :, 0:TTt], in0=yg[:, :, 0:TTt], in1=yg[:, :, 0:TTt], op=ALU.mult)
            yield
            for gi in range(2):
                bi = 6 + gi
                pbt, pbk = PB[bi], ("PB", bi)
                for cc in range(2):
                    S.pe(r=["sq2", "cb"], w=[pbk]).matmul(pbt[:, 0:TTt], lhsT=ONESB, rhs=sq2[:, 2 * gi + cc, 0:TTt], start=(cc == 0), stop=(cc == 1))
                rms_rstd(S, eps_t, pbt[:, 0:TTt], pbk, lnb2[:, 0:TTt], "lnb2", rstd2[:, 0:TTt], "rstd2", 1.0 / 256)
                for cc in range(2):
                    c = 2 * gi + cc
                    S.dve(r=ygk + ["pp", "rstd2"], w=[("cssd", c)]).scalar_tensor_tensor(
                        out=cssd[:, c, 0:TTt], in0=yg[:, c, 0:TTt], scalar=pp[:, P_SNW + c:P_SNW + c + 1], in1=rstd2[:, 0:TTt],
                        op0=ALU.mult, op1=ALU.mult)
                yield
            for m in range(8):
                bi = 6 + m % 2
                pbt, pbk = PB[bi], ("PB", bi)
                for c in range(8):
                    src = cssd[:, c, 0:TTt] if c < 4 else catt[:, c - 4, 0:TTt]
                    rk = [("cssd", c)] if c < 4 else cak
                    S.pe(r=["Wout"] + rk, w=[pbk]).matmul(pbt[:, 0:TTt], lhsT=Wout[:, c, m * 128:(m + 1) * 128], rhs=src, start=(c == 0), stop=(c == 7))
                S.dve(r=[pbk, xkey], w=[xkey]).tensor_tensor(out=X[:, m, 0:TTt], in0=pbt[:, 0:TTt], in1=X[:, m, 0:TTt], op=ALU.add)
                yield
            S.dma(("xst", slot), r=[xkey]).dma_start(out=kp(xdst)[:, :, col0:col0 + TTt], in_=X[:, :, 0:TTt])
            yield

        interleave(front(0))
        for ti, tl in enumerate(tiles):
            nseq, L, sample = tl["nseq"], tl["L"], tl["sample"]
            Lc = min(L, 128)
            nj = L // Lc
            gl = []
            for s in range(nseq):
                if sample:
                    sslot = s % 2
                    Scur, skey = Sst[sslot], ("S", sslot)
                    S.dma(("sld", sslot), w=[skey]).dma_start(out=Scur[:].rearrange("p h d -> p (h d)"), in_=D["s_ssm"][l, s])
                    S.act(r=[skey], w=["Sbf"]).activation(out=Sbf[:], in_=Scur[:], func=AF.Copy)
                    S.dma(("kcl", sslot), w=[("kcache", sslot)], eng="pool").dma_start(out=kcache[sslot][:], in_=D["s_kT"][l, s])
                    S.dma(("vcl", sslot), w=[("vcache", sslot)], eng="pool").dma_start(out=vcache[sslot][:], in_=D["s_v"][l, s])
                else:
                    Scur, skey = Sst[0], ("S", 0)
                for j in range(nj):
                    g = gchunk[0]
                    gchunk[0] += 1
                    gl.append(g)
                    if os.environ.get("KLOG"):
                        print("KLOG chunk start", ti, s, j, "nrec", getattr(S, "nrec", 0))
                        interleave(ssd_chunk(tl, s, j, g, Scur, skey))
                        print("KLOG  after ssd nrec", getattr(S, "nrec", 0))
                        interleave(att_chunk(tl, s, j, g))
                        print("KLOG  after att nrec", getattr(S, "nrec", 0))
                    else:
                        interleave(ssd_chunk(tl, s, j, g, Scur, skey), att_chunk(tl, s, j, g))
            if os.environ.get("KLOG"):
                print("KLOG before back", ti, "nrec", getattr(S, "nrec", 0))
            interleave(back(ti, gl), front(ti + 1) if ti + 1 < len(tiles) else None)
            if os.environ.get("KLOG"):
                print("KLOG after back+front", ti, "nrec", getattr(S, "nrec", 0))
        S.analyze().emit(nc)


def ffn_stage(nc, l, tiles, xsrc, xdst, D):
    _UID[0] += 1
    with contextlib.ExitStack() as st:
        sb = lambda name, shape, dt: st.enter_context(nc.sbuf_tensor(_uname(name), shape, dt))
        S = Sched()
        C = load_common(nc, S, st, l, D)
        pp, cb = C["pp"], C["cb"]
        ONESB = cb[:, B_ONES:B_ONES + 128]
        eps_t = sb("eps_t", [128, 1], F32)
        S.dve(w=["eps"]).memset(eps_t[:], EPS)
        Wup = sb("Wup", [128, 8, 2 * D_FF], BF16)
        Wdn = sb("Wdn", [128, NPAIR, D_MODEL], BF16)
        for k in range(8):
            S.dma("wup", w=["Wup"], eng="pool", group=True).dma_start(out=Wup[:, k, :], in_=D["w_up"][l, k * 128:(k + 1) * 128, :])
        for j in range(NPAIR):
            S.dma("wdn", w=["Wdn"], eng="pool", group=True).dma_start(out=Wdn[:, j, :], in_=D["w_down"][l, j * 128:(j + 1) * 128, :])
        xt = [sb(f"xt{i}", [128, 8, TT], F32) for i in range(2)]
        sq = sb("sq", [128, 8, TT], BF16)
        hn = [sb(f"hn{i}", [128, 8, TT], BF16) for i in range(2)]
        lnb = sb("lnb", [128, TT], F32)
        rstd = sb("rstd", [128, TT], F32)
        btmp = [sb(f"btmp{i}", [128, NSQ, 2], F32) for i in range(2)]
        t0 = [sb(f"t0{i}", [128, TT], F32) for i in range(2)]
        sg = [sb(f"sg{i}", [128, TT], F32) for i in range(2)]
        gb = sb("gb", [128, NPAIR, TT], BF16)
        car = [sb(f"car{i}", [128, 44, NSQ, 2], F32) for i in range(2)]
        PB = [st.enter_context(nc.psum_tensor(_uname(f"pb{i}"), [128, 512], F32)) for i in range(8)]
        cark = [[("car", pr_, ch) for ch in range(44)] for pr_ in range(2)]
        S.dve(w=cark[0]).memset(car[0][:], 0.0)
        S.pool(w=cark[1]).memset(car[1][:], 0.0)
        rot = [0]
        uct = [0]

        def front(ti):
            tl = tiles[ti]
            col0, nseq, L = tl["col0"], tl["nseq"], tl["L"]
            TTt = nseq * L
            slot = ti % 2
            X, xkey = xt[slot], ("xt", slot)
            H = hn[slot]
            xap = X[:, :, 0:TTt]
            S.dma(("xld", slot), w=[xkey]).dma_start(out=X[:, :, 0:TTt], in_=kp(xsrc)[:, :, col0:col0 + TTt])
            yield
            S.pool(r=[xkey], w=["sq"]).tensor_tensor(out=sq[:, :, 0:TTt], in0=xap, in1=xap, op=ALU.mult)
            yield
            for k in range(8):
                S.pe(r=["sq", "cb"], w=[("PB", 0)]).matmul(PB[0][:, 0:TTt], lhsT=ONESB, rhs=sq[:, k, 0:TTt], start=(k == 0), stop=(k == 7))
            yield
            rms_rstd(S, eps_t, PB[0][:, 0:TTt], ("PB", 0), lnb[:, 0:TTt], "lnb", rstd[:, 0:TTt], "rstd", 1.0 / D_MODEL)
            yield
            for k in range(8):
                S.dve(r=[xkey, "rstd", "pp"], w=[("hn", slot, k)]).scalar_tensor_tensor(
                    out=H[:, k, 0:TTt], in0=xap[:, k, :], scalar=pp[:, P_N2 + k:P_N2 + k + 1], in1=rstd[:, 0:TTt], op0=ALU.mult, op1=ALU.mult)
                yield

        def up(ti):
            tl = tiles[ti]
            nseq, L, sample = tl["nseq"], tl["L"], tl["sample"]
            TTt = nseq * L
            slot = ti % 2
            H = hn[slot]
            hnk = [("hn", slot, k) for k in range(8)]
            rp, wp = (ti + 1) % 2, ti % 2
            CR, CW = car[rp], car[wp]
            if sample:
                S.dma("sffn", w=cark[rp]).dma_start(out=CR[:], in_=D["s_ffn"][l])
            for j in range(NPAIR):
                for half in range(2):
                    ch = j + half * NPAIR
                    bi = 1 + rot[0] % 6
                    rot[0] += 1
                    pbt, pbk = PB[bi], ("PB", bi)
                    us = uct[0] % 2
                    uct[0] += 1
                    T0, tkey = t0[us], ("t0", us)
                    BT, bkey = btmp[us], ("btmp", us)
                    t3 = T0[:, 0:TTt].rearrange("p (s t) -> p s t", s=nseq)
                    p3 = pbt[:, 0:TTt].rearrange("p (s t) -> p s t", s=nseq)
                    for k in range(8):
                        S.pe(r=["Wup"] + hnk, w=[pbk]).matmul(pbt[:, 0:TTt], lhsT=Wup[:, k, ch * 128:(ch + 1) * 128], rhs=H[:, k, 0:TTt],
                                                              start=(k == 0), stop=(k == 7))
                    fw = lambda jj, ch=ch: pp[:, P_FW + ch * 3 + jj:P_FW + ch * 3 + jj + 1]
                    S.act(r=[pbk, "pp"], w=[tkey]).activation(out=t3, in_=p3, func=AF.Identity, scale=fw(2), bias=pp[:, P_FB + ch:P_FB + ch + 1])
                    S.act(r=[pbk], w=[("car", wp, ch)]).activation(out=CW[:, ch, 0:nseq, :], in_=p3[:, :, L - 2:L], func=AF.Copy)
                    S.dve(r=[pbk, "pp", tkey], w=[tkey]).scalar_tensor_tensor(out=t3[:, :, 1:L], in0=p3[:, :, 0:L - 1], scalar=fw(1),
                                                                               in1=t3[:, :, 1:L], op0=ALU.mult, op1=ALU.add)
                    S.dve(r=[pbk, "pp", tkey], w=[tkey]).scalar_tensor_tensor(out=t3[:, :, 2:L], in0=p3[:, :, 0:L - 2], scalar=fw(0),
                                                                               in1=t3[:, :, 2:L], op0=ALU.mult, op1=ALU.add)
                    S.pool(r=[("car", rp, ch), "pp"], w=[bkey]).tensor_scalar(out=BT[:, 0:nseq, 0:1], in0=CR[:, ch, 0:nseq, 1:2], scalar1=fw(1),
                                                                             scalar2=None, op0=ALU.mult)
                    S.pool(r=[bkey, tkey], w=[tkey]).tensor_tensor(out=t3[:, :, 0:1], in0=t3[:, :, 0:1], in1=BT[:, 0:nseq, 0:1], op=ALU.add)
                    S.pool(r=[("car", rp, ch), "pp"], w=[bkey]).tensor_scalar(out=BT[:, 0:nseq, 0:2], in0=CR[:, ch, 0:nseq, 0:2], scalar1=fw(0),
                                                                             scalar2=None, op0=ALU.mult)
                    S.pool(r=[bkey, tkey], w=[tkey]).tensor_tensor(out=t3[:, :, 0:2], in0=t3[:, :, 0:2], in1=BT[:, 0:nseq, 0:2], op=ALU.add)
                    if half == 0:
                        SG, sgk = sg[j % 2], ("sg", j % 2)
                        S.act(r=[tkey], w=[sgk]).activation(out=SG[:, 0:TTt], in_=T0[:, 0:TTt], func=AF.Silu)
                    else:
                        S.pool(r=[sgk, tkey], w=[("gb", j)]).tensor_tensor(out=gb[:, j, 0:TTt], in0=SG[:, 0:TTt], in1=T0[:, 0:TTt], op=ALU.mult)
                    yield
            if sample:
                S.dma("offn", r=cark[wp]).dma_start(out=D["o_ffn"][l], in_=CW[:])
            elif tl["last"]:
                S.dma("pffn", r=cark[wp]).dma_start(out=D["p_ffn"][l], in_=CW[:, :, 0, :])
            yield

        def down(ti):
            tl = tiles[ti]
            col0, nseq, L = tl["col0"], tl["nseq"], tl["L"]
            TTt = nseq * L
            slot = ti % 2
            X, xkey = xt[slot], ("xt", slot)
            gbk = [("gb", j) for j in range(NPAIR)]
            for m in range(8):
                bi = 4 + m % 4
                pbt, pbk = PB[bi], ("PB", bi)
                for j in range(NPAIR):
                    S.pe(r=["Wdn"] + gbk, w=[pbk]).matmul(pbt[:, 0:TTt], lhsT=Wdn[:, j, m * 128:(m + 1) * 128], rhs=gb[:, j, 0:TTt],
                                                          start=(j == 0), stop=(j == NPAIR - 1))
                    if j % 6 == 5:
                        yield
                S.dve(r=[pbk, xkey], w=[xkey]).tensor_tensor(out=X[:, m, 0:TTt], in0=pbt[:, 0:TTt], in1=X[:, m, 0:TTt], op=ALU.add)
                yield
            S.dma(("xst", slot), r=[xkey]).dma_start(out=kp(xdst)[:, :, col0:col0 + TTt], in_=X[:, :, 0:TTt])
            yield

        interleave(front(0))
        for ti in range(len(tiles)):
            interleave(up(ti))
            interleave(down(ti), front(ti + 1) if ti + 1 < len(tiles) else None)
        S.analyze().emit(nc)


def _consts():
    s = np.arange(128)[:, None]
    q = np.arange(128)[None, :]
    tri = (q >= s).astype(np.float32)
    ones = np.ones((128, 128), np.float32)
    slopes = 2.0 ** (-8.0 * np.arange(1, 9) / 8)
    E = np.zeros((128, 8, 2, 128), np.float64)
    for h in range(8):
        rel_prev = (q + 128 - s).astype(np.float64)
        E[:, h, 0, :] = np.where(s > q, np.exp(-slopes[h] * rel_prev), 0.0)
        rel_own = (q - s).astype(np.float64)
        E[:, h, 1, :] = np.where(s <= q, np.exp(-slopes[h] * rel_own), 0.0)
    cstf = np.concatenate([tri, ones, E.reshape(128, 2048).astype(np.float32)], axis=1)
    ident = np.eye(128, dtype=np.float32)
    bd = np.zeros((128, 128), np.float32)
    bd[:64, :64] = 1.0
    bd[64:, 64:] = 1.0
    cstb = np.concatenate([ident, ones, bd], axis=1)
    return np.ascontiguousarray(cstf), np.ascontiguousarray(cstb)


def _pack_params(inp):
    pp = np.zeros((DEPTH, 128, NPP), np.float32)
    p = np.arange(128)
    for l in range(DEPTH):
        pp[l, :, P_N1:P_N1 + 8] = inp["norm1_w"][l].reshape(8, 128).T
        pp[l, :, P_N2:P_N2 + 8] = inp["norm2_w"][l].reshape(8, 128).T
        cw = inp["ssd_conv_w"][l].reshape(4, 8, 128)
        pp[l, :, P_CW:P_CW + 32] = cw.transpose(2, 1, 0).reshape(128, 32)
        pp[l, :, P_CB:P_CB + 8] = inp["ssd_conv_b"][l].reshape(8, 128).T
        pp[l, :, P_DTB:P_DTB + 8] = inp["dt_bias"][l][None, :]
        pp[l, :, P_ALOG:P_ALOG + 8] = inp["a_log"][l][None, :]
        pp[l, :, P_DSK:P_DSK + 4] = inp["d_skip"][l].reshape(4, 2)[:, (p >= 64).astype(int)].T
        pp[l, :, P_SNW:P_SNW + 4] = inp["ssd_norm_w"][l].reshape(4, 128).T
        pp[l, :, P_QN] = inp["q_norm_w"][l][p % 64]
        pp[l, :, P_KN] = inp["k_norm_w"][l][p % 64]
        sk = inp["attn_sinks"][l]
        pp[l, :, P_SINK:P_SINK + 4] = np.stack([np.where(p < 64, sk[c], sk[4 + c]) for c in range(4)], axis=1)
        fw = inp["ffn_conv_w"][l].reshape(3, 44, 128)
        pp[l, :, P_FW:P_FW + 132] = fw.transpose(2, 1, 0).reshape(128, 132)
        pp[l, :, P_FB:P_FB + 44] = inp["ffn_conv_b"][l].reshape(44, 128).T
    return pp


def _perm_weights(inp):
    w_in = inp["w_in"]
    qcols = []
    for c in range(4):
        qcols += list(range(1544 + c * 64, 1544 + (c + 1) * 64))
        qcols += list(range(1544 + (4 + c) * 64, 1544 + (5 + c) * 64))
    cols = list(range(0, 1536)) + qcols + list(range(2056, 2312)) + list(range(1536, 1544))
    w_in_p = np.ascontiguousarray(w_in[:, :, cols])
    rows = list(range(512))
    for c in range(4):
        rows += list(range(512 + c * 64, 512 + (c + 1) * 64))
        rows += list(range(512 + (4 + c) * 64, 512 + (5 + c) * 64))
    w_out_p = np.ascontiguousarray(inp["w_out"][:, rows, :])
    return w_in_p, w_out_p


_NC_CACHE = {}


def kernel(**inp):
    inp = {k: np.asarray(v) for k, v in inp.items()}
    xp = inp["x_prompt"]
    B, TP, _ = xp.shape
    xs = inp["x_sample"]
    n_stages = int(inp.pop("_n_stages", 4)) if "_n_stages" in inp else 4
    debug = bool(inp.pop("_debug", False)) if "_debug" in inp else False
    key = (TP, n_stages, debug)
    if key not in _NC_CACHE:
        _NC_CACHE[key] = build_nc(TP, n_stages, debug)
    nc = _NC_CACHE[key]
    cstf, cstb = _consts()
    pp = _pack_params(inp)
    w_in_p, w_out_p = _perm_weights(inp)
    w_up = np.ascontiguousarray(inp["w_up"])
    w_down = np.ascontiguousarray(inp["w_down"])
    in_maps = []
    for c in range(NCORES):
        b = c % B
        sl = slice(c * NSQ, (c + 1) * NSQ)
        xT = np.concatenate([xp[b].T, xs[sl].reshape(TS, D_MODEL).T], axis=1)
        m = {
            "xT": np.ascontiguousarray(xT, dtype=np.float32),
            "w_in": w_in_p, "w_out": w_out_p, "w_up": w_up, "w_down": w_down,
            "pp": pp, "cstf": cstf, "cstb": cstb,
            "s_ssm": np.ascontiguousarray(inp["state_ssm"][:, sl].transpose(0, 1, 4, 2, 3).reshape(DEPTH, NSQ, 128, 512)),
            "s_conv": np.ascontiguousarray(inp["state_ssd_conv"][:, sl].reshape(DEPTH, NSQ, 3, 8, 128).transpose(0, 4, 3, 1, 2)),
            "s_kT": np.ascontiguousarray(inp["cache_swa_k"][:, sl].reshape(DEPTH, NSQ, 128, 128).transpose(0, 1, 3, 2)),
            "s_v": np.ascontiguousarray(inp["cache_swa_v"][:, sl].reshape(DEPTH, NSQ, 128, 128)),
            "s_ffn": np.ascontiguousarray(inp["state_ffn_conv"][:, sl].reshape(DEPTH, NSQ, 2, 44, 128).transpose(0, 4, 3, 1, 2)),
        }
        in_maps.append(m)
    res = run_bass_kernel_spmd(nc, in_maps, core_ids=list(range(NCORES)))
    R = res.results
    f32 = np.float32
    y_prompt = np.stack([R[b]["yT"][:, :TP].T for b in range(B)]).astype(f32)
    y_sample = np.concatenate([R[c]["yT"][:, TP:].T.reshape(NSQ, LS, D_MODEL) for c in range(NCORES)]).astype(f32)
    p_ssm = np.stack([R[b]["p_ssm"].reshape(DEPTH, 128, 8, 64).transpose(0, 2, 3, 1) for b in range(B)], axis=1)
    p_conv = np.stack([R[b]["p_conv"].transpose(0, 3, 2, 1).reshape(DEPTH, 3, 1024) for b in range(B)], axis=1)
    p_k = np.stack([R[b]["p_kT"].transpose(0, 2, 1).reshape(DEPTH, 128, 2, 64) for b in range(B)], axis=1)
    p_v = np.stack([R[b]["p_v"].reshape(DEPTH, 128, 2, 64) for b in range(B)], axis=1)
    p_ffn = np.stack([R[b]["p_ffn"].transpose(0, 3, 2, 1).reshape(DEPTH, 2, 2 * D_FF) for b in range(B)], axis=1)
    o_ssm = np.concatenate([R[c]["o_ssm"].reshape(DEPTH, NSQ, 128, 8, 64).transpose(0, 1, 3, 4, 2) for c in range(NCORES)], axis=1)
    o_conv = np.concatenate([R[c]["o_conv"].transpose(0, 3, 4, 2, 1).reshape(DEPTH, NSQ, 3, 1024) for c in range(NCORES)], axis=1)
    o_k = np.concatenate([R[c]["o_kT"].transpose(0, 1, 3, 2).reshape(DEPTH, NSQ, 128, 2, 64) for c in range(NCORES)], axis=1)
    o_v = np.concatenate([R[c]["o_v"].reshape(DEPTH, NSQ, 128, 2, 64) for c in range(NCORES)], axis=1)
    o_ffn = np.concatenate([R[c]["o_ffn"].transpose(0, 3, 4, 2, 1).reshape(DEPTH, NSQ, 2, 2 * D_FF) for c in range(NCORES)], axis=1)
    outs = (y_prompt, y_sample, p_ssm, p_conv, p_k, p_v, p_ffn, o_ssm, o_conv, o_k, o_v, o_ffn)
    outs = tuple(np.ascontiguousarray(o, dtype=f32) for o in outs)
    if debug:
        kernel._dbg = R
    return outs
```

```python
import contextlib
import numpy as np
import concourse.bass as bass
import concourse.mybir as mybir
from concourse.bass_utils import run_bass_kernel_spmd

F32 = mybir.dt.float32
BF16 = mybir.dt.bfloat16
ALU = mybir.AluOpType
AF = mybir.ActivationFunctionType

D_MODEL = 1024
DEPTH = 2
NCORES = 8
NSQ = 16
LS = 4
TS = NSQ * LS
TT = 256
D_FF = 2816
NPAIR = 22
EPS = 1e-6
Z0, XBC0, Q0, K0, V0, DT0, IN_DIM = 0, 512, 1536, 2048, 2176, 2304, 2312
P_N1, P_N2, P_CW, P_CB, P_DTB, P_ALOG, P_DSK, P_SNW, P_QN, P_KN, P_SINK, P_FW, P_FB, NPP = \
    0, 8, 16, 48, 56, 64, 72, 76, 80, 81, 82, 86, 218, 262
C_TRI, C_ONES, C_E, NCF = 0, 128, 256, 256 + 2048
B_ID, B_ONES, B_BD, NCB = 0, 128, 256, 384

import os
POOL_ENG = os.environ.get("KPOOL", "pool")
MIXER_IMPL = os.environ.get("KMIXER", "r2")
ENGINES = ("pe", "act", "dve", "pool", "sp")
_UID = [0]


def _uname(name):
    return f"{name}_u{_UID[0]}"

SEM_CHUNK = 2000


class Op:
    __slots__ = ("eng", "fn", "reads", "writes", "dma_key", "group", "deps", "marked",
                 "sem_idx", "sem_val", "id")

    def __init__(self, eng, fn, reads, writes, dma_key, group):
        self.eng, self.fn, self.reads, self.writes = eng, fn, reads, writes
        self.dma_key, self.group = dma_key, group
        self.deps = set()
        self.marked = False
        self.sem_idx = self.sem_val = None


class _Rec:
    def __init__(self, sched, eng, r, w, key, group):
        self._a = (sched, eng, r, w, key, group)

    def __getattr__(self, name):
        sched, eng, r, w, key, group = self._a

        def rec(*args, **kwargs):
            return sched.op(eng, (name, args, kwargs), r, w, dma_key=key, group=group)
        return rec


class Sched:
    def __init__(self):
        self.ops = []

    def op(self, eng, fn, reads=(), writes=(), dma_key=None, group=False):
        self.nrec = getattr(self, "nrec", 0) + 1
        cut = int(os.environ.get("KCUT", "0"))
        if cut and self.nrec > cut:
            return None
        o = Op(eng, fn, tuple(reads), tuple(writes), dma_key, group)
        o.id = len(self.ops)
        self.ops.append(o)
        return o

    def pe(self, r=(), w=()): return _Rec(self, "pe", r, w, None, False)
    def act(self, r=(), w=()): return _Rec(self, "act", r, w, None, False)
    def dve(self, r=(), w=()): return _Rec(self, "dve", r, w, None, False)
    def pool(self, r=(), w=()): return _Rec(self, POOL_ENG, r, w, None, False)

    def dma(self, key, r=(), w=(), eng="sp", group=False):
        return _Rec(self, eng, r, w, key, group)

    def analyze(self):
        last_w, readers, last_dma = {}, {}, {}
        ops = self.ops
        for o in ops:
            deps = set()
            for r in o.reads:
                if r in last_w:
                    deps.add(last_w[r])
                if isinstance(r, tuple) and r and r[0] == "PB":
                    deps.update(d for d in readers.get(r, ()) if ops[d].eng != o.eng)
            for w in o.writes:
                if w in last_w:
                    po = ops[last_w[w]]
                    if not (o.group and po.group and o.dma_key is not None and po.dma_key == o.dma_key):
                        deps.add(po.id)
                deps.update(readers.get(w, ()))
            if o.dma_key is not None:
                if not o.group and o.dma_key in last_dma:
                    deps.add(last_dma[o.dma_key])
                last_dma[o.dma_key] = o.id
            deps.discard(o.id)
            if o.eng == "pe" and o.dma_key is None:
                deps = {d for d in deps if not (ops[d].eng == "pe" and ops[d].dma_key is None)}
            o.deps = deps
            for r in o.reads:
                readers.setdefault(r, set()).add(o.id)
            for w in o.writes:
                last_w[w] = o.id
                readers[w] = set()
        for o in ops:
            for d in o.deps:
                ops[d].marked = True
        cnt = {e: 0 for e in ENGINES}
        dma_cum = {}
        for o in ops:
            if o.dma_key is not None:
                dma_cum[o.dma_key] = dma_cum.get(o.dma_key, 0) + 16
                o.sem_idx = ("dma", o.dma_key)
                o.sem_val = dma_cum[o.dma_key]
            elif o.marked:
                k = cnt[o.eng]
                o.sem_idx = (o.eng, k // SEM_CHUNK)
                o.sem_val = (k % SEM_CHUNK) + 1
                cnt[o.eng] = k + 1
        self.dma_final = dma_cum
        self.n_eng_sems = {e: (cnt[e] + SEM_CHUNK - 1) // SEM_CHUNK for e in ENGINES}
        return self

    def emit(self, nc):
        ops = self.ops
        handles = {}
        for e in ENGINES:
            for i in range(self.n_eng_sems[e]):
                handles[(e, i)] = nc.alloc_semaphore(name=_uname(f"s_{e}_{i}"))
        for j, k in enumerate(self.dma_final.keys()):
            handles[("dma", k)] = nc.alloc_semaphore(name=_uname(f"d_{j}"))
        by_eng = {e: [o for o in ops if o.eng == e] for e in ENGINES}
        dma_final = self.dma_final

        def make(engname):
            def body(eng):
                waited = {}
                for o in by_eng[engname]:
                    need = {}
                    for d in o.deps:
                        po = ops[d]
                        if need.get(po.sem_idx, 0) < po.sem_val:
                            need[po.sem_idx] = po.sem_val
                    for key, v in need.items():
                        if waited.get(key, 0) < v:
                            eng.wait_ge(handles[key], v)
                            waited[key] = v
                    name, args, kwargs = o.fn
                    inst = getattr(eng, name)(*args, **kwargs)
                    if o.dma_key is not None:
                        inst.then_inc(handles[o.sem_idx], 16)
                    elif o.marked:
                        inst.then_inc(handles[o.sem_idx], 1)
                if engname == "sp":
                    for k, v in dma_final.items():
                        eng.wait_ge(handles[("dma", k)], v)
            return body

        with nc.Block() as block:
            block.tensor(make("pe"))
            block.scalar(make("act"))
            block.vector(make("dve"))
            block.gpsimd(make("pool"))
            block.sync(make("sp"))
        nc.clear_and_free_semaphores(list(handles.values()))
        nc.all_engine_barrier()


def tile_list(TP):
    tiles = []
    for i in range(TP // TT):
        tiles.append(dict(col0=i * TT, nseq=1, L=TT, sample=False, first=(i == 0),
                          last=(i == TP // TT - 1)))
    tiles.append(dict(col0=TP, nseq=NSQ, L=LS, sample=True, first=False, last=False))
    return tiles


def build_nc(TP, n_stages=4, debug=False):
    TTOT = TP + TS
    nc = bass.Bass("TRN2", target_bir_lowering=False)

    def din(name, shape):
        return nc.dram_tensor(name, list(shape), F32, kind="ExternalInput").ap()

    def dout(name, shape):
        return nc.dram_tensor(name, list(shape), F32, kind="ExternalOutput").ap()

    D = {}
    D["xT"] = din("xT", [D_MODEL, TTOT])
    D["w_in"] = din("w_in", [DEPTH, D_MODEL, IN_DIM])
    D["w_out"] = din("w_out", [DEPTH, D_MODEL, D_MODEL])
    D["w_up"] = din("w_up", [DEPTH, D_MODEL, 2 * D_FF])
    D["w_down"] = din("w_down", [DEPTH, D_FF, D_MODEL])
    D["pp"] = din("pp", [DEPTH, 128, NPP])
    D["cstf"] = din("cstf", [128, NCF])
    D["cstb"] = din("cstb", [128, NCB])
    D["s_ssm"] = din("s_ssm", [DEPTH, NSQ, 128, 512])
    D["s_conv"] = din("s_conv", [DEPTH, 128, 8, NSQ, 3])
    D["s_kT"] = din("s_kT", [DEPTH, NSQ, 128, 128])
    D["s_v"] = din("s_v", [DEPTH, NSQ, 128, 128])
    D["s_ffn"] = din("s_ffn", [DEPTH, 128, 44, NSQ, 2])
    D["yT"] = dout("yT", [D_MODEL, TTOT])
    D["p_ssm"] = dout("p_ssm", [DEPTH, 128, 512])
    D["p_conv"] = dout("p_conv", [DEPTH, 128, 8, 3])
    D["p_kT"] = dout("p_kT", [DEPTH, 128, 128])
    D["p_v"] = dout("p_v", [DEPTH, 128, 128])
    D["p_ffn"] = dout("p_ffn", [DEPTH, 128, 44, 2])
    D["o_ssm"] = dout("o_ssm", [DEPTH, NSQ, 128, 512])
    D["o_conv"] = dout("o_conv", [DEPTH, 128, 8, NSQ, 3])
    D["o_kT"] = dout("o_kT", [DEPTH, NSQ, 128, 128])
    D["o_v"] = dout("o_v", [DEPTH, NSQ, 128, 128])
    D["o_ffn"] = dout("o_ffn", [DEPTH, 128, 44, NSQ, 2])
    if debug:
        D["xmid"] = dout("xmid", [DEPTH, D_MODEL, TTOT])
        D["x1"] = dout("x1", [D_MODEL, TTOT])
    else:
        D["xmid"] = nc.dram_tensor("xmid", [DEPTH, D_MODEL, TTOT], F32, kind="Internal").ap()
        D["x1"] = nc.dram_tensor("x1", [D_MODEL, TTOT], F32, kind="Internal").ap()

    tiles = tile_list(TP)
    stage = 0
    for l in range(DEPTH):
        xsrc = D["xT"] if l == 0 else D["x1"]
        if stage < n_stages:
            (mixer_stage_r1 if MIXER_IMPL == "r1" else mixer_stage)(nc, l, tiles, xsrc, D["xmid"][l], D)
        stage += 1
        xdst = D["x1"] if l == 0 else D["yT"]
        if stage < n_stages:
            ffn_stage(nc, l, tiles, D["xmid"][l], xdst, D)
        stage += 1
    return nc


def kp(ap):
    return ap.rearrange("(k p) t -> p k t", p=128)


def load_common(nc, S, st, l, D, want_bf=True):
    sb = lambda name, shape, dt: st.enter_context(nc.sbuf_tensor(_uname(name), shape, dt))
    C = {}
    C["pp"] = sb("pp", [128, NPP], F32)
    C["cb"] = sb("cb", [128, NCB], BF16)
    S.dma("pp", w=["pp"]).dma_start(out=C["pp"][:], in_=D["pp"][l])
    S.dma("cb", w=["cb"], eng="pool").dma_start(out=C["cb"][:], in_=D["cstb"])
    return C


def rmsnorm_tile(nc, S, xt_ap, xkey, W, Cst, B, PB, pkey, ncol, hn, sq, std, rstd, TTt):
    ones = Cst["cb"][:, B_ONES:B_ONES + 128]
    S.act(r=[xkey], w=["sq"]).activation(out=sq[:, :, 0:TTt], in_=xt_ap, func=AF.Square)
    for k in range(8):
        S.pe(r=["sq", "cb"], w=[pkey]).matmul(PB[:, 0:TTt], lhsT=ones, rhs=sq[:, k, 0:TTt], start=(k == 0), stop=(k == 7))
    S.act(r=[pkey, "eps"], w=["std"]).activation(out=std[:, 0:TTt], in_=PB[:, 0:TTt], func=AF.Sqrt, bias=Cst["eps"][:, 0:1],
                                 scale=1.0 / D_MODEL)
    S.dve(r=["std"], w=["rstd"]).reciprocal(out=rstd[:, 0:TTt], in_=std[:, 0:TTt])
    for k in range(8):
        S.dve(r=[xkey, "rstd", "pp"], w=[("hn", k)]).scalar_tensor_tensor(out=hn[:, k, 0:TTt], in0=xt_ap[:, k, :],
                                                    scalar=Cst["pp"][:, ncol + k:ncol + k + 1], in1=rstd[:, 0:TTt],
                                                    op0=ALU.mult, op1=ALU.mult)


def mixer_stage_r1(nc, l, tiles, xsrc, xdst, D):
    _UID[0] += 1
    with contextlib.ExitStack() as st:
        sb = lambda name, shape, dt: st.enter_context(nc.sbuf_tensor(_uname(name), shape, dt))
        S = Sched()
        C = load_common(nc, S, st, l, D)
        pp, cb = C["pp"], C["cb"]
        cf = sb("cf", [128, NCF], F32)
        S.dma("cf", w=["cf"]).dma_start(out=cf[:], in_=D["cstf"])
        TRI = cf[:, C_TRI:C_TRI + 128]
        ONESF = cf[:, C_ONES:C_ONES + 128]
        ETAB = cf[:, C_E:C_E + 2048].rearrange("p (h b q) -> p h b q", h=8, b=2)
        IDB = cb[:, B_ID:B_ID + 128]
        ONESB = cb[:, B_ONES:B_ONES + 128]
        BDB = cb[:, B_BD:B_BD + 128]
        eps_t = sb("eps_t", [128, 1], F32)
        S.dve(w=["eps"]).memset(eps_t[:], EPS)
        C["eps"] = eps_t

        Win = sb("Win", [128, 8, IN_DIM], BF16)
        Wout = sb("Wout", [128, 8, D_MODEL], BF16)
        for k in range(8):
            S.dma("win", w=["Win"], eng="pool", group=True).dma_start(out=Win[:, k, :], in_=D["w_in"][l, k * 128:(k + 1) * 128, :])
        for k in range(8):
            S.dma("wout", w=["Wout"], eng="pool", group=True).dma_start(out=Wout[:, k, :], in_=D["w_out"][l, k * 128:(k + 1) * 128, :])

        a_row = sb("a_row", [128, 8], F32)
        esink = sb("esink", [128, 4], F32)
        wq8 = sb("wq8", [128, 1], F32)
        S.act(r=["pp"], w=["a_row"]).activation(out=a_row[:], in_=pp[:, P_ALOG:P_ALOG + 8], func=AF.Exp)
        S.dve(r=["a_row"], w=["a_row"]).tensor_scalar(out=a_row[:], in0=a_row[:], scalar1=-1.0, scalar2=None, op0=ALU.mult)
        S.act(r=["pp"], w=["esink"]).activation(out=esink[:], in_=pp[:, P_SINK:P_SINK + 4], func=AF.Exp)
        S.dve(r=["pp"], w=["wq8"]).tensor_scalar(out=wq8[:], in0=pp[:, P_QN:P_QN + 1], scalar1=0.125, scalar2=None,
                                        op0=ALU.mult)

        xt = [sb(f"xt{i}", [128, 8, TT], F32) for i in range(2)]
        sq = sb("sq", [128, 8, TT], BF16)
        hn = sb("hn", [128, 8, TT], BF16)
        std = sb("std", [128, TT], F32)
        rstd = sb("rstd", [128, TT], F32)
        sz = sb("sz", [128, 4, TT], F32)
        pcb = sb("pcb", [128, 8, TT + 4], F32)
        pcar = sb("pcar", [128, 8, 3], F32)
        cv = [sb(f"cv{i}", [128, TT], F32) for i in range(2)]
        xbc = sb("xbc", [128, 8, TT], BF16)
        qk32 = sb("qk32", [128, 5, TT], F32)
        qn = sb("qn", [128, 4, TT], BF16)
        kn32 = sb("kn32", [128, TT], F32)
        kbuf = sb("kbuf", [128, 128 + TT], BF16)
        kcar = sb("kcar", [128, 128], BF16)
        kcache = [sb(f"kcache{i}", [128, 128], BF16) for i in range(2)]
        vcache = [sb(f"vcache{i}", [128, 128], BF16) for i in range(2)]
        yg = sb("yg", [128, 4, TT], F32)
        cssd = sb("cssd", [128, 4, TT], BF16)
        catt = sb("catt", [128, 4, TT], BF16)
        dtb = sb("dtb", [128, 8], F32)
        e1 = sb("e1", [128, 8], F32)
        dt = sb("dt", [128, 8], F32)
        dA = sb("dA", [128, 8], F32)
        acum = sb("acum", [128, 8], F32)
        diff = sb("diff", [128, 8], F32)
        tail = sb("tail", [128, 8], F32)
        cd = sb("cd", [128, 8], F32)
        dtt = sb("dtt", [128, 8], F32)
        dA_rep = sb("dA_rep", [128, 8, 128], F32)
        xtm = sb("xtm", [128, 768], BF16)
        xdt = sb("xdt", [128, 8, 64], BF16)
        xst = sb("xst", [128, 8, 64], BF16)
        cbm = sb("cbm", [128, 2, 128], F32)
        seg = sb("seg", [128, 8, 128], F32)
        MT = sb("MT", [128, 8, 128], BF16)
        eb = sb("eb", [128, 8, 128], F32)
        Cp = sb("Cp", [128, 8, 128], BF16)
        ygt = sb("ygt", [128, 4, 128], F32)
        Sst = [sb(f"Sst{i}", [128, 8, 64], F32) for i in range(2)]
        Sbf = sb("Sbf", [128, 8, 64], BF16)
        vb = [sb(f"vb{i}", [128, 128], BF16) for i in range(3)]
        v32 = sb("v32", [128, 128], F32)
        pexp = sb("pexp", [128, 4, 2, 128], F32)
        PT = sb("PT", [128, 4, 2, 128], BF16)
        den = sb("den", [128, 4, 128], F32)
        rden = sb("rden", [128, 4, 128], F32)

        PB = [st.enter_context(nc.psum_tensor(_uname(f"pb{i}"), [128, 512], F32)) for i in range(8)]
        small = PB[2]
        tr_ps = PB[3][:].bitcast(BF16)
        ns_ps = PB[3]
        acb_ps = [PB[4], PB[5]]
        sc_ps = [PB[4], PB[5]]
        y_ps = PB[6]
        o_ps = PB[6]
        den_ps = PB[7]
        pbrot = [0]

        def next_pb():
            i = pbrot[0] % 2
            pbrot[0] += 1
            return PB[i], ("PB", i)

        S.dve(w=["pcar"]).memset(pcar[:], 0.0)
        S.dve(w=[("S", 0)]).memset(Sst[0][:], 0.0)
        S.pool(w=["Sbf"]).memset(Sbf[:], 0.0)
        S.pool(w=["kcar"]).memset(kcar[:], 0.0)

        gchunk = [0]

        for ti, tl in enumerate(tiles):
            col0, nseq, L, sample = tl["col0"], tl["nseq"], tl["L"], tl["sample"]
            TTt = nseq * L
            Lc = min(L, 128)
            nj = L // Lc
            slot = ti % 2
            X = xt[slot]
            xkey = ("xt", slot)
            xap = X[:, :, 0:TTt]
            S.dma(("xld", slot), w=[xkey]).dma_start(out=X[:, :, 0:TTt], in_=kp(xsrc)[:, :, col0:col0 + TTt])
            pbt, pbk = next_pb()
            rmsnorm_tile(nc, S, xap, xkey, None, C, None, pbt, pbk, P_N1, hn, sq, std, rstd, TTt)
            hnk = [("hn", k) for k in range(8)]
            pcv = pcb[:, :, 0:nseq * (3 + L)].rearrange("p c (s t) -> p c s t", s=nseq)
            if sample:
                for c in range(8):
                    S.dma("sconv", r=[], w=["pcb_prev"], group=True).dma_start(out=pcv[:, c, :, 0:3], in_=D["s_conv"][l, :, c])
            else:
                S.pool(r=["pcar"], w=["pcb_prev"]).tensor_copy(out=pcv[:, :, 0, 0:3], in_=pcar[:])
            chunks = [("z", c, Z0 + c * 128) for c in range(4)] + [("x", c, XBC0 + c * 128) for c in range(8)] + \
                     [("q", c, Q0 + c * 128) for c in range(4)] + [("q", 4, K0)]
            for kind, c, wc in chunks:
                pbt, pbk = next_pb()
                for k in range(8):
                    S.pe(r=["Win"] + hnk, w=[pbk]).matmul(pbt[:, 0:TTt], lhsT=Win[:, k, wc:wc + 128],
                                                                 rhs=hn[:, k, 0:TTt], start=(k == 0), stop=(k == 7))
                if kind == "z":
                    S.act(r=[pbk], w=[("sz", c)]).activation(out=sz[:, c, 0:TTt], in_=pbt[:, 0:TTt], func=AF.Silu)
                elif kind == "x":
                    S.act(r=[pbk], w=[("pcb", c)]).activation(
                        out=pcv[:, c, :, 3:3 + L], in_=pbt[:, 0:TTt].rearrange("p (s t) -> p s t", s=nseq), func=AF.Copy)
                else:
                    S.dve(r=[pbk], w=[("qk32", c)]).tensor_copy(out=qk32[:, c, 0:TTt], in_=pbt[:, 0:TTt])
            for c in range(8):
                t = cv[c % 2]
                tk = ("cv", c % 2)
                t3 = t[:, 0:TTt].rearrange("p (s t) -> p s t", s=nseq)
                cw = lambda j, c=c: pp[:, P_CW + c * 4 + j:P_CW + c * 4 + j + 1]
                S.dve(r=[("pcb", c), "pp"], w=[tk]).tensor_scalar(
                    out=t3, in0=pcv[:, c, :, 3:3 + L], scalar1=cw(3), scalar2=pp[:, P_CB + c:P_CB + c + 1],
                    op0=ALU.mult, op1=ALU.add)
                for j in (2, 1, 0):
                    S.dve(r=[("pcb", c), "pcb_prev", "pp", tk], w=[tk]).scalar_tensor_tensor(
                        out=t3, in0=pcv[:, c, :, j:j + L], scalar=cw(j), in1=t3, op0=ALU.mult, op1=ALU.add)
                S.act(r=[tk], w=[("xbc", c)]).activation(out=xbc[:, c, 0:TTt], in_=t[:, 0:TTt], func=AF.Silu)
            allpcb = [("pcb", c) for c in range(8)]
            if sample:
                for c in range(8):
                    S.dma("oconv", r=allpcb + ["pcb_prev"], w=["oconv"], group=True).dma_start(out=D["o_conv"][l, :, c], in_=pcv[:, c, :, L:L + 3])
            else:
                S.pool(r=allpcb + ["pcb_prev"],
                       w=["pcar"]).tensor_copy(out=pcar[:], in_=pcv[:, :, 0, L:L + 3])
                if tl["last"]:
                    S.dma("pconv", r=["pcar"]).dma_start(out=D["p_conv"][l], in_=pcar[:])
            if not sample:
                S.pool(r=["kcar"], w=["kbuf_prev"]).tensor_copy(out=kbuf[:, 0:128], in_=kcar[:])
            for c in range(5):
                pbt, pbk = next_pb()
                S.act(r=[("qk32", c)], w=["sq"]).activation(out=sq[:, 0, 0:TTt], in_=qk32[:, c, 0:TTt], func=AF.Square)
                S.pe(r=["sq", "cb"], w=[pbk]).matmul(pbt[:, 0:TTt], lhsT=BDB, rhs=sq[:, 0, 0:TTt], start=True, stop=True)
                S.act(r=[pbk, "eps"], w=["std"]).activation(out=std[:, 0:TTt], in_=pbt[:, 0:TTt], func=AF.Sqrt,
                                                      bias=eps_t[:, 0:1], scale=1.0 / 64)
                S.dve(r=["std"], w=["rstd"]).reciprocal(out=rstd[:, 0:TTt], in_=std[:, 0:TTt])
                if c < 4:
                    S.dve(r=[("qk32", c), "wq8", "rstd"], w=[("qn", c)]).scalar_tensor_tensor(out=qn[:, c, 0:TTt], in0=qk32[:, c, 0:TTt], scalar=wq8[:, 0:1],
                                                                in1=rstd[:, 0:TTt], op0=ALU.mult, op1=ALU.mult)
                else:
                    S.dve(r=[("qk32", 4), "pp", "rstd"], w=["kn32"]).scalar_tensor_tensor(out=kn32[:, 0:TTt], in0=qk32[:, 4, 0:TTt],
                                                           scalar=pp[:, P_KN:P_KN + 1], in1=rstd[:, 0:TTt],
                                                           op0=ALU.mult, op1=ALU.mult)
                    S.act(r=["kn32"], w=["kbuf"]).activation(out=kbuf[:, 128:128 + TTt], in_=kn32[:, 0:TTt], func=AF.Copy)
            if not sample:
                S.pool(r=["kbuf", "kbuf_prev"], w=["kcar"]).tensor_copy(out=kcar[:], in_=kbuf[:, TTt:TTt + 128])
                if tl["last"]:
                    S.dma("pk", r=["kn32"]).dma_start(out=D["p_kT"][l], in_=kn32[:, TTt - 128:TTt])
            else:
                for s in range(nseq):
                    S.dma("okc", group=True, w=["okc"]).dma_start(out=D["o_kT"][l, s, :, 0:124], in_=D["s_kT"][l, s, :, 4:128])
                    S.dma("okn", group=True, r=["kn32"], w=["okn"]).dma_start(out=D["o_kT"][l, s, :, 124:128], in_=kn32[:, s * LS:(s + 1) * LS])
                    S.dma("ovc", group=True, w=["ovc"]).dma_start(out=D["o_v"][l, s, 0:124, :], in_=D["s_v"][l, s, 4:128, :])

            for s in range(nseq):
                if sample:
                    sslot = s % 2
                    Scur, skey = Sst[sslot], ("S", sslot)
                    S.dma(("sld", sslot), w=[skey]).dma_start(out=Scur[:].rearrange("p h d -> p (h d)"),
                                                                in_=D["s_ssm"][l, s])
                    S.act(r=[skey], w=["Sbf"]).activation(out=Sbf[:], in_=Scur[:], func=AF.Copy)
                    kc, vc = kcache[sslot], vcache[sslot]
                    S.dma(("kcl", sslot),
                          w=[("kcache", sslot)], eng="pool").dma_start(out=kc[:], in_=D["s_kT"][l, s])
                    S.dma(("vcl", sslot),
                          w=[("vcache", sslot)], eng="pool").dma_start(out=vc[:], in_=D["s_v"][l, s])
                else:
                    Scur, skey = Sst[0], ("S", 0)
                for j in range(nj):
                    g = gchunk[0]
                    gchunk[0] += 1
                    c0 = s * L + j * Lc
                    cs = slice(c0, c0 + Lc)
                    has_prev = sample or not (tl["first"] and j == 0)
                    for k in range(8):
                        S.pe(r=["Win"] + hnk, w=[("PB", 2)]).matmul(small[0:Lc, 0:8], lhsT=hn[:, k, cs], rhs=Win[:, k, DT0:DT0 + 8],
                                                            start=(k == 0), stop=(k == 7))
                    for k in range(8):
                        S.pe(r=["Win"] + hnk, w=[("PB", 7)]).matmul(PB[7][0:Lc, 0:128], lhsT=hn[:, k, cs], rhs=Win[:, k, V0:V0 + 128],
                                                            start=(k == 0), stop=(k == 7))
                    S.dve(r=[("PB", 2), "pp"], w=["dtb"]).tensor_tensor(out=dtb[0:Lc, :], in0=small[0:Lc, 0:8], in1=pp[0:Lc, P_DTB:P_DTB + 8],
                                                    op=ALU.add)
                    S.act(r=["dtb"], w=["e1"]).activation(out=e1[0:Lc, :], in_=dtb[0:Lc, :], func=AF.Exp)
                    S.act(r=["e1"], w=["dt"]).activation(out=dt[0:Lc, :], in_=e1[0:Lc, :], func=AF.Ln, bias=1.0)
                    S.dve(r=["dt", "a_row"], w=["dA"]).tensor_tensor(out=dA[0:Lc, :], in0=dt[0:Lc, :], in1=a_row[0:Lc, :], op=ALU.mult)
                    S.pe(r=["dA", "cf"], w=[("PB", 2)]).matmul(small[0:Lc, 144:152], lhsT=TRI[0:Lc, 0:Lc], rhs=dA[0:Lc, :], start=True, stop=True)
                    S.pe(r=["dA", "cf"], w=[("PB", 2)]).matmul(small[:, 152:160], lhsT=ONESF[0:Lc, :], rhs=dA[0:Lc, :], start=True, stop=True)
                    S.dve(r=[("PB", 2)], w=["acum"]).tensor_copy(out=acum[0:Lc, :], in_=small[0:Lc, 144:152])
                    S.dve(r=[("PB", 2), "acum"], w=["diff"]).tensor_tensor(out=diff[0:Lc, :], in0=small[0:Lc, 152:160], in1=acum[0:Lc, :],
                                                    op=ALU.subtract)
                    S.act(r=["diff"], w=["tail"]).activation(out=tail[0:Lc, :], in_=diff[0:Lc, :], func=AF.Exp)
                    S.act(r=[("PB", 2)], w=["cd"]).activation(out=cd[:, :], in_=small[:, 152:160], func=AF.Exp)
                    S.dve(r=["dt", "tail"], w=["dtt"]).tensor_tensor(out=dtt[0:Lc, :], in0=dt[0:Lc, :], in1=tail[0:Lc, :], op=ALU.mult)
                    S.dve(r=["dA"], w=["dA_rep"]).tensor_copy(out=dA_rep[0:Lc, :, :], in_=dA[0:Lc, :].unsqueeze(2).to_broadcast([Lc, 8, 128]))
                    for h in range(8):
                        S.pe(r=["dA_rep", "cf"], w=[("PB", 4 + h // 4)]).matmul(acb_ps[h // 4][:, (h % 4) * 128:(h % 4) * 128 + Lc],
                                                     lhsT=dA_rep[0:Lc, h, :], rhs=TRI[0:Lc, 0:Lc], start=True, stop=True)
                    for i in range(6):
                        S.pe(r=[("xbc", i), "cb"], w=[("PB", 3)]).transpose(tr_ps[0:Lc, i * 128:(i + 1) * 128], xbc[:, i, cs], IDB)
                    S.act(r=[("PB", 3)], w=["xtm"]).activation(out=xtm[0:Lc, :], in_=tr_ps[0:Lc, 0:768], func=AF.Copy)
                    xtm3 = xtm[0:Lc, 0:512].rearrange("p (h d) -> p h d", h=8)
                    S.dve(r=["xtm", "dt"], w=["xdt"]).tensor_tensor(out=xdt[0:Lc], in0=xtm3,
                                                               in1=dt[0:Lc, :].unsqueeze(2).to_broadcast([Lc, 8, 64]), op=ALU.mult)
                    S.dve(r=["xtm", "dtt"], w=["xst"]).tensor_tensor(out=xst[0:Lc], in0=xtm3,
                                                               in1=dtt[0:Lc, :].unsqueeze(2).to_broadcast([Lc, 8, 64]), op=ALU.mult)
                    for gidx in range(2):
                        S.pe(r=[("xbc", 4 + gidx), ("xbc", 6 + gidx)], w=[("PB", 2)]).matmul(small[0:Lc, 160 + gidx * 128:160 + gidx * 128 + Lc],
                                                                  lhsT=xbc[:, 4 + gidx, cs], rhs=xbc[:, 6 + gidx, cs],
                                                                  start=True, stop=True)
                    S.dve(r=[("PB", 2), "cf"], w=["cbm"]).tensor_tensor(out=cbm[0:Lc, :, 0:Lc],
                                                    in0=small[0:Lc, 160:416].rearrange("p (g q) -> p g q", g=2)[:, :, 0:Lc],
                                                    in1=TRI[0:Lc, 0:Lc].unsqueeze(1).to_broadcast([Lc, 2, Lc]), op=ALU.mult)
                    for h in range(8):
                        S.dve(r=[("PB", 4 + h // 4), "acum"], w=["seg"]).tensor_scalar(out=seg[0:Lc, h, 0:Lc],
                                                             in0=acb_ps[h // 4][0:Lc, (h % 4) * 128:(h % 4) * 128 + Lc],
                                                             scalar1=acum[0:Lc, h:h + 1], scalar2=0.0,
                                                             op0=ALU.subtract, op1=ALU.min)
                    S.act(r=["seg"], w=["seg"]).activation(out=seg[0:Lc, :, 0:Lc], in_=seg[0:Lc, :, 0:Lc], func=AF.Exp)
                    S.dve(r=["seg", "cbm"], w=["MT"]).tensor_tensor(out=MT[0:Lc, :, 0:Lc].rearrange("p (g r) q -> p g r q", g=2),
                                                    in0=seg[0:Lc, :, 0:Lc].rearrange("p (g r) q -> p g r q", g=2),
                                                    in1=cbm[0:Lc, :, 0:Lc].unsqueeze(2).to_broadcast([Lc, 2, 4, Lc]), op=ALU.mult)
                    for hh in range(2):
                        S.act(r=[("PB", 4 + hh)], w=[("eb", hh)]).activation(
                            out=eb[:, hh * 4:(hh + 1) * 4, 0:Lc],
                            in_=acb_ps[hh][:, :].rearrange("p (r q) -> p r q", r=4)[:, :, 0:Lc], func=AF.Exp)
                        S.dve(r=[("eb", hh), ("xbc", 6 + hh)], w=[("Cp", hh)]).tensor_tensor(
                            out=Cp[:, hh * 4:(hh + 1) * 4, 0:Lc], in0=eb[:, hh * 4:(hh + 1) * 4, 0:Lc],
                            in1=xbc[:, 6 + hh, cs].unsqueeze(1).to_broadcast([128, 4, Lc]), op=ALU.mult)
                    for h in range(8):
                        pr = (h % 2) * 64
                        yo = y_ps[pr:pr + 64, (h // 2) * 128:(h // 2) * 128 + Lc]
                        S.pe(r=["xdt", "MT"], w=[("PB", 6)]).matmul(yo, lhsT=xdt[0:Lc, h, :], rhs=MT[0:Lc, h, 0:Lc], start=True, stop=False)
                        S.pe(r=["Sbf", ("Cp", h // 4)], w=[("PB", 6)]).matmul(yo, lhsT=Sbf[:, h, :], rhs=Cp[:, h, 0:Lc], start=False, stop=True)
                    for c in range(4):
                        S.dve(r=[("xbc", c), "pp", ("PB", 6)], w=["ygt"]).scalar_tensor_tensor(
                            out=ygt[:, c, 0:Lc], in0=xbc[:, c, cs], scalar=pp[:, P_DSK + c:P_DSK + c + 1],
                            in1=y_ps[:, c * 128:c * 128 + Lc], op0=ALU.mult, op1=ALU.add)
                    S.dve(r=["ygt"] + [("sz", c) for c in range(4)], w=[("yg", g)]).tensor_tensor(out=yg[:, :, cs], in0=ygt[:, :, 0:Lc], in1=sz[:, :, cs], op=ALU.mult)
                    for gidx in range(2):
                        S.pe(r=["xtm", "xst"], w=[("PB", 3)]).matmul(ns_ps[:, gidx * 256:(gidx + 1) * 256],
                                                           lhsT=xtm[0:Lc, 512 + gidx * 128:512 + (gidx + 1) * 128],
                                                           rhs=xst[0:Lc, gidx * 4:(gidx + 1) * 4, :], start=True, stop=True)
                    S.dve(r=[skey, "cd"], w=[skey]).tensor_tensor(out=Scur[:], in0=Scur[:],
                                                               in1=cd[:, :].unsqueeze(2).to_broadcast([128, 8, 64]), op=ALU.mult)
                    S.dve(r=[skey, ("PB", 3)], w=[skey]).tensor_tensor(out=Scur[:], in0=Scur[:],
                                                               in1=ns_ps[:, :].rearrange("p (h d) -> p h d", h=8), op=ALU.add)
                    if sample:
                        S.dma(("sst", s % 2), r=[skey]).dma_start(out=D["o_ssm"][l, s], in_=Scur[:].rearrange("p h d -> p (h d)"))
                    else:
                        S.act(r=[skey], w=["Sbf"]).activation(out=Sbf[:], in_=Scur[:], func=AF.Copy)
                        if tl["last"] and j == nj - 1:
                            S.dma("pssm", r=[skey]).dma_start(out=D["p_ssm"][l], in_=Scur[:].rearrange("p h d -> p (h d)"))

                    vown = vb[g % 3]
                    vkey = ("vb", g % 3)
                    S.act(r=[("PB", 7)], w=[vkey]).activation(out=vown[0:Lc, :], in_=PB[7][0:Lc, 0:128], func=AF.Copy)
                    if sample:
                        S.dve(r=[("PB", 7)], w=["v32"]).tensor_copy(out=v32[0:Lc, :], in_=PB[7][0:Lc, 0:128])
                        S.dma("ovn", r=["v32"]).dma_start(out=D["o_v"][l, s, 124:128, :], in_=v32[0:Lc, :])
                        kprev, kpkey = kcache[s % 2], ("kcache", s % 2)
                        vprev, vpkey = vcache[s % 2], ("vcache", s % 2)
                        kown = kbuf[:, 128 + c0:128 + c0 + Lc]
                        kpre = kprev[:, :]
                    else:
                        if tl["last"] and j == nj - 1:
                            S.dve(r=[("PB", 7)], w=["v32"]).tensor_copy(out=v32[0:Lc, :], in_=PB[7][0:Lc, 0:128])
                            S.dma("pv", r=["v32"]).dma_start(out=D["p_v"][l], in_=v32[:, :])
                        kpre, kpkey = kbuf[:, j * 128:(j + 1) * 128], "kbuf_prev" if j == 0 else "kbuf"
                        kown = kbuf[:, 128 + j * 128:128 + (j + 1) * 128]
                        vprev, vpkey = vb[(g - 1) % 3], ("vb", (g - 1) % 3)
                    for hg in range(2):
                        pr = hg * 64
                        for i in range(4):
                            bank = sc_ps[i // 2]
                            bk = ("PB", 4 + i // 2)
                            base = (i % 2) * 256
                            qh = qn[pr:pr + 64, i, cs]
                            if has_prev:
                                S.pe(r=[kpkey, ("qn", i)], w=[bk]).matmul(
                                    bank[:, base:base + Lc], lhsT=kpre[pr:pr + 64, :], rhs=qh, start=True, stop=True)
                            S.pe(r=["kbuf", ("qn", i)], w=[bk]).matmul(
                                bank[0:Lc, base + 128:base + 128 + Lc], lhsT=kown[pr:pr + 64, :], rhs=qh, start=True, stop=True)
                        for bi in range(2):
                            scv = sc_ps[bi][:, :].rearrange("p (i b q) -> p i b q", i=2, b=2)
                            if has_prev:
                                S.act(r=[("PB", 4 + bi)], w=[("pexp", bi)]).activation(out=pexp[:, 2 * bi:2 * bi + 2, 0, 0:Lc],
                                                                             in_=scv[:, :, 0, 0:Lc], func=AF.Exp)
                            S.act(r=[("PB", 4 + bi)], w=[("pexp", bi)]).activation(out=pexp[0:Lc, 2 * bi:2 * bi + 2, 1, 0:Lc],
                                                                         in_=scv[0:Lc, :, 1, 0:Lc], func=AF.Exp)
                        pkeys = [("pexp", 0), ("pexp", 1)]
                        if has_prev:
                            S.dve(r=pkeys + ["cf"], w=["PT"]).tensor_tensor(out=PT[:, :, 0, 0:Lc], in0=pexp[:, :, 0, 0:Lc],
                                                                   in1=ETAB[:, 4 * hg:4 * hg + 4, 0, 0:Lc], op=ALU.mult)
                        S.dve(r=pkeys + ["cf"], w=["PT"]).tensor_tensor(out=PT[0:Lc, :, 1, 0:Lc], in0=pexp[0:Lc, :, 1, 0:Lc],
                                                               in1=ETAB[0:Lc, 4 * hg:4 * hg + 4, 1, 0:Lc], op=ALU.mult)
                        for i in range(4):
                            oo = o_ps[pr:pr + 64, i * 128:i * 128 + Lc]
                            dd = den_ps[pr:pr + 64, i * 128:i * 128 + Lc]
                            if has_prev:
                                S.pe(r=[vpkey, "PT"], w=[("PB", 6)]).matmul(
                                    oo, lhsT=vprev[:, pr:pr + 64], rhs=PT[:, i, 0, 0:Lc], start=True, stop=False)
                            S.pe(r=[vkey, "PT"], w=[("PB", 6)]).matmul(
                                oo, lhsT=vown[0:Lc, pr:pr + 64], rhs=PT[0:Lc, i, 1, 0:Lc], start=(not has_prev), stop=True)
                            if has_prev:
                                S.pe(r=["cb", "PT"], w=[("PB", 7)]).matmul(dd, lhsT=ONESB[:, 0:64], rhs=PT[:, i, 0, 0:Lc],
                                                                    start=True, stop=False)
                            S.pe(r=["cb", "PT"], w=[("PB", 7)]).matmul(dd, lhsT=ONESB[0:Lc, 0:64], rhs=PT[0:Lc, i, 1, 0:Lc],
                                                                start=(not has_prev), stop=True)
                    S.dve(r=[("PB", 7), "esink"], w=["den"]).tensor_tensor(out=den[:, :, 0:Lc],
                                                    in0=den_ps[:, :].rearrange("p (i q) -> p i q", i=4)[:, :, 0:Lc],
                                                    in1=esink[:, :].unsqueeze(2).to_broadcast([128, 4, Lc]), op=ALU.add)
                    S.dve(r=["den"], w=["rden"]).reciprocal(out=rden[:, :, 0:Lc], in_=den[:, :, 0:Lc])
                    S.dve(r=[("PB", 6), "rden"], w=[("catt", g)]).tensor_tensor(out=catt[:, :, cs],
                                                           in0=o_ps[:, :].rearrange("p (i q) -> p i q", i=4)[:, :, 0:Lc],
                                                           in1=rden[:, :, 0:Lc], op=ALU.mult)
            ng = nseq * nj
            gl = [gchunk[0] - ng + i for i in range(ng)]
            ygk = [("yg", g) for g in gl]
            cak = [("catt", g) for g in gl]
            S.act(r=ygk, w=["sq"]).activation(out=sq[:, 0:4, 0:TTt], in_=yg[:, :, 0:TTt], func=AF.Square)
            for gi in range(2):
                pbt, pbk = next_pb()
                for cc in range(2):
                    S.pe(r=["sq", "cb"], w=[pbk]).matmul(pbt[:, 0:TTt], lhsT=ONESB, rhs=sq[:, 2 * gi + cc, 0:TTt],
                                                                   start=(cc == 0), stop=(cc == 1))
                S.act(r=[pbk, "eps"], w=["std"]).activation(out=std[:, 0:TTt], in_=pbt[:, 0:TTt], func=AF.Sqrt, bias=eps_t[:, 0:1],
                                                      scale=1.0 / 256)
                S.dve(r=["std"], w=["rstd"]).reciprocal(out=rstd[:, 0:TTt], in_=std[:, 0:TTt])
                for cc in range(2):
                    c = 2 * gi + cc
                    S.dve(r=ygk + ["pp", "rstd"], w=[("cssd", c)]).scalar_tensor_tensor(out=cssd[:, c, 0:TTt], in0=yg[:, c, 0:TTt],
                                                                scalar=pp[:, P_SNW + c:P_SNW + c + 1], in1=rstd[:, 0:TTt],
                                                                op0=ALU.mult, op1=ALU.mult)
            for m in range(8):
                pbt, pbk = next_pb()
                for c in range(8):
                    src = cssd[:, c, 0:TTt] if c < 4 else catt[:, c - 4, 0:TTt]
                    rk = [("cssd", c)] if c < 4 else cak
                    S.pe(r=["Wout"] + rk, w=[pbk]).matmul(pbt[:, 0:TTt], lhsT=Wout[:, c, m * 128:(m + 1) * 128],
                                                                        rhs=src, start=(c == 0), stop=(c == 7))
                S.dve(r=[pbk, xkey], w=[xkey]).tensor_tensor(out=X[:, m, 0:TTt], in0=pbt[:, 0:TTt], in1=X[:, m, 0:TTt],
                                                                   op=ALU.add)
            S.dma(("xst", slot), r=[xkey]).dma_start(out=kp(xdst)[:, :, col0:col0 + TTt], in_=X[:, :, 0:TTt])
        S.analyze().emit(nc)


def interleave(*gens):
    gens = [g for g in gens if g is not None]
    if os.environ.get("KSEQ"):
        for g in gens:
            for _ in g:
                pass
        return
    while gens:
        for g in list(gens):
            try:
                next(g)
            except StopIteration:
                gens.remove(g)


def rms_rstd(S, eps_t, ps_ap, pkey, ln_ap, lnkey, out_ap, okey, inv_n):
    if os.environ.get("KRSQ"):
        S.act(r=[pkey, "eps"], w=[lnkey]).activation(out=ln_ap, in_=ps_ap, func=AF.Sqrt, bias=eps_t[:, 0:1], scale=inv_n)
        S.dve(r=[lnkey], w=[okey]).reciprocal(out=out_ap, in_=ln_ap)
        return
    S.act(r=[pkey, "eps"], w=[lnkey]).activation(out=ln_ap, in_=ps_ap, func=AF.Ln, bias=eps_t[:, 0:1], scale=inv_n)
    S.act(r=[lnkey], w=[okey]).activation(out=out_ap, in_=ln_ap, func=AF.Exp, scale=-0.5)


def mixer_stage(nc, l, tiles, xsrc, xdst, D):
    _UID[0] += 1
    with contextlib.ExitStack() as st:
        sb = lambda name, shape, dt: st.enter_context(nc.sbuf_tensor(_uname(name), shape, dt))
        S = Sched()
        C = load_common(nc, S, st, l, D)
        pp, cb = C["pp"], C["cb"]
        cf = sb("cf", [128, NCF], F32)
        S.dma("cf", w=["cf"]).dma_start(out=cf[:], in_=D["cstf"])
        TRI = cf[:, C_TRI:C_TRI + 128]
        ONESF = cf[:, C_ONES:C_ONES + 128]
        ETAB = cf[:, C_E:C_E + 2048].rearrange("p (h b q) -> p h b q", h=8, b=2)
        IDB = cb[:, B_ID:B_ID + 128]
        ONESB = cb[:, B_ONES:B_ONES + 128]
        BDB = cb[:, B_BD:B_BD + 128]
        eps_t = sb("eps_t", [128, 1], F32)
        S.dve(w=["eps"]).memset(eps_t[:], EPS)

        Win = sb("Win", [128, 8, IN_DIM], BF16)
        Wout = sb("Wout", [128, 8, D_MODEL], BF16)
        for k in range(8):
            S.dma("win", w=["Win"], eng="pool", group=True).dma_start(out=Win[:, k, :], in_=D["w_in"][l, k * 128:(k + 1) * 128, :])
        for k in range(8):
            S.dma("wout", w=["Wout"], eng="pool", group=True).dma_start(out=Wout[:, k, :], in_=D["w_out"][l, k * 128:(k + 1) * 128, :])

        a_row = sb("a_row", [128, 8], F32)
        esink = sb("esink", [128, 4], F32)
        wq8 = sb("wq8", [128, 1], F32)
        S.act(r=["pp"], w=["a_row"]).activation(out=a_row[:], in_=pp[:, P_ALOG:P_ALOG + 8], func=AF.Exp)
        S.dve(r=["a_row"], w=["a_row"]).tensor_scalar(out=a_row[:], in0=a_row[:], scalar1=-1.0, scalar2=None, op0=ALU.mult)
        S.act(r=["pp"], w=["esink"]).activation(out=esink[:], in_=pp[:, P_SINK:P_SINK + 4], func=AF.Exp)
        S.dve(r=["pp"], w=["wq8"]).tensor_scalar(out=wq8[:], in0=pp[:, P_QN:P_QN + 1], scalar1=0.125, scalar2=None, op0=ALU.mult)

        xt = [sb(f"xt{i}", [128, 8, TT], F32) for i in range(2)]
        sq = sb("sq", [128, 8, TT], BF16)
        hn = sb("hn", [128, 8, TT], BF16)
        lnb0 = sb("lnb0", [128, TT], F32)
        rstd0 = sb("rstd0", [128, TT], F32)
        lnb = sb("lnb", [128, 2 * TT], F32)
        rstd = sb("rstd", [128, 2 * TT], F32)
        sq2 = sb("sq2", [128, 4, TT], BF16)
        lnb2 = sb("lnb2", [128, TT], F32)
        rstd2 = sb("rstd2", [128, TT], F32)
        sz = sb("sz", [128, 4, TT], F32)
        pcb = sb("pcb", [128, 8, TT + 4], F32)
        pcar = sb("pcar", [128, 8, 3], F32)
        cv = [sb(f"cv{i}", [128, TT], F32) for i in range(2)]
        xbc = sb("xbc", [128, 8, TT], BF16)
        qk32 = sb("qk32", [128, 5, TT], F32)
        qn = sb("qn", [128, 4, TT], BF16)
        kn32 = sb("kn32", [128, TT], F32)
        kbuf = sb("kbuf", [128, 128 + TT], BF16)
        kcar = sb("kcar", [128, 128], BF16)
        kcache = [sb(f"kcache{i}", [128, 128], BF16) for i in range(2)]
        vcache = [sb(f"vcache{i}", [128, 128], BF16) for i in range(2)]
        yg = sb("yg", [128, 4, TT], F32)
        cssd = sb("cssd", [128, 4, TT], BF16)
        catt = sb("catt", [128, 4, TT], BF16)
        dtb = sb("dtb", [128, 8], F32)
        e1 = sb("e1", [128, 8], F32)
        dt = sb("dt", [128, 8], F32)
        dA = sb("dA", [128, 8], F32)
        acum = sb("acum", [128, 8], F32)
        diff = sb("diff", [128, 8], F32)
        tail = sb("tail", [128, 8], F32)
        cd = sb("cd", [128, 8], F32)
        dtt = sb("dtt", [128, 8], F32)
        dA_rep = sb("dA_rep", [128, 8, 128], F32)
        xtm = sb("xtm", [128, 768], BF16)
        xdt = sb("xdt", [128, 8, 64], BF16)
        xst = sb("xst", [128, 8, 64], BF16)
        cbm = sb("cbm", [128, 2, 128], F32)
        seg = sb("seg", [128, 8, 128], F32)
        MT = sb("MT", [128, 8, 128], BF16)
        eb = sb("eb", [128, 8, 128], F32)
        Cp = sb("Cp", [128, 8, 128], BF16)
        ygt = sb("ygt", [128, 4, 128], F32)
        Sst = [sb(f"Sst{i}", [128, 8, 64], F32) for i in range(2)]
        Sbf = sb("Sbf", [128, 8, 64], BF16)
        vb = [sb(f"vb{i}", [128, 128], BF16) for i in range(3)]
        v32 = sb("v32", [128, 128], F32)
        pexp = [sb(f"pexp{i}", [128, 2, 2, 128], F32) for i in range(2)]
        PT = [sb(f"PT{i}", [128, 2, 2, 128], BF16) for i in range(2)]
        den = sb("den", [128, 4, 128], F32)
        lnd = sb("lnd", [128, 4, 128], F32)
        rden = sb("rden", [128, 4, 128], F32)

        PB = [st.enter_context(nc.psum_tensor(_uname(f"pb{i}"), [128, 512], F32)) for i in range(8)]
        small = PB[2]
        tr_ps = PB[3][:].bitcast(BF16)
        ns_ps = PB[3]
        y_ps = PB[3]
        acb_ps = PB[4]
        sc_ps = PB[5]
        o_ps = PB[6]
        den_ps = PB[7]
        K3, K4, K5, K6, K7 = ("PB", 3), ("PB", 4), ("PB", 5), ("PB", 6), ("PB", 7)
        pbrot = [0]

        def next_pb():
            i = pbrot[0] % 2
            pbrot[0] += 1
            return PB[i], ("PB", i)

        S.dve(w=["pcar"]).memset(pcar[:], 0.0)
        S.dve(w=[("S", 0)]).memset(Sst[0][:], 0.0)
        S.pool(w=["Sbf"]).memset(Sbf[:], 0.0)
        S.pool(w=["kcar"]).memset(kcar[:], 0.0)

        gchunk = [0]
        hnk = [("hn", k) for k in range(8)]
        rndc = [0]

        def front(ti):
            tl = tiles[ti]
            col0, nseq, L, sample = tl["col0"], tl["nseq"], tl["L"], tl["sample"]
            TTt = nseq * L
            slot = ti % 2
            X = xt[slot]
            xkey = ("xt", slot)
            xap = X[:, :, 0:TTt]
            S.dma(("xld", slot), w=[xkey]).dma_start(out=X[:, :, 0:TTt], in_=kp(xsrc)[:, :, col0:col0 + TTt])
            yield
            S.pool(r=[xkey], w=["sq"]).tensor_tensor(out=sq[:, :, 0:TTt], in0=xap, in1=xap, op=ALU.mult)
            yield
            pbt, pbk = next_pb()
            for k in range(8):
                S.pe(r=["sq", "cb"], w=[pbk]).matmul(pbt[:, 0:TTt], lhsT=ONESB, rhs=sq[:, k, 0:TTt], start=(k == 0), stop=(k == 7))
            yield
            rms_rstd(S, eps_t, pbt[:, 0:TTt], pbk, lnb0[:, 0:TTt], "lnb0", rstd0[:, 0:TTt], "rstd0", 1.0 / D_MODEL)
            yield
            for k in range(8):
                S.dve(r=[xkey, "rstd0", "pp"], w=[("hn", k)]).scalar_tensor_tensor(
                    out=hn[:, k, 0:TTt], in0=xap[:, k, :], scalar=pp[:, P_N1 + k:P_N1 + k + 1], in1=rstd0[:, 0:TTt],
                    op0=ALU.mult, op1=ALU.mult)
                if k % 2 == 1:
                    yield
            pcv = pcb[:, :, 0:nseq * (3 + L)].rearrange("p c (s t) -> p c s t", s=nseq)
            if sample:
                for c in range(8):
                    S.dma("sconv", r=[], w=["pcb_prev"], group=True).dma_start(out=pcv[:, c, :, 0:3], in_=D["s_conv"][l, :, c])
            else:
                S.pool(r=["pcar"], w=["pcb_prev"]).tensor_copy(out=pcv[:, :, 0, 0:3], in_=pcar[:])
                S.pool(r=["kcar"], w=["kbuf_prev"]).tensor_copy(out=kbuf[:, 0:128], in_=kcar[:])
            yield

            def proj(kind, c, wc):
                pbt, pbk = next_pb()
                for k in range(8):
                    S.pe(r=["Win"] + hnk, w=[pbk]).matmul(pbt[:, 0:TTt], lhsT=Win[:, k, wc:wc + 128], rhs=hn[:, k, 0:TTt],
                                                          start=(k == 0), stop=(k == 7))
                if kind == "z":
                    S.act(r=[pbk], w=[("sz", c)]).activation(out=sz[:, c, 0:TTt], in_=pbt[:, 0:TTt], func=AF.Silu)
                elif kind == "x":
                    S.act(r=[pbk], w=[("pcb", c)]).activation(out=pcv[:, c, :, 3:3 + L],
                                                             in_=pbt[:, 0:TTt].rearrange("p (s t) -> p s t", s=nseq), func=AF.Copy)
                else:
                    S.dve(r=[pbk], w=[("qk32", c)]).tensor_copy(out=qk32[:, c, 0:TTt], in_=pbt[:, 0:TTt])

            for c in range(4):
                proj("q", c, Q0 + c * 128)
                yield
            proj("q", 4, K0)
            yield
            S.pool(r=[("qk32", c) for c in range(5)], w=["sq"]).tensor_tensor(out=sq[:, 0:5, 0:TTt], in0=qk32[:, :, 0:TTt],
                                                                             in1=qk32[:, :, 0:TTt], op=ALU.mult)
            for c in range(8):
                proj("x", c, XBC0 + c * 128)
                yield
            for grp in [(0, 1), (2, 3), (4,)]:
                pbt, pbk = next_pb()
                n = len(grp)
                for i, c in enumerate(grp):
                    S.pe(r=["sq", "cb"], w=[pbk]).matmul(pbt[:, i * TTt:(i + 1) * TTt], lhsT=BDB, rhs=sq[:, c, 0:TTt], start=True, stop=True)
                rms_rstd(S, eps_t, pbt[:, 0:n * TTt], pbk, lnb[:, 0:n * TTt], "lnb", rstd[:, 0:n * TTt], "rstd", 1.0 / 64)
                for i, c in enumerate(grp):
                    rs = rstd[:, i * TTt:(i + 1) * TTt]
                    if c < 4:
                        S.dve(r=[("qk32", c), "wq8", "rstd"], w=[("qn", c)]).scalar_tensor_tensor(
                            out=qn[:, c, 0:TTt], in0=qk32[:, c, 0:TTt], scalar=wq8[:, 0:1], in1=rs, op0=ALU.mult, op1=ALU.mult)
                    else:
                        S.dve(r=[("qk32", 4), "pp", "rstd"], w=["kn32"]).scalar_tensor_tensor(
                            out=kn32[:, 0:TTt], in0=qk32[:, 4, 0:TTt], scalar=pp[:, P_KN:P_KN + 1], in1=rs, op0=ALU.mult, op1=ALU.mult)
                        S.act(r=["kn32"], w=["kbuf"]).activation(out=kbuf[:, 128:128 + TTt], in_=kn32[:, 0:TTt], func=AF.Copy)
                yield
            if not sample:
                S.pool(r=["kbuf", "kbuf_prev"], w=["kcar"]).tensor_copy(out=kcar[:], in_=kbuf[:, TTt:TTt + 128])
                if tl["last"]:
                    S.dma("pk", r=["kn32"]).dma_start(out=D["p_kT"][l], in_=kn32[:, TTt - 128:TTt])
            else:
                for s in range(nseq):
                    S.dma("okc", group=True, w=["okc"]).dma_start(out=D["o_kT"][l, s, :, 0:124], in_=D["s_kT"][l, s, :, 4:128])
                    S.dma("okn", group=True, r=["kn32"], w=["okn"]).dma_start(out=D["o_kT"][l, s, :, 124:128], in_=kn32[:, s * LS:(s + 1) * LS])
                    S.dma("ovc", group=True, w=["ovc"]).dma_start(out=D["o_v"][l, s, 0:124, :], in_=D["s_v"][l, s, 4:128, :])
            yield
            for c in range(4):
                proj("z", c, Z0 + c * 128)
                yield
            for c in range(8):
                t = cv[c % 2]
                tk = ("cv", c % 2)
                t3 = t[:, 0:TTt].rearrange("p (s t) -> p s t", s=nseq)
                cw = lambda j, c=c: pp[:, P_CW + c * 4 + j:P_CW + c * 4 + j + 1]
                S.pool(r=[("pcb", c), "pp"], w=[tk]).tensor_scalar(out=t3, in0=pcv[:, c, :, 3:3 + L], scalar1=cw(3),
                                                                  scalar2=pp[:, P_CB + c:P_CB + c + 1], op0=ALU.mult, op1=ALU.add)
                for j in (2, 1, 0):
                    S.dve(r=[("pcb", c), "pcb_prev", "pp", tk], w=[tk]).scalar_tensor_tensor(
                        out=t3, in0=pcv[:, c, :, j:j + L], scalar=cw(j), in1=t3, op0=ALU.mult, op1=ALU.add)
                S.act(r=[tk], w=[("xbc", c)]).activation(out=xbc[:, c, 0:TTt], in_=t[:, 0:TTt], func=AF.Silu)
                yield
            allpcb = [("pcb", c) for c in range(8)]
            if sample:
                for c in range(8):
                    S.dma("oconv", r=allpcb + ["pcb_prev"], w=["oconv"], group=True).dma_start(out=D["o_conv"][l, :, c], in_=pcv[:, c, :, L:L + 3])
            else:
                S.pool(r=allpcb + ["pcb_prev"], w=["pcar"]).tensor_copy(out=pcar[:], in_=pcv[:, :, 0, L:L + 3])
                if tl["last"]:
                    S.dma("pconv", r=["pcar"]).dma_start(out=D["p_conv"][l], in_=pcar[:])
            yield

        def ssd_chunk(tl, s, j, g, Scur, skey):
            nseq, L, sample = tl["nseq"], tl["L"], tl["sample"]
            Lc = min(L, 128)
            nj = L // Lc
            c0 = s * L + j * Lc
            cs = slice(c0, c0 + Lc)
            for k in range(8):
                S.pe(r=["Win"] + hnk, w=[("PB", 2)]).matmul(small[0:Lc, 0:8], lhsT=hn[:, k, cs], rhs=Win[:, k, DT0:DT0 + 8],
                                                            start=(k == 0), stop=(k == 7))
            for gidx in range(2):
                S.pe(r=[("xbc", 4 + gidx), ("xbc", 6 + gidx)], w=[("PB", 2)]).matmul(
                    small[0:Lc, 160 + gidx * 128:160 + gidx * 128 + Lc], lhsT=xbc[:, 4 + gidx, cs], rhs=xbc[:, 6 + gidx, cs], start=True, stop=True)
            yield
            S.dve(r=[("PB", 2), "pp"], w=["dtb"]).tensor_tensor(out=dtb[0:Lc, :], in0=small[0:Lc, 0:8], in1=pp[0:Lc, P_DTB:P_DTB + 8], op=ALU.add)
            S.dve(r=[("PB", 2), "cf"], w=["cbm"]).tensor_tensor(out=cbm[0:Lc, :, 0:Lc],
                                                               in0=small[0:Lc, 160:416].rearrange("p (g q) -> p g q", g=2)[:, :, 0:Lc],
                                                               in1=TRI[0:Lc, 0:Lc].unsqueeze(1).to_broadcast([Lc, 2, Lc]), op=ALU.mult)
            yield
            S.act(r=["dtb"], w=["e1"]).activation(out=e1[0:Lc, :], in_=dtb[0:Lc, :], func=AF.Exp)
            S.act(r=["e1"], w=["dt"]).activation(out=dt[0:Lc, :], in_=e1[0:Lc, :], func=AF.Ln, bias=1.0)
            yield
            S.dve(r=["dt", "a_row"], w=["dA"]).tensor_tensor(out=dA[0:Lc, :], in0=dt[0:Lc, :], in1=a_row[0:Lc, :], op=ALU.mult)
            yield
            S.pe(r=["dA", "cf"], w=[("PB", 2)]).matmul(small[0:Lc, 144:152], lhsT=TRI[0:Lc, 0:Lc], rhs=dA[0:Lc, :], start=True, stop=True)
            S.pe(r=["dA", "cf"], w=[("PB", 2)]).matmul(small[:, 152:160], lhsT=ONESF[0:Lc, :], rhs=dA[0:Lc, :], start=True, stop=True)
            S.pool(r=["dA"], w=["dA_rep"]).tensor_copy(out=dA_rep[0:Lc, :, :], in_=dA[0:Lc, :].unsqueeze(2).to_broadcast([Lc, 8, 128]))
            yield
            S.dve(r=[("PB", 2)], w=["acum"]).tensor_copy(out=acum[0:Lc, :], in_=small[0:Lc, 144:152])
            S.dve(r=[("PB", 2), "acum"], w=["diff"]).tensor_tensor(out=diff[0:Lc, :], in0=small[0:Lc, 152:160], in1=acum[0:Lc, :], op=ALU.subtract)
            S.act(r=["diff"], w=["tail"]).activation(out=tail[0:Lc, :], in_=diff[0:Lc, :], func=AF.Exp)
            S.act(r=[("PB", 2)], w=["cd"]).activation(out=cd[:, :], in_=small[:, 152:160], func=AF.Exp)
            yield
            S.dve(r=["dt", "tail"], w=["dtt"]).tensor_tensor(out=dtt[0:Lc, :], in0=dt[0:Lc, :], in1=tail[0:Lc, :], op=ALU.mult)
            yield
            for hh in range(2):
                for r_ in range(4):
                    h = hh * 4 + r_
                    S.pe(r=["dA_rep", "cf", "acum", "diff", "cd", "tail", "dtt"], w=[K4]).matmul(
                        acb_ps[:, r_ * 128:r_ * 128 + Lc], lhsT=dA_rep[0:Lc, h, :], rhs=TRI[0:Lc, 0:Lc], start=True, stop=True)
                yield
                for r_ in range(4):
                    h = hh * 4 + r_
                    S.dve(r=[K4, "acum"], w=[("seg", hh)]).tensor_scalar(out=seg[0:Lc, h, 0:Lc], in0=acb_ps[0:Lc, r_ * 128:r_ * 128 + Lc],
                                                                         scalar1=acum[0:Lc, h:h + 1], scalar2=0.0, op0=ALU.subtract, op1=ALU.min)
                S.act(r=[K4], w=[("eb", hh)]).activation(out=eb[:, hh * 4:(hh + 1) * 4, 0:Lc],
                                                         in_=acb_ps[:, :].rearrange("p (r q) -> p r q", r=4)[:, :, 0:Lc], func=AF.Exp)
                yield
                S.act(r=[("seg", hh)], w=[("seg", hh)]).activation(out=seg[0:Lc, hh * 4:(hh + 1) * 4, 0:Lc], in_=seg[0:Lc, hh * 4:(hh + 1) * 4, 0:Lc], func=AF.Exp)
                S.pool(r=[("eb", hh), ("xbc", 6 + hh)], w=[("Cp", hh)]).tensor_tensor(
                    out=Cp[:, hh * 4:(hh + 1) * 4, 0:Lc], in0=eb[:, hh * 4:(hh + 1) * 4, 0:Lc],
                    in1=xbc[:, 6 + hh, cs].unsqueeze(1).to_broadcast([128, 4, Lc]), op=ALU.mult)
                yield
                S.dve(r=[("seg", hh), "cbm"], w=[("MT", hh)]).tensor_tensor(
                    out=MT[0:Lc, hh * 4:(hh + 1) * 4, 0:Lc], in0=seg[0:Lc, hh * 4:(hh + 1) * 4, 0:Lc],
                    in1=cbm[0:Lc, hh, 0:Lc].unsqueeze(1).to_broadcast([Lc, 4, Lc]), op=ALU.mult)
                yield
            for i in range(6):
                S.pe(r=[("xbc", i), "cb"], w=[K3]).transpose(tr_ps[0:Lc, i * 128:(i + 1) * 128], xbc[:, i, cs], IDB)
            yield
            S.act(r=[K3], w=["xtm"]).activation(out=xtm[0:Lc, :], in_=tr_ps[0:Lc, 0:768], func=AF.Copy)
            yield
            xtm3 = xtm[0:Lc, 0:512].rearrange("p (h d) -> p h d", h=8)
            S.pool(r=["xtm", "dt"], w=["xdt"]).tensor_tensor(out=xdt[0:Lc], in0=xtm3, in1=dt[0:Lc, :].unsqueeze(2).to_broadcast([Lc, 8, 64]), op=ALU.mult)
            S.pool(r=["xtm", "dtt"], w=["xst"]).tensor_tensor(out=xst[0:Lc], in0=xtm3, in1=dtt[0:Lc, :].unsqueeze(2).to_broadcast([Lc, 8, 64]), op=ALU.mult)
            yield
            for gidx in range(2):
                S.pe(r=["xtm", "xst"], w=[K3]).matmul(ns_ps[:, gidx * 256:(gidx + 1) * 256], lhsT=xtm[0:Lc, 512 + gidx * 128:512 + (gidx + 1) * 128],
                                                      rhs=xst[0:Lc, gidx * 4:(gidx + 1) * 4, :], start=True, stop=True)
            yield
            S.pool(r=[skey, "cd"], w=[skey]).tensor_tensor(out=Scur[:], in0=Scur[:], in1=cd[:, :].unsqueeze(2).to_broadcast([128, 8, 64]), op=ALU.mult)
            S.dve(r=[skey, K3], w=[skey]).tensor_tensor(out=Scur[:], in0=Scur[:], in1=ns_ps[:, :].rearrange("p (h d) -> p h d", h=8), op=ALU.add)
            yield
            for h in range(8):
                pr = (h % 2) * 64
                yo = y_ps[pr:pr + 64, (h // 2) * 128:(h // 2) * 128 + Lc]
                S.pe(r=["xdt", ("MT", h // 4)], w=[K3]).matmul(yo, lhsT=xdt[0:Lc, h, :], rhs=MT[0:Lc, h, 0:Lc], start=True, stop=False)
                S.pe(r=["Sbf", ("Cp", h // 4)], w=[K3]).matmul(yo, lhsT=Sbf[:, h, :], rhs=Cp[:, h, 0:Lc], start=False, stop=True)
                if h % 4 == 3:
                    yield
            for c in range(4):
                S.dve(r=[("xbc", c), "pp", K3], w=["ygt"]).scalar_tensor_tensor(
                    out=ygt[:, c, 0:Lc], in0=xbc[:, c, cs], scalar=pp[:, P_DSK + c:P_DSK + c + 1], in1=y_ps[:, c * 128:c * 128 + Lc],
                    op0=ALU.mult, op1=ALU.add)
            yield
            S.pool(r=["ygt"] + [("sz", c) for c in range(4)], w=[("yg", g)]).tensor_tensor(out=yg[:, :, cs], in0=ygt[:, :, 0:Lc], in1=sz[:, :, cs], op=ALU.mult)
            if sample:
                S.dma(("sst", s % 2), r=[skey]).dma_start(out=D["o_ssm"][l, s], in_=Scur[:].rearrange("p h d -> p (h d)"))
            else:
                S.act(r=[skey], w=["Sbf"]).activation(out=Sbf[:], in_=Scur[:], func=AF.Copy)
                if tl["last"] and j == nj - 1:
                    S.dma("pssm", r=[skey]).dma_start(out=D["p_ssm"][l], in_=Scur[:].rearrange("p h d -> p (h d)"))
            yield

        def att_chunk(tl, s, j, g):
            nseq, L, sample = tl["nseq"], tl["L"], tl["sample"]
            Lc = min(L, 128)
            nj = L // Lc
            c0 = s * L + j * Lc
            cs = slice(c0, c0 + Lc)
            has_prev = sample or not (tl["first"] and j == 0)
            for k in range(8):
                S.pe(r=["Win"] + hnk, w=[K7]).matmul(PB[7][0:Lc, 0:128], lhsT=hn[:, k, cs], rhs=Win[:, k, V0:V0 + 128],
                                                     start=(k == 0), stop=(k == 7))
            yield
            vown = vb[g % 3]
            vkey = ("vb", g % 3)
            S.act(r=[K7], w=[vkey]).activation(out=vown[0:Lc, :], in_=PB[7][0:Lc, 0:128], func=AF.Copy)
            if sample:
                S.dve(r=[K7], w=["v32"]).tensor_copy(out=v32[0:Lc, :], in_=PB[7][0:Lc, 0:128])
                S.dma("ovn", r=["v32"]).dma_start(out=D["o_v"][l, s, 124:128, :], in_=v32[0:Lc, :])
                kpre, kpkey = kcache[s % 2][:, :], ("kcache", s % 2)
                vprev, vpkey = vcache[s % 2], ("vcache", s % 2)
                kown = kbuf[:, 128 + c0:128 + c0 + Lc]
            else:
                if tl["last"] and j == nj - 1:
                    S.dve(r=[K7], w=["v32"]).tensor_copy(out=v32[0:Lc, :], in_=PB[7][0:Lc, 0:128])
                    S.dma("pv", r=["v32"]).dma_start(out=D["p_v"][l], in_=v32[:, :])
                kpre, kpkey = kbuf[:, j * 128:(j + 1) * 128], ("kbuf_prev" if j == 0 else "kbuf")
                kown = kbuf[:, 128 + j * 128:128 + (j + 1) * 128]
                vprev, vpkey = vb[(g - 1) % 3], ("vb", (g - 1) % 3)
            yield
            for hg in range(2):
                pr = hg * 64
                for rnd in range(2):
                    sl_ = rndc[0] % 2
                    rndc[0] += 1
                    PX, pxk = pexp[sl_], ("pexp", sl_)
                    PTt, ptk = PT[sl_], ("PT", sl_)
                    for ii in range(2):
                        i = 2 * rnd + ii
                        base = ii * 256
                        qh = qn[pr:pr + 64, i, cs]
                        if has_prev:
                            S.pe(r=[kpkey, ("qn", i)], w=[K5]).matmul(sc_ps[:, base:base + Lc], lhsT=kpre[pr:pr + 64, :], rhs=qh, start=True, stop=True)
                        S.pe(r=["kbuf", ("qn", i)], w=[K5]).matmul(sc_ps[0:Lc, base + 128:base + 128 + Lc], lhsT=kown[pr:pr + 64, :], rhs=qh,
                                                                   start=True, stop=True)
                    yield
                    scv = sc_ps[:, :].rearrange("p (i b q) -> p i b q", i=2, b=2)
                    h0 = 4 * hg + 2 * rnd
                    if has_prev:
                        S.act(r=[K5], w=[pxk]).activation(out=PX[:, :, 0, 0:Lc], in_=scv[:, :, 0, 0:Lc], func=AF.Exp)
                    S.act(r=[K5], w=[pxk]).activation(out=PX[0:Lc, :, 1, 0:Lc], in_=scv[0:Lc, :, 1, 0:Lc], func=AF.Exp)
                    yield
                    if has_prev:
                        S.dve(r=[pxk, "cf"], w=[ptk]).tensor_tensor(out=PTt[:, :, 0, 0:Lc], in0=PX[:, :, 0, 0:Lc],
                                                                    in1=ETAB[:, h0:h0 + 2, 0, 0:Lc], op=ALU.mult)
                    S.dve(r=[pxk, "cf"], w=[ptk]).tensor_tensor(out=PTt[0:Lc, :, 1, 0:Lc], in0=PX[0:Lc, :, 1, 0:Lc],
                                                                in1=ETAB[0:Lc, h0:h0 + 2, 1, 0:Lc], op=ALU.mult)
                    yield
                    for ii in range(2):
                        i = 2 * rnd + ii
                        oo = o_ps[pr:pr + 64, i * 128:i * 128 + Lc]
                        dd = den_ps[pr:pr + 64, i * 128:i * 128 + Lc]
                        if has_prev:
                            S.pe(r=[vpkey, ptk], w=[K6]).matmul(oo, lhsT=vprev[:, pr:pr + 64], rhs=PTt[:, ii, 0, 0:Lc], start=True, stop=False)
                        S.pe(r=[vkey, ptk], w=[K6]).matmul(oo, lhsT=vown[0:Lc, pr:pr + 64], rhs=PTt[0:Lc, ii, 1, 0:Lc], start=(not has_prev), stop=True)
                        if has_prev:
                            S.pe(r=["cb", ptk], w=[K7]).matmul(dd, lhsT=ONESB[:, 0:64], rhs=PTt[:, ii, 0, 0:Lc], start=True, stop=False)
                        S.pe(r=["cb", ptk], w=[K7]).matmul(dd, lhsT=ONESB[0:Lc, 0:64], rhs=PTt[0:Lc, ii, 1, 0:Lc], start=(not has_prev), stop=True)
                    yield
            S.dve(r=[K7, "esink"], w=["den"]).tensor_tensor(out=den[:, :, 0:Lc], in0=den_ps[:, :].rearrange("p (i q) -> p i q", i=4)[:, :, 0:Lc],
                                                            in1=esink[:, :].unsqueeze(2).to_broadcast([128, 4, Lc]), op=ALU.add)
            yield
            if os.environ.get("KRSQ"):
                S.dve(r=["den"], w=["rden"]).reciprocal(out=rden[:, :, 0:Lc], in_=den[:, :, 0:Lc])
            else:
                S.act(r=["den"], w=["lnd"]).activation(out=lnd[:, :, 0:Lc], in_=den[:, :, 0:Lc], func=AF.Ln)
                S.act(r=["lnd"], w=["rden"]).activation(out=rden[:, :, 0:Lc], in_=lnd[:, :, 0:Lc], func=AF.Exp, scale=-1.0)
            yield
            S.dve(r=[K6, "rden"], w=[("catt", g)]).tensor_tensor(out=catt[:, :, cs], in0=o_ps[:, :].rearrange("p (i q) -> p i q", i=4)[:, :, 0:Lc],
                                                                 in1=rden[:, :, 0:Lc], op=ALU.mult)
            yield

        def back(ti, gl):
            tl = tiles[ti]
            col0, nseq, L = tl["col0"], tl["nseq"], tl["L"]
            TTt = nseq * L
            slot = ti % 2
            X = xt[slot]
            xkey = ("xt", slot)
            ygk = [("yg", g) for g in gl]
            cak = [("catt", g) for g in gl]
            S.pool(r=ygk, w=["sq2"]).tensor_tensor(out=sq2[:, :, 0:TTt], in0=yg[:, :, 0:TTt], in1=yg[:, :, 0:TTt], op=ALU.mult)
            yield
            for gi in range(2):
                bi = 6 + gi
                pbt, pbk = PB[bi], ("PB", bi)
                for cc in range(2):
                    S.pe(r=["sq2", "cb"], w=[pbk]).matmul(pbt[:, 0:TTt], lhsT=ONESB, rhs=sq2[:, 2 * gi + cc, 0:TTt], start=(cc == 0), stop=(cc == 1))
                rms_rstd(S, eps_t, pbt[:, 0:TTt], pbk, lnb2[:, 0:TTt], "lnb2", rstd2[:, 0:TTt], "rstd2", 1.0 / 256)
                for cc in range(2):
                    c = 2 * gi + cc
                    S.dve(r=ygk + ["pp", "rstd2"], w=[("cssd", c)]).scalar_tensor_tensor(
                        out=cssd[:, c, 0:TTt], in0=yg[:, c, 0:TTt], scalar=pp[:, P_SNW + c:P_SNW + c + 1], in1=rstd2[:, 0:TTt],
                        op0=ALU.mult, op1=ALU.mult)
                yield
            for m in range(8):
                bi = 6 + m % 2
                pbt, pbk = PB[bi], ("PB", bi)
                for c in range(8):
                    src = cssd[:, c, 0:TTt] if c < 4 else catt[:, c - 4, 0:TTt]
                    rk = [("cssd", c)] if c < 4 else cak
                    S.pe(r=["Wout"] + rk, w=[pbk]).matmul(pbt[:, 0:TTt], lhsT=Wout[:, c, m * 128:(m + 1) * 128], rhs=src, start=(c == 0), stop=(c == 7))
                S.dve(r=[pbk, xkey], w=[xkey]).tensor_tensor(out=X[:, m, 0:TTt], in0=pbt[:, 0:TTt], in1=X[:, m, 0:TTt], op=ALU.add)
                yield
            S.dma(("xst", slot), r=[xkey]).dma_start(out=kp(xdst)[:, :, col0:col0 + TTt], in_=X[:, :, 0:TTt])
            yield

        interleave(front(0))
        for ti, tl in enumerate(tiles):
            nseq, L, sample = tl["nseq"], tl["L"], tl["sample"]
            Lc = min(L, 128)
            nj = L // Lc
            gl = []
            for s in range(nseq):
                if sample:
                    sslot = s % 2
                    Scur, skey = Sst[sslot], ("S", sslot)
                    S.dma(("sld", sslot), w=[skey]).dma_start(out=Scur[:].rearrange("p h d -> p (h d)"), in_=D["s_ssm"][l, s])
                    S.act(r=[skey], w=["Sbf"]).activation(out=Sbf[:], in_=Scur[:], func=AF.Copy)
                    S.dma(("kcl", sslot), w=[("kcache", sslot)], eng="pool").dma_start(out=kcache[sslot][:], in_=D["s_kT"][l, s])
                    S.dma(("vcl", sslot), w=[("vcache", sslot)], eng="pool").dma_start(out=vcache[sslot][:], in_=D["s_v"][l, s])
                else:
                    Scur, skey = Sst[0], ("S", 0)
                for j in range(nj):
                    g = gchunk[0]
                    gchunk[0] += 1
                    gl.append(g)
                    if os.environ.get("KLOG"):
                        print("KLOG chunk start", ti, s, j, "nrec", getattr(S, "nrec", 0))
                        interleave(ssd_chunk(tl, s, j, g, Scur, skey))
                        print("KLOG  after ssd nrec", getattr(S, "nrec", 0))
                        interleave(att_chunk(tl, s, j, g))
                        print("KLOG  after att nrec", getattr(S, "nrec", 0))
                    else:
                        interleave(ssd_chunk(tl, s, j, g, Scur, skey), att_chunk(tl, s, j, g))
            if os.environ.get("KLOG"):
                print("KLOG before back", ti, "nrec", getattr(S, "nrec", 0))
            interleave(back(ti, gl), front(ti + 1) if ti + 1 < len(tiles) else None)
            if os.environ.get("KLOG"):
                print("KLOG after back+front", ti, "nrec", getattr(S, "nrec", 0))
        S.analyze().emit(nc)


def ffn_stage(nc, l, tiles, xsrc, xdst, D):
    _UID[0] += 1
    with contextlib.ExitStack() as st:
        sb = lambda name, shape, dt: st.enter_context(nc.sbuf_tensor(_uname(name), shape, dt))
        S = Sched()
        C = load_common(nc, S, st, l, D)
        pp, cb = C["pp"], C["cb"]
        ONESB = cb[:, B_ONES:B_ONES + 128]
        eps_t = sb("eps_t", [128, 1], F32)
        S.dve(w=["eps"]).memset(eps_t[:], EPS)
        Wup = sb("Wup", [128, 8, 2 * D_FF], BF16)
        Wdn = sb("Wdn", [128, NPAIR, D_MODEL], BF16)
        for k in range(8):
            S.dma("wup", w=["Wup"], eng="pool", group=True).dma_start(out=Wup[:, k, :], in_=D["w_up"][l, k * 128:(k + 1) * 128, :])
        for j in range(NPAIR):
            S.dma("wdn", w=["Wdn"], eng="pool", group=True).dma_start(out=Wdn[:, j, :], in_=D["w_down"][l, j * 128:(j + 1) * 128, :])
        xt = [sb(f"xt{i}", [128, 8, TT], F32) for i in range(2)]
        sq = sb("sq", [128, 8, TT], BF16)
        hn = [sb(f"hn{i}", [128, 8, TT], BF16) for i in range(2)]
        lnb = sb("lnb", [128, TT], F32)
        rstd = sb("rstd", [128, TT], F32)
        btmp = [sb(f"btmp{i}", [128, NSQ, 2], F32) for i in range(2)]
        t0 = [sb(f"t0{i}", [128, TT], F32) for i in range(4)]
        sg = [sb(f"sg{i}", [128, TT], F32) for i in range(2)]
        gb = sb("gb", [128, NPAIR, TT], BF16)
        car = [sb(f"car{i}", [128, 44, NSQ, 2], F32) for i in range(2)]
        PB = [st.enter_context(nc.psum_tensor(_uname(f"pb{i}"), [128, 512], F32)) for i in range(8)]
        cark = [[("car", pr_, ch) for ch in range(44)] for pr_ in range(2)]
        S.dve(w=cark[0]).memset(car[0][:], 0.0)
        S.pool(w=cark[1]).memset(car[1][:], 0.0)
        rot = [0]
        uct = [0]

        def front(ti):
            tl = tiles[ti]
            col0, nseq, L = tl["col0"], tl["nseq"], tl["L"]
            TTt = nseq * L
            slot = ti % 2
            X, xkey = xt[slot], ("xt", slot)
            H = hn[slot]
            xap = X[:, :, 0:TTt]
            S.dma(("xld", slot), w=[xkey]).dma_start(out=X[:, :, 0:TTt], in_=kp(xsrc)[:, :, col0:col0 + TTt])
            yield
            S.pool(r=[xkey], w=["sq"]).tensor_tensor(out=sq[:, :, 0:TTt], in0=xap, in1=xap, op=ALU.mult)
            yield
            for k in range(8):
                S.pe(r=["sq", "cb"], w=[("PB", 0)]).matmul(PB[0][:, 0:TTt], lhsT=ONESB, rhs=sq[:, k, 0:TTt], start=(k == 0), stop=(k == 7))
            yield
            rms_rstd(S, eps_t, PB[0][:, 0:TTt], ("PB", 0), lnb[:, 0:TTt], "lnb", rstd[:, 0:TTt], "rstd", 1.0 / D_MODEL)
            yield
            for k in range(8):
                S.dve(r=[xkey, "rstd", "pp"], w=[("hn", slot, k)]).scalar_tensor_tensor(
                    out=H[:, k, 0:TTt], in0=xap[:, k, :], scalar=pp[:, P_N2 + k:P_N2 + k + 1], in1=rstd[:, 0:TTt], op0=ALU.mult, op1=ALU.mult)
                yield

        def up(ti):
            tl = tiles[ti]
            nseq, L, sample = tl["nseq"], tl["L"], tl["sample"]
            TTt = nseq * L
            slot = ti % 2
            H = hn[slot]
            hnk = [("hn", slot, k) for k in range(8)]
            rp, wp = (ti + 1) % 2, ti % 2
            CR, CW = car[rp], car[wp]
            if sample:
                S.dma("sffn", w=cark[rp]).dma_start(out=CR[:], in_=D["s_ffn"][l])
            for j in range(NPAIR):
                for half in range(2):
                    ch = j + half * NPAIR
                    bi = 1 + rot[0] % 6
                    rot[0] += 1
                    pbt, pbk = PB[bi], ("PB", bi)
                    us = uct[0] % 4
                    uct[0] += 1
                    T0, tkey = t0[us], ("t0", us)
                    t3 = T0[:, 0:TTt].rearrange("p (s t) -> p s t", s=nseq)
                    p3 = pbt[:, 0:TTt].rearrange("p (s t) -> p s t", s=nseq)
                    for k in range(8):
                        S.pe(r=["Wup"] + hnk, w=[pbk]).matmul(pbt[:, 0:TTt], lhsT=Wup[:, k, ch * 128:(ch + 1) * 128], rhs=H[:, k, 0:TTt],
                                                              start=(k == 0), stop=(k == 7))
                    fw = lambda jj, ch=ch: pp[:, P_FW + ch * 3 + jj:P_FW + ch * 3 + jj + 1]
                    S.act(r=[pbk, "pp"], w=[tkey]).activation(out=t3, in_=p3, func=AF.Identity, scale=fw(2), bias=pp[:, P_FB + ch:P_FB + ch + 1])
                    S.act(r=[pbk], w=[("car", wp, ch)]).activation(out=CW[:, ch, 0:nseq, :], in_=p3[:, :, L - 2:L], func=AF.Copy)
                    S.dve(r=[pbk, "pp", tkey], w=[tkey]).scalar_tensor_tensor(out=t3[:, :, 1:L], in0=p3[:, :, 0:L - 1], scalar=fw(1),
                                                                               in1=t3[:, :, 1:L], op0=ALU.mult, op1=ALU.add)
                    S.dve(r=[pbk, "pp", tkey], w=[tkey]).scalar_tensor_tensor(out=t3[:, :, 2:L], in0=p3[:, :, 0:L - 2], scalar=fw(0),
                                                                               in1=t3[:, :, 2:L], op0=ALU.mult, op1=ALU.add)
                    S.dve(r=[("car", rp, ch), "pp", tkey], w=[tkey]).scalar_tensor_tensor(
                        out=t3[:, :, 0:1], in0=CR[:, ch, 0:nseq, 1:2], scalar=fw(1), in1=t3[:, :, 0:1], op0=ALU.mult, op1=ALU.add)
                    S.dve(r=[("car", rp, ch), "pp", tkey], w=[tkey]).scalar_tensor_tensor(
                        out=t3[:, :, 0:2], in0=CR[:, ch, 0:nseq, 0:2], scalar=fw(0), in1=t3[:, :, 0:2], op0=ALU.mult, op1=ALU.add)
                    if half == 0:
                        SG, sgk = sg[j % 2], ("sg", j % 2)
                        S.act(r=[tkey], w=[sgk]).activation(out=SG[:, 0:TTt], in_=T0[:, 0:TTt], func=AF.Silu)
                    else:
                        S.pool(r=[sgk, tkey], w=[("gb", j)]).tensor_tensor(out=gb[:, j, 0:TTt], in0=SG[:, 0:TTt], in1=T0[:, 0:TTt], op=ALU.mult)
                    yield
            if sample:
                S.dma("offn", r=cark[wp]).dma_start(out=D["o_ffn"][l], in_=CW[:])
            elif tl["last"]:
                S.dma("pffn", r=cark[wp]).dma_start(out=D["p_ffn"][l], in_=CW[:, :, 0, :])
            yield

        def down(ti):
            tl = tiles[ti]
            col0, nseq, L = tl["col0"], tl["nseq"], tl["L"]
            TTt = nseq * L
            slot = ti % 2
            X, xkey = xt[slot], ("xt", slot)
            gbk = [("gb", j) for j in range(NPAIR)]
            for m in range(8):
                bi = 4 + m % 4
                pbt, pbk = PB[bi], ("PB", bi)
                for j in range(NPAIR):
                    S.pe(r=["Wdn"] + gbk, w=[pbk]).matmul(pbt[:, 0:TTt], lhsT=Wdn[:, j, m * 128:(m + 1) * 128], rhs=gb[:, j, 0:TTt],
                                                          start=(j == 0), stop=(j == NPAIR - 1))
                    if j % 6 == 5:
                        yield
                S.dve(r=[pbk, xkey], w=[xkey]).tensor_tensor(out=X[:, m, 0:TTt], in0=pbt[:, 0:TTt], in1=X[:, m, 0:TTt], op=ALU.add)
                yield
            S.dma(("xst", slot), r=[xkey]).dma_start(out=kp(xdst)[:, :, col0:col0 + TTt], in_=X[:, :, 0:TTt])
            yield

        interleave(front(0))
        for ti in range(len(tiles)):
            interleave(up(ti))
            interleave(down(ti), front(ti + 1) if ti + 1 < len(tiles) else None)
        S.analyze().emit(nc)


def _consts():
    s = np.arange(128)[:, None]
    q = np.arange(128)[None, :]
    tri = (q >= s).astype(np.float32)
    ones = np.ones((128, 128), np.float32)
    slopes = 2.0 ** (-8.0 * np.arange(1, 9) / 8)
    E = np.zeros((128, 8, 2, 128), np.float64)
    for h in range(8):
        rel_prev = (q + 128 - s).astype(np.float64)
        E[:, h, 0, :] = np.where(s > q, np.exp(-slopes[h] * rel_prev), 0.0)
        rel_own = (q - s).astype(np.float64)
        E[:, h, 1, :] = np.where(s <= q, np.exp(-slopes[h] * rel_own), 0.0)
    cstf = np.concatenate([tri, ones, E.reshape(128, 2048).astype(np.float32)], axis=1)
    ident = np.eye(128, dtype=np.float32)
    bd = np.zeros((128, 128), np.float32)
    bd[:64, :64] = 1.0
    bd[64:, 64:] = 1.0
    cstb = np.concatenate([ident, ones, bd], axis=1)
    return np.ascontiguousarray(cstf), np.ascontiguousarray(cstb)


def _pack_params(inp):
    pp = np.zeros((DEPTH, 128, NPP), np.float32)
    p = np.arange(128)
    for l in range(DEPTH):
        pp[l, :, P_N1:P_N1 + 8] = inp["norm1_w"][l].reshape(8, 128).T
        pp[l, :, P_N2:P_N2 + 8] = inp["norm2_w"][l].reshape(8, 128).T
        cw = inp["ssd_conv_w"][l].reshape(4, 8, 128)
        pp[l, :, P_CW:P_CW + 32] = cw.transpose(2, 1, 0).reshape(128, 32)
        pp[l, :, P_CB:P_CB + 8] = inp["ssd_conv_b"][l].reshape(8, 128).T
        pp[l, :, P_DTB:P_DTB + 8] = inp["dt_bias"][l][None, :]
        pp[l, :, P_ALOG:P_ALOG + 8] = inp["a_log"][l][None, :]
        pp[l, :, P_DSK:P_DSK + 4] = inp["d_skip"][l].reshape(4, 2)[:, (p >= 64).astype(int)].T
        pp[l, :, P_SNW:P_SNW + 4] = inp["ssd_norm_w"][l].reshape(4, 128).T
        pp[l, :, P_QN] = inp["q_norm_w"][l][p % 64]
        pp[l, :, P_KN] = inp["k_norm_w"][l][p % 64]
        sk = inp["attn_sinks"][l]
        pp[l, :, P_SINK:P_SINK + 4] = np.stack([np.where(p < 64, sk[c], sk[4 + c]) for c in range(4)], axis=1)
        fw = inp["ffn_conv_w"][l].reshape(3, 44, 128)
        pp[l, :, P_FW:P_FW + 132] = fw.transpose(2, 1, 0).reshape(128, 132)
        pp[l, :, P_FB:P_FB + 44] = inp["ffn_conv_b"][l].reshape(44, 128).T
    return pp


def _perm_weights(inp):
    w_in = inp["w_in"]
    qcols = []
    for c in range(4):
        qcols += list(range(1544 + c * 64, 1544 + (c + 1) * 64))
        qcols += list(range(1544 + (4 + c) * 64, 1544 + (5 + c) * 64))
    cols = list(range(0, 1536)) + qcols + list(range(2056, 2312)) + list(range(1536, 1544))
    w_in_p = np.ascontiguousarray(w_in[:, :, cols])
    rows = list(range(512))
    for c in range(4):
        rows += list(range(512 + c * 64, 512 + (c + 1) * 64))
        rows += list(range(512 + (4 + c) * 64, 512 + (5 + c) * 64))
    w_out_p = np.ascontiguousarray(inp["w_out"][:, rows, :])
    return w_in_p, w_out_p


_NC_CACHE = {}


def kernel(**inp):
    inp = {k: np.asarray(v) for k, v in inp.items()}
    xp = inp["x_prompt"]
    B, TP, _ = xp.shape
    xs = inp["x_sample"]
    n_stages = int(inp.pop("_n_stages", 4)) if "_n_stages" in inp else 4
    debug = bool(inp.pop("_debug", False)) if "_debug" in inp else False
    key = (TP, n_stages, debug)
    if key not in _NC_CACHE:
        _NC_CACHE[key] = build_nc(TP, n_stages, debug)
    nc = _NC_CACHE[key]
    cstf, cstb = _consts()
    pp = _pack_params(inp)
    w_in_p, w_out_p = _perm_weights(inp)
    w_up = np.ascontiguousarray(inp["w_up"])
    w_down = np.ascontiguousarray(inp["w_down"])
    in_maps = []
    for c in range(NCORES):
        b = c % B
        sl = slice(c * NSQ, (c + 1) * NSQ)
        xT = np.concatenate([xp[b].T, xs[sl].reshape(TS, D_MODEL).T], axis=1)
        m = {
            "xT": np.ascontiguousarray(xT, dtype=np.float32),
            "w_in": w_in_p, "w_out": w_out_p, "w_up": w_up, "w_down": w_down,
            "pp": pp, "cstf": cstf, "cstb": cstb,
            "s_ssm": np.ascontiguousarray(inp["state_ssm"][:, sl].transpose(0, 1, 4, 2, 3).reshape(DEPTH, NSQ, 128, 512)),
            "s_conv": np.ascontiguousarray(inp["state_ssd_conv"][:, sl].reshape(DEPTH, NSQ, 3, 8, 128).transpose(0, 4, 3, 1, 2)),
            "s_kT": np.ascontiguousarray(inp["cache_swa_k"][:, sl].reshape(DEPTH, NSQ, 128, 128).transpose(0, 1, 3, 2)),
            "s_v": np.ascontiguousarray(inp["cache_swa_v"][:, sl].reshape(DEPTH, NSQ, 128, 128)),
            "s_ffn": np.ascontiguousarray(inp["state_ffn_conv"][:, sl].reshape(DEPTH, NSQ, 2, 44, 128).transpose(0, 4, 3, 1, 2)),
        }
        in_maps.append(m)
    res = run_bass_kernel_spmd(nc, in_maps, core_ids=list(range(NCORES)))
    R = res.results
    f32 = np.float32
    y_prompt = np.stack([R[b]["yT"][:, :TP].T for b in range(B)]).astype(f32)
    y_sample = np.concatenate([R[c]["yT"][:, TP:].T.reshape(NSQ, LS, D_MODEL) for c in range(NCORES)]).astype(f32)
    p_ssm = np.stack([R[b]["p_ssm"].reshape(DEPTH, 128, 8, 64).transpose(0, 2, 3, 1) for b in range(B)], axis=1)
    p_conv = np.stack([R[b]["p_conv"].transpose(0, 3, 2, 1).reshape(DEPTH, 3, 1024) for b in range(B)], axis=1)
    p_k = np.stack([R[b]["p_kT"].transpose(0, 2, 1).reshape(DEPTH, 128, 2, 64) for b in range(B)], axis=1)
    p_v = np.stack([R[b]["p_v"].reshape(DEPTH, 128, 2, 64) for b in range(B)], axis=1)
    p_ffn = np.stack([R[b]["p_ffn"].transpose(0, 3, 2, 1).reshape(DEPTH, 2, 2 * D_FF) for b in range(B)], axis=1)
    o_ssm = np.concatenate([R[c]["o_ssm"].reshape(DEPTH, NSQ, 128, 8, 64).transpose(0, 1, 3, 4, 2) for c in range(NCORES)], axis=1)
    o_conv = np.concatenate([R[c]["o_conv"].transpose(0, 3, 4, 2, 1).reshape(DEPTH, NSQ, 3, 1024) for c in range(NCORES)], axis=1)
    o_k = np.concatenate([R[c]["o_kT"].transpose(0, 1, 3, 2).reshape(DEPTH, NSQ, 128, 2, 64) for c in range(NCORES)], axis=1)
    o_v = np.concatenate([R[c]["o_v"].reshape(DEPTH, NSQ, 128, 2, 64) for c in range(NCORES)], axis=1)
    o_ffn = np.concatenate([R[c]["o_ffn"].transpose(0, 3, 4, 2, 1).reshape(DEPTH, NSQ, 2, 2 * D_FF) for c in range(NCORES)], axis=1)
    outs = (y_prompt, y_sample, p_ssm, p_conv, p_k, p_v, p_ffn, o_ssm, o_conv, o_k, o_v, o_ffn)
    outs = tuple(np.ascontiguousarray(o, dtype=f32) for o in outs)
    if debug:
        kernel._dbg = R
    return outs
```

```python
import contextlib
import numpy as np
import concourse.bass as bass
import concourse.mybir as mybir
from concourse.bass_utils import run_bass_kernel_spmd

F32 = mybir.dt.float32
BF16 = mybir.dt.bfloat16
ALU = mybir.AluOpType
AF = mybir.ActivationFunctionType

D_MODEL = 1024
DEPTH = 2
NCORES = 8
NSQ = 16
LS = 4
TS = NSQ * LS
TT = 256
D_FF = 2816
NPAIR = 22
EPS = 1e-6
Z0, XBC0, Q0, K0, V0, DT0, IN_DIM = 0, 512, 1536, 2048, 2176, 2304, 2312
P_N1, P_N2, P_CW, P_CB, P_DTB, P_ALOG, P_DSK, P_SNW, P_QN, P_KN, P_SINK, P_FW, P_FB, NPP = \
    0, 8, 16, 48, 56, 64, 72, 76, 80, 81, 82, 86, 218, 262
C_TRI, C_ONES, C_E, NCF = 0, 128, 256, 256 + 2048
B_ID, B_ONES, B_BD, NCB = 0, 128, 256, 384

import os
POOL_ENG = os.environ.get("KPOOL", "pool")
MIXER_IMPL = os.environ.get("KMIXER", "r2")
ENGINES = ("pe", "act", "dve", "pool", "sp")
_UID = [0]


def _uname(name):
    return f"{name}_u{_UID[0]}"

SEM_CHUNK = 2000


class Op:
    __slots__ = ("eng", "fn", "reads", "writes", "dma_key", "group", "deps", "marked",
                 "sem_idx", "sem_val", "id")

    def __init__(self, eng, fn, reads, writes, dma_key, group):
        self.eng, self.fn, self.reads, self.writes = eng, fn, reads, writes
        self.dma_key, self.group = dma_key, group
        self.deps = set()
        self.marked = False
        self.sem_idx = self.sem_val = None


class _Rec:
    def __init__(self, sched, eng, r, w, key, group):
        self._a = (sched, eng, r, w, key, group)

    def __getattr__(self, name):
        sched, eng, r, w, key, group = self._a

        def rec(*args, **kwargs):
            return sched.op(eng, (name, args, kwargs), r, w, dma_key=key, group=group)
        return rec


class Sched:
    def __init__(self):
        self.ops = []

    def op(self, eng, fn, reads=(), writes=(), dma_key=None, group=False):
        self.nrec = getattr(self, "nrec", 0) + 1
        cut = int(os.environ.get("KCUT", "0"))
        if cut and self.nrec > cut:
            return None
        o = Op(eng, fn, tuple(reads), tuple(writes), dma_key, group)
        o.id = len(self.ops)
        self.ops.append(o)
        return o

    def pe(self, r=(), w=()): return _Rec(self, "pe", r, w, None, False)
    def act(self, r=(), w=()): return _Rec(self, "act", r, w, None, False)
    def dve(self, r=(), w=()): return _Rec(self, "dve", r, w, None, False)
    def pool(self, r=(), w=()): return _Rec(self, POOL_ENG, r, w, None, False)

    def dma(self, key, r=(), w=(), eng="sp", group=False):
        return _Rec(self, eng, r, w, key, group)

    def analyze(self):
        last_w, readers, last_dma = {}, {}, {}
        ops = self.ops
        for o in ops:
            deps = set()
            for r in o.reads:
                if r in last_w:
                    deps.add(last_w[r])
                if isinstance(r, tuple) and r and r[0] == "PB":
                    deps.update(d for d in readers.get(r, ()) if ops[d].eng != o.eng)
            for w in o.writes:
                if w in last_w:
                    po = ops[last_w[w]]
                    if not (o.group and po.group and o.dma_key is not None and po.dma_key == o.dma_key):
                        deps.add(po.id)
                deps.update(readers.get(w, ()))
            if o.dma_key is not None:
                if not o.group and o.dma_key in last_dma:
                    deps.add(last_dma[o.dma_key])
                last_dma[o.dma_key] = o.id
            deps.discard(o.id)
            if o.eng == "pe" and o.dma_key is None:
                deps = {d for d in deps if not (ops[d].eng == "pe" and ops[d].dma_key is None)}
            o.deps = deps
            for r in o.reads:
                readers.setdefault(r, set()).add(o.id)
            for w in o.writes:
                last_w[w] = o.id
                readers[w] = set()
        for o in ops:
            for d in o.deps:
                ops[d].marked = True
        cnt = {e: 0 for e in ENGINES}
        dma_cum = {}
        for o in ops:
            if o.dma_key is not None:
                dma_cum[o.dma_key] = dma_cum.get(o.dma_key, 0) + 16
                o.sem_idx = ("dma", o.dma_key)
                o.sem_val = dma_cum[o.dma_key]
            elif o.marked:
                k = cnt[o.eng]
                o.sem_idx = (o.eng, k // SEM_CHUNK)
                o.sem_val = (k % SEM_CHUNK) + 1
                cnt[o.eng] = k + 1
        self.dma_final = dma_cum
        self.n_eng_sems = {e: (cnt[e] + SEM_CHUNK - 1) // SEM_CHUNK for e in ENGINES}
        return self

    def emit(self, nc):
        ops = self.ops
        handles = {}
        for e in ENGINES:
            for i in range(self.n_eng_sems[e]):
                handles[(e, i)] = nc.alloc_semaphore(name=_uname(f"s_{e}_{i}"))
        for j, k in enumerate(self.dma_final.keys()):
            handles[("dma", k)] = nc.alloc_semaphore(name=_uname(f"d_{j}"))
        by_eng = {e: [o for o in ops if o.eng == e] for e in ENGINES}
        dma_final = self.dma_final

        def make(engname):
            def body(eng):
                waited = {}
                for o in by_eng[engname]:
                    need = {}
                    for d in o.deps:
                        po = ops[d]
                        if need.get(po.sem_idx, 0) < po.sem_val:
                            need[po.sem_idx] = po.sem_val
                    for key, v in need.items():
                        if waited.get(key, 0) < v:
                            eng.wait_ge(handles[key], v)
                            waited[key] = v
                    name, args, kwargs = o.fn
                    inst = getattr(eng, name)(*args, **kwargs)
                    if o.dma_key is not None:
                        inst.then_inc(handles[o.sem_idx], 16)
                    elif o.marked:
                        inst.then_inc(handles[o.sem_idx], 1)
                if engname == "sp":
                    for k, v in dma_final.items():
                        eng.wait_ge(handles[("dma", k)], v)
            return body

        with nc.Block() as block:
            block.tensor(make("pe"))
            block.scalar(make("act"))
            block.vector(make("dve"))
            block.gpsimd(make("pool"))
            block.sync(make("sp"))
        nc.clear_and_free_semaphores(list(handles.values()))
        nc.all_engine_barrier()


def tile_list(TP):
    tiles = []
    for i in range(TP // TT):
        tiles.append(dict(col0=i * TT, nseq=1, L=TT, sample=False, first=(i == 0),
                          last=(i == TP // TT - 1)))
    tiles.append(dict(col0=TP, nseq=NSQ, L=LS, sample=True, first=False, last=False))
    return tiles


def build_nc(TP, n_stages=4, debug=False):
    TTOT = TP + TS
    nc = bass.Bass("TRN2", target_bir_lowering=False)

    def din(name, shape):
        return nc.dram_tensor(name, list(shape), F32, kind="ExternalInput").ap()

    def dout(name, shape):
        return nc.dram_tensor(name, list(shape), F32, kind="ExternalOutput").ap()

    D = {}
    D["xT"] = din("xT", [D_MODEL, TTOT])
    D["w_in"] = din("w_in", [DEPTH, D_MODEL, IN_DIM])
    D["w_out"] = din("w_out", [DEPTH, D_MODEL, D_MODEL])
    D["w_up"] = din("w_up", [DEPTH, D_MODEL, 2 * D_FF])
    D["w_down"] = din("w_down", [DEPTH, D_FF, D_MODEL])
    D["pp"] = din("pp", [DEPTH, 128, NPP])
    D["cstf"] = din("cstf", [128, NCF])
    D["cstb"] = din("cstb", [128, NCB])
    D["s_ssm"] = din("s_ssm", [DEPTH, NSQ, 128, 512])
    D["s_conv"] = din("s_conv", [DEPTH, 128, 8, NSQ, 3])
    D["s_kT"] = din("s_kT", [DEPTH, NSQ, 128, 128])
    D["s_v"] = din("s_v", [DEPTH, NSQ, 128, 128])
    D["s_ffn"] = din("s_ffn", [DEPTH, 128, 44, NSQ, 2])
    D["yT"] = dout("yT", [D_MODEL, TTOT])
    D["p_ssm"] = dout("p_ssm", [DEPTH, 128, 512])
    D["p_conv"] = dout("p_conv", [DEPTH, 128, 8, 3])
    D["p_kT"] = dout("p_kT", [DEPTH, 128, 128])
    D["p_v"] = dout("p_v", [DEPTH, 128, 128])
    D["p_ffn"] = dout("p_ffn", [DEPTH, 128, 44, 2])
    D["o_ssm"] = dout("o_ssm", [DEPTH, NSQ, 128, 512])
    D["o_conv"] = dout("o_conv", [DEPTH, 128, 8, NSQ, 3])
    D["o_kT"] = dout("o_kT", [DEPTH, NSQ, 128, 128])
    D["o_v"] = dout("o_v", [DEPTH, NSQ, 128, 128])
    D["o_ffn"] = dout("o_ffn", [DEPTH, 128, 44, NSQ, 2])
    if debug:
        D["xmid"] = dout("xmid", [DEPTH, D_MODEL, TTOT])
        D["x1"] = dout("x1", [D_MODEL, TTOT])
    else:
        D["xmid"] = nc.dram_tensor("xmid", [DEPTH, D_MODEL, TTOT], F32, kind="Internal").ap()
        D["x1"] = nc.dram_tensor("x1", [D_MODEL, TTOT], F32, kind="Internal").ap()

    tiles = tile_list(TP)
    stage = 0
    for l in range(DEPTH):
        xsrc = D["xT"] if l == 0 else D["x1"]
        if stage < n_stages:
            (mixer_stage_r1 if MIXER_IMPL == "r1" else mixer_stage)(nc, l, tiles, xsrc, D["xmid"][l], D)
        stage += 1
        xdst = D["x1"] if l == 0 else D["yT"]
        if stage < n_stages:
            ffn_stage(nc, l, tiles, D["xmid"][l], xdst, D)
        stage += 1
    return nc


def kp(ap):
    return ap.rearrange("(k p) t -> p k t", p=128)


def load_common(nc, S, st, l, D, want_bf=True):
    sb = lambda name, shape, dt: st.enter_context(nc.sbuf_tensor(_uname(name), shape, dt))
    C = {}
    C["pp"] = sb("pp", [128, NPP], F32)
    C["cb"] = sb("cb", [128, NCB], BF16)
    S.dma("pp", w=["pp"]).dma_start(out=C["pp"][:], in_=D["pp"][l])
    S.dma("cb", w=["cb"], eng="pool").dma_start(out=C["cb"][:], in_=D["cstb"])
    return C


def rmsnorm_tile(nc, S, xt_ap, xkey, W, Cst, B, PB, pkey, ncol, hn, sq, std, rstd, TTt):
    ones = Cst["cb"][:, B_ONES:B_ONES + 128]
    S.act(r=[xkey], w=["sq"]).activation(out=sq[:, :, 0:TTt], in_=xt_ap, func=AF.Square)
    for k in range(8):
        S.pe(r=["sq", "cb"], w=[pkey]).matmul(PB[:, 0:TTt], lhsT=ones, rhs=sq[:, k, 0:TTt], start=(k == 0), stop=(k == 7))
    S.act(r=[pkey, "eps"], w=["std"]).activation(out=std[:, 0:TTt], in_=PB[:, 0:TTt], func=AF.Sqrt, bias=Cst["eps"][:, 0:1],
                                 scale=1.0 / D_MODEL)
    S.dve(r=["std"], w=["rstd"]).reciprocal(out=rstd[:, 0:TTt], in_=std[:, 0:TTt])
    for k in range(8):
        S.dve(r=[xkey, "rstd", "pp"], w=[("hn", k)]).scalar_tensor_tensor(out=hn[:, k, 0:TTt], in0=xt_ap[:, k, :],
                                                    scalar=Cst["pp"][:, ncol + k:ncol + k + 1], in1=rstd[:, 0:TTt],
                                                    op0=ALU.mult, op1=ALU.mult)


def mixer_stage_r1(nc, l, tiles, xsrc, xdst, D):
    _UID[0] += 1
    with contextlib.ExitStack() as st:
        sb = lambda name, shape, dt: st.enter_context(nc.sbuf_tensor(_uname(name), shape, dt))
        S = Sched()
        C = load_common(nc, S, st, l, D)
        pp, cb = C["pp"], C["cb"]
        cf = sb("cf", [128, NCF], F32)
        S.dma("cf", w=["cf"]).dma_start(out=cf[:], in_=D["cstf"])
        TRI = cf[:, C_TRI:C_TRI + 128]
        ONESF = cf[:, C_ONES:C_ONES + 128]
        ETAB = cf[:, C_E:C_E + 2048].rearrange("p (h b q) -> p h b q", h=8, b=2)
        IDB = cb[:, B_ID:B_ID + 128]
        ONESB = cb[:, B_ONES:B_ONES + 128]
        BDB = cb[:, B_BD:B_BD + 128]
        eps_t = sb("eps_t", [128, 1], F32)
        S.dve(w=["eps"]).memset(eps_t[:], EPS)
        C["eps"] = eps_t

        Win = sb("Win", [128, 8, IN_DIM], BF16)
        Wout = sb("Wout", [128, 8, D_MODEL], BF16)
        for k in range(8):
            S.dma("win", w=["Win"], eng="pool", group=True).dma_start(out=Win[:, k, :], in_=D["w_in"][l, k * 128:(k + 1) * 128, :])
        for k in range(8):
            S.dma("wout", w=["Wout"], eng="pool", group=True).dma_start(out=Wout[:, k, :], in_=D["w_out"][l, k * 128:(k + 1) * 128, :])

        a_row = sb("a_row", [128, 8], F32)
        esink = sb("esink", [128, 4], F32)
        wq8 = sb("wq8", [128, 1], F32)
        S.act(r=["pp"], w=["a_row"]).activation(out=a_row[:], in_=pp[:, P_ALOG:P_ALOG + 8], func=AF.Exp)
        S.dve(r=["a_row"], w=["a_row"]).tensor_scalar(out=a_row[:], in0=a_row[:], scalar1=-1.0, scalar2=None, op0=ALU.mult)
        S.act(r=["pp"], w=["esink"]).activation(out=esink[:], in_=pp[:, P_SINK:P_SINK + 4], func=AF.Exp)
        S.dve(r=["pp"], w=["wq8"]).tensor_scalar(out=wq8[:], in0=pp[:, P_QN:P_QN + 1], scalar1=0.125, scalar2=None,
                                        op0=ALU.mult)

        xt = [sb(f"xt{i}", [128, 8, TT], F32) for i in range(2)]
        sq = sb("sq", [128, 8, TT], BF16)
        hn = sb("hn", [128, 8, TT], BF16)
        std = sb("std", [128, TT], F32)
        rstd = sb("rstd", [128, TT], F32)
        sz = sb("sz", [128, 4, TT], F32)
        pcb = sb("pcb", [128, 8, TT + 4], F32)
        pcar = sb("pcar", [128, 8, 3], F32)
        cv = [sb(f"cv{i}", [128, TT], F32) for i in range(2)]
        xbc = sb("xbc", [128, 8, TT], BF16)
        qk32 = sb("qk32", [128, 5, TT], F32)
        qn = sb("qn", [128, 4, TT], BF16)
        kn32 = sb("kn32", [128, TT], F32)
        kbuf = sb("kbuf", [128, 128 + TT], BF16)
        kcar = sb("kcar", [128, 128], BF16)
        kcache = [sb(f"kcache{i}", [128, 128], BF16) for i in range(2)]
        vcache = [sb(f"vcache{i}", [128, 128], BF16) for i in range(2)]
        yg = sb("yg", [128, 4, TT], F32)
        cssd = sb("cssd", [128, 4, TT], BF16)
        catt = sb("catt", [128, 4, TT], BF16)
        dtb = sb("dtb", [128, 8], F32)
        e1 = sb("e1", [128, 8], F32)
        dt = sb("dt", [128, 8], F32)
        dA = sb("dA", [128, 8], F32)
        acum = sb("acum", [128, 8], F32)
        diff = sb("diff", [128, 8], F32)
        tail = sb("tail", [128, 8], F32)
        cd = sb("cd", [128, 8], F32)
        dtt = sb("dtt", [128, 8], F32)
        dA_rep = sb("dA_rep", [128, 8, 128], F32)
        xtm = sb("xtm", [128, 768], BF16)
        xdt = sb("xdt", [128, 8, 64], BF16)
        xst = sb("xst", [128, 8, 64], BF16)
        cbm = sb("cbm", [128, 2, 128], F32)
        seg = sb("seg", [128, 8, 128], F32)
        MT = sb("MT", [128, 8, 128], BF16)
        eb = sb("eb", [128, 8, 128], F32)
        Cp = sb("Cp", [128, 8, 128], BF16)
        ygt = sb("ygt", [128, 4, 128], F32)
        Sst = [sb(f"Sst{i}", [128, 8, 64], F32) for i in range(2)]
        Sbf = sb("Sbf", [128, 8, 64], BF16)
        vb = [sb(f"vb{i}", [128, 128], BF16) for i in range(3)]
        v32 = sb("v32", [128, 128], F32)
        pexp = sb("pexp", [128, 4, 2, 128], F32)
        PT = sb("PT", [128, 4, 2, 128], BF16)
        den = sb("den", [128, 4, 128], F32)
        rden = sb("rden", [128, 4, 128], F32)

        PB = [st.enter_context(nc.psum_tensor(_uname(f"pb{i}"), [128, 512], F32)) for i in range(8)]
        small = PB[2]
        tr_ps = PB[3][:].bitcast(BF16)
        ns_ps = PB[3]
        acb_ps = [PB[4], PB[5]]
        sc_ps = [PB[4], PB[5]]
        y_ps = PB[6]
        o_ps = PB[6]
        den_ps = PB[7]
        pbrot = [0]

        def next_pb():
            i = pbrot[0] % 2
            pbrot[0] += 1
            return PB[i], ("PB", i)

        S.dve(w=["pcar"]).memset(pcar[:], 0.0)
        S.dve(w=[("S", 0)]).memset(Sst[0][:], 0.0)
        S.pool(w=["Sbf"]).memset(Sbf[:], 0.0)
        S.pool(w=["kcar"]).memset(kcar[:], 0.0)

        gchunk = [0]

        for ti, tl in enumerate(tiles):
            col0, nseq, L, sample = tl["col0"], tl["nseq"], tl["L"], tl["sample"]
            TTt = nseq * L
            Lc = min(L, 128)
            nj = L // Lc
            slot = ti % 2
            X = xt[slot]
            xkey = ("xt", slot)
            xap = X[:, :, 0:TTt]
            S.dma(("xld", slot), w=[xkey]).dma_start(out=X[:, :, 0:TTt], in_=kp(xsrc)[:, :, col0:col0 + TTt])
            pbt, pbk = next_pb()
            rmsnorm_tile(nc, S, xap, xkey, None, C, None, pbt, pbk, P_N1, hn, sq, std, rstd, TTt)
            hnk = [("hn", k) for k in range(8)]
            pcv = pcb[:, :, 0:nseq * (3 + L)].rearrange("p c (s t) -> p c s t", s=nseq)
            if sample:
                for c in range(8):
                    S.dma("sconv", r=[], w=["pcb_prev"], group=True).dma_start(out=pcv[:, c, :, 0:3], in_=D["s_conv"][l, :, c])
            else:
                S.pool(r=["pcar"], w=["pcb_prev"]).tensor_copy(out=pcv[:, :, 0, 0:3], in_=pcar[:])
            chunks = [("z", c, Z0 + c * 128) for c in range(4)] + [("x", c, XBC0 + c * 128) for c in range(8)] + \
                     [("q", c, Q0 + c * 128) for c in range(4)] + [("q", 4, K0)]
            for kind, c, wc in chunks:
                pbt, pbk = next_pb()
                for k in range(8):
                    S.pe(r=["Win"] + hnk, w=[pbk]).matmul(pbt[:, 0:TTt], lhsT=Win[:, k, wc:wc + 128],
                                                                 rhs=hn[:, k, 0:TTt], start=(k == 0), stop=(k == 7))
                if kind == "z":
                    S.act(r=[pbk], w=[("sz", c)]).activation(out=sz[:, c, 0:TTt], in_=pbt[:, 0:TTt], func=AF.Silu)
                elif kind == "x":
                    S.act(r=[pbk], w=[("pcb", c)]).activation(
                        out=pcv[:, c, :, 3:3 + L], in_=pbt[:, 0:TTt].rearrange("p (s t) -> p s t", s=nseq), func=AF.Copy)
                else:
                    S.dve(r=[pbk], w=[("qk32", c)]).tensor_copy(out=qk32[:, c, 0:TTt], in_=pbt[:, 0:TTt])
            for c in range(8):
                t = cv[c % 2]
                tk = ("cv", c % 2)
                t3 = t[:, 0:TTt].rearrange("p (s t) -> p s t", s=nseq)
                cw = lambda j, c=c: pp[:, P_CW + c * 4 + j:P_CW + c * 4 + j + 1]
                S.dve(r=[("pcb", c), "pp"], w=[tk]).tensor_scalar(
                    out=t3, in0=pcv[:, c, :, 3:3 + L], scalar1=cw(3), scalar2=pp[:, P_CB + c:P_CB + c + 1],
                    op0=ALU.mult, op1=ALU.add)
                for j in (2, 1, 0):
                    S.dve(r=[("pcb", c), "pcb_prev", "pp", tk], w=[tk]).scalar_tensor_tensor(
                        out=t3, in0=pcv[:, c, :, j:j + L], scalar=cw(j), in1=t3, op0=ALU.mult, op1=ALU.add)
                S.act(r=[tk], w=[("xbc", c)]).activation(out=xbc[:, c, 0:TTt], in_=t[:, 0:TTt], func=AF.Silu)
            allpcb = [("pcb", c) for c in range(8)]
            if sample:
                for c in range(8):
                    S.dma("oconv", r=allpcb + ["pcb_prev"], w=["oconv"], group=True).dma_start(out=D["o_conv"][l, :, c], in_=pcv[:, c, :, L:L + 3])
            else:
                S.pool(r=allpcb + ["pcb_prev"],
                       w=["pcar"]).tensor_copy(out=pcar[:], in_=pcv[:, :, 0, L:L + 3])
                if tl["last"]:
                    S.dma("pconv", r=["pcar"]).dma_start(out=D["p_conv"][l], in_=pcar[:])
            if not sample:
                S.pool(r=["kcar"], w=["kbuf_prev"]).tensor_copy(out=kbuf[:, 0:128], in_=kcar[:])
            for c in range(5):
                pbt, pbk = next_pb()
                S.act(r=[("qk32", c)], w=["sq"]).activation(out=sq[:, 0, 0:TTt], in_=qk32[:, c, 0:TTt], func=AF.Square)
                S.pe(r=["sq", "cb"], w=[pbk]).matmul(pbt[:, 0:TTt], lhsT=BDB, rhs=sq[:, 0, 0:TTt], start=True, stop=True)
                S.act(r=[pbk, "eps"], w=["std"]).activation(out=std[:, 0:TTt], in_=pbt[:, 0:TTt], func=AF.Sqrt,
                                                      bias=eps_t[:, 0:1], scale=1.0 / 64)
                S.dve(r=["std"], w=["rstd"]).reciprocal(out=rstd[:, 0:TTt], in_=std[:, 0:TTt])
                if c < 4:
                    S.dve(r=[("qk32", c), "wq8", "rstd"], w=[("qn", c)]).scalar_tensor_tensor(out=qn[:, c, 0:TTt], in0=qk32[:, c, 0:TTt], scalar=wq8[:, 0:1],
                                                                in1=rstd[:, 0:TTt], op0=ALU.mult, op1=ALU.mult)
                else:
                    S.dve(r=[("qk32", 4), "pp", "rstd"], w=["kn32"]).scalar_tensor_tensor(out=kn32[:, 0:TTt], in0=qk32[:, 4, 0:TTt],
                                                           scalar=pp[:, P_KN:P_KN + 1], in1=rstd[:, 0:TTt],
                                                           op0=ALU.mult, op1=ALU.mult)
                    S.act(r=["kn32"], w=["kbuf"]).activation(out=kbuf[:, 128:128 + TTt], in_=kn32[:, 0:TTt], func=AF.Copy)
            if not sample:
                S.pool(r=["kbuf", "kbuf_prev"], w=["kcar"]).tensor_copy(out=kcar[:], in_=kbuf[:, TTt:TTt + 128])
                if tl["last"]:
                    S.dma("pk", r=["kn32"]).dma_start(out=D["p_kT"][l], in_=kn32[:, TTt - 128:TTt])
            else:
                for s in range(nseq):
                    S.dma("okc", group=True, w=["okc"]).dma_start(out=D["o_kT"][l, s, :, 0:124], in_=D["s_kT"][l, s, :, 4:128])
                    S.dma("okn", group=True, r=["kn32"], w=["okn"]).dma_start(out=D["o_kT"][l, s, :, 124:128], in_=kn32[:, s * LS:(s + 1) * LS])
                    S.dma("ovc", group=True, w=["ovc"]).dma_start(out=D["o_v"][l, s, 0:124, :], in_=D["s_v"][l, s, 4:128, :])

            for s in range(nseq):
                if sample:
                    sslot = s % 2
                    Scur, skey = Sst[sslot], ("S", sslot)
                    S.dma(("sld", sslot), w=[skey]).dma_start(out=Scur[:].rearrange("p h d -> p (h d)"),
                                                                in_=D["s_ssm"][l, s])
                    S.act(r=[skey], w=["Sbf"]).activation(out=Sbf[:], in_=Scur[:], func=AF.Copy)
                    kc, vc = kcache[sslot], vcache[sslot]
                    S.dma(("kcl", sslot),
                          w=[("kcache", sslot)], eng="pool").dma_start(out=kc[:], in_=D["s_kT"][l, s])
                    S.dma(("vcl", sslot),
                          w=[("vcache", sslot)], eng="pool").dma_start(out=vc[:], in_=D["s_v"][l, s])
                else:
                    Scur, skey = Sst[0], ("S", 0)
                for j in range(nj):
                    g = gchunk[0]
                    gchunk[0] += 1
                    c0 = s * L + j * Lc
                    cs = slice(c0, c0 + Lc)
                    has_prev = sample or not (tl["first"] and j == 0)
                    for k in range(8):
                        S.pe(r=["Win"] + hnk, w=[("PB", 2)]).matmul(small[0:Lc, 0:8], lhsT=hn[:, k, cs], rhs=Win[:, k, DT0:DT0 + 8],
                                                            start=(k == 0), stop=(k == 7))
                    for k in range(8):
                        S.pe(r=["Win"] + hnk, w=[("PB", 7)]).matmul(PB[7][0:Lc, 0:128], lhsT=hn[:, k, cs], rhs=Win[:, k, V0:V0 + 128],
                                                            start=(k == 0), stop=(k == 7))
                    S.dve(r=[("PB", 2), "pp"], w=["dtb"]).tensor_tensor(out=dtb[0:Lc, :], in0=small[0:Lc, 0:8], in1=pp[0:Lc, P_DTB:P_DTB + 8],
                                                    op=ALU.add)
                    S.act(r=["dtb"], w=["e1"]).activation(out=e1[0:Lc, :], in_=dtb[0:Lc, :], func=AF.Exp)
                    S.act(r=["e1"], w=["dt"]).activation(out=dt[0:Lc, :], in_=e1[0:Lc, :], func=AF.Ln, bias=1.0)
                    S.dve(r=["dt", "a_row"], w=["dA"]).tensor_tensor(out=dA[0:Lc, :], in0=dt[0:Lc, :], in1=a_row[0:Lc, :], op=ALU.mult)
                    S.pe(r=["dA", "cf"], w=[("PB", 2)]).matmul(small[0:Lc, 144:152], lhsT=TRI[0:Lc, 0:Lc], rhs=dA[0:Lc, :], start=True, stop=True)
                    S.pe(r=["dA", "cf"], w=[("PB", 2)]).matmul(small[:, 152:160], lhsT=ONESF[0:Lc, :], rhs=dA[0:Lc, :], start=True, stop=True)
                    S.dve(r=[("PB", 2)], w=["acum"]).tensor_copy(out=acum[0:Lc, :], in_=small[0:Lc, 144:152])
                    S.dve(r=[("PB", 2), "acum"], w=["diff"]).tensor_tensor(out=diff[0:Lc, :], in0=small[0:Lc, 152:160], in1=acum[0:Lc, :],
                                                    op=ALU.subtract)
                    S.act(r=["diff"], w=["tail"]).activation(out=tail[0:Lc, :], in_=diff[0:Lc, :], func=AF.Exp)
                    S.act(r=[("PB", 2)], w=["cd"]).activation(out=cd[:, :], in_=small[:, 152:160], func=AF.Exp)
                    S.dve(r=["dt", "tail"], w=["dtt"]).tensor_tensor(out=dtt[0:Lc, :], in0=dt[0:Lc, :], in1=tail[0:Lc, :], op=ALU.mult)
                    S.dve(r=["dA"], w=["dA_rep"]).tensor_copy(out=dA_rep[0:Lc, :, :], in_=dA[0:Lc, :].unsqueeze(2).to_broadcast([Lc, 8, 128]))
                    for h in range(8):
                        S.pe(r=["dA_rep", "cf"], w=[("PB", 4 + h // 4)]).matmul(acb_ps[h // 4][:, (h % 4) * 128:(h % 4) * 128 + Lc],
                                                     lhsT=dA_rep[0:Lc, h, :], rhs=TRI[0:Lc, 0:Lc], start=True, stop=True)
                    for i in range(6):
                        S.pe(r=[("xbc", i), "cb"], w=[("PB", 3)]).transpose(tr_ps[0:Lc, i * 128:(i + 1) * 128], xbc[:, i, cs], IDB)
                    S.act(r=[("PB", 3)], w=["xtm"]).activation(out=xtm[0:Lc, :], in_=tr_ps[0:Lc, 0:768], func=AF.Copy)
                    xtm3 = xtm[0:Lc, 0:512].rearrange("p (h d) -> p h d", h=8)
                    S.dve(r=["xtm", "dt"], w=["xdt"]).tensor_tensor(out=xdt[0:Lc], in0=xtm3,
                                                               in1=dt[0:Lc, :].unsqueeze(2).to_broadcast([Lc, 8, 64]), op=ALU.mult)
                    S.dve(r=["xtm", "dtt"], w=["xst"]).tensor_tensor(out=xst[0:Lc], in0=xtm3,
                                                               in1=dtt[0:Lc, :].unsqueeze(2).to_broadcast([Lc, 8, 64]), op=ALU.mult)
                    for gidx in range(2):
                        S.pe(r=[("xbc", 4 + gidx), ("xbc", 6 + gidx)], w=[("PB", 2)]).matmul(small[0:Lc, 160 + gidx * 128:160 + gidx * 128 + Lc],
                                                                  lhsT=xbc[:, 4 + gidx, cs], rhs=xbc[:, 6 + gidx, cs],
                                                                  start=True, stop=True)
                    S.dve(r=[("PB", 2), "cf"], w=["cbm"]).tensor_tensor(out=cbm[0:Lc, :, 0:Lc],
                                                    in0=small[0:Lc, 160:416].rearrange("p (g q) -> p g q", g=2)[:, :, 0:Lc],
                                                    in1=TRI[0:Lc, 0:Lc].unsqueeze(1).to_broadcast([Lc, 2, Lc]), op=ALU.mult)
                    for h in range(8):
                        S.dve(r=[("PB", 4 + h // 4), "acum"], w=["seg"]).tensor_scalar(out=seg[0:Lc, h, 0:Lc],
                                                             in0=acb_ps[h // 4][0:Lc, (h % 4) * 128:(h % 4) * 128 + Lc],
                                                             scalar1=acum[0:Lc, h:h + 1], scalar2=0.0,
                                                             op0=ALU.subtract, op1=ALU.min)
                    S.act(r=["seg"], w=["seg"]).activation(out=seg[0:Lc, :, 0:Lc], in_=seg[0:Lc, :, 0:Lc], func=AF.Exp)
                    S.dve(r=["seg", "cbm"], w=["MT"]).tensor_tensor(out=MT[0:Lc, :, 0:Lc].rearrange("p (g r) q -> p g r q", g=2),
                                                    in0=seg[0:Lc, :, 0:Lc].rearrange("p (g r) q -> p g r q", g=2),
                                                    in1=cbm[0:Lc, :, 0:Lc].unsqueeze(2).to_broadcast([Lc, 2, 4, Lc]), op=ALU.mult)
                    for hh in range(2):
                        S.act(r=[("PB", 4 + hh)], w=[("eb", hh)]).activation(
                            out=eb[:, hh * 4:(hh + 1) * 4, 0:Lc],
                            in_=acb_ps[hh][:, :].rearrange("p (r q) -> p r q", r=4)[:, :, 0:Lc], func=AF.Exp)
                        S.dve(r=[("eb", hh), ("xbc", 6 + hh)], w=[("Cp", hh)]).tensor_tensor(
                            out=Cp[:, hh * 4:(hh + 1) * 4, 0:Lc], in0=eb[:, hh * 4:(hh + 1) * 4, 0:Lc],
                            in1=xbc[:, 6 + hh, cs].unsqueeze(1).to_broadcast([128, 4, Lc]), op=ALU.mult)
                    for h in range(8):
                        pr = (h % 2) * 64
                        yo = y_ps[pr:pr + 64, (h // 2) * 128:(h // 2) * 128 + Lc]
                        S.pe(r=["xdt", "MT"], w=[("PB", 6)]).matmul(yo, lhsT=xdt[0:Lc, h, :], rhs=MT[0:Lc, h, 0:Lc], start=True, stop=False)
                        S.pe(r=["Sbf", ("Cp", h // 4)], w=[("PB", 6)]).matmul(yo, lhsT=Sbf[:, h, :], rhs=Cp[:, h, 0:Lc], start=False, stop=True)
                    for c in range(4):
                        S.dve(r=[("xbc", c), "pp", ("PB", 6)], w=["ygt"]).scalar_tensor_tensor(
                            out=ygt[:, c, 0:Lc], in0=xbc[:, c, cs], scalar=pp[:, P_DSK + c:P_DSK + c + 1],
                            in1=y_ps[:, c * 128:c * 128 + Lc], op0=ALU.mult, op1=ALU.add)
                    S.dve(r=["ygt"] + [("sz", c) for c in range(4)], w=[("yg", g)]).tensor_tensor(out=yg[:, :, cs], in0=ygt[:, :, 0:Lc], in1=sz[:, :, cs], op=ALU.mult)
                    for gidx in range(2):
                        S.pe(r=["xtm", "xst"], w=[("PB", 3)]).matmul(ns_ps[:, gidx * 256:(gidx + 1) * 256],
                                                           lhsT=xtm[0:Lc, 512 + gidx * 128:512 + (gidx + 1) * 128],
                                                           rhs=xst[0:Lc, gidx * 4:(gidx + 1) * 4, :], start=True, stop=True)
                    S.dve(r=[skey, "cd"], w=[skey]).tensor_tensor(out=Scur[:], in0=Scur[:],
                                                               in1=cd[:, :].unsqueeze(2).to_broadcast([128, 8, 64]), op=ALU.mult)
                    S.dve(r=[skey, ("PB", 3)], w=[skey]).tensor_tensor(out=Scur[:], in0=Scur[:],
                                                               in1=ns_ps[:, :].rearrange("p (h d) -> p h d", h=8), op=ALU.add)
                    if sample:
                        S.dma(("sst", s % 2), r=[skey]).dma_start(out=D["o_ssm"][l, s], in_=Scur[:].rearrange("p h d -> p (h d)"))
                    else:
                        S.act(r=[skey], w=["Sbf"]).activation(out=Sbf[:], in_=Scur[:], func=AF.Copy)
                        if tl["last"] and j == nj - 1:
                            S.dma("pssm", r=[skey]).dma_start(out=D["p_ssm"][l], in_=Scur[:].rearrange("p h d -> p (h d)"))

                    vown = vb[g % 3]
                    vkey = ("vb", g % 3)
                    S.act(r=[("PB", 7)], w=[vkey]).activation(out=vown[0:Lc, :], in_=PB[7][0:Lc, 0:128], func=AF.Copy)
                    if sample:
                        S.dve(r=[("PB", 7)], w=["v32"]).tensor_copy(out=v32[0:Lc, :], in_=PB[7][0:Lc, 0:128])
                        S.dma("ovn", r=["v32"]).dma_start(out=D["o_v"][l, s, 124:128, :], in_=v32[0:Lc, :])
                        kprev, kpkey = kcache[s % 2], ("kcache", s % 2)
                        vprev, vpkey = vcache[s % 2], ("vcache", s % 2)
                        kown = kbuf[:, 128 + c0:128 + c0 + Lc]
                        kpre = kprev[:, :]
                    else:
                        if tl["last"] and j == nj - 1:
                            S.dve(r=[("PB", 7)], w=["v32"]).tensor_copy(out=v32[0:Lc, :], in_=PB[7][0:Lc, 0:128])
                            S.dma("pv", r=["v32"]).dma_start(out=D["p_v"][l], in_=v32[:, :])
                        kpre, kpkey = kbuf[:, j * 128:(j + 1) * 128], "kbuf_prev" if j == 0 else "kbuf"
                        kown = kbuf[:, 128 + j * 128:128 + (j + 1) * 128]
                        vprev, vpkey = vb[(g - 1) % 3], ("vb", (g - 1) % 3)
                    for hg in range(2):
                        pr = hg * 64
                        for i in range(4):
                            bank = sc_ps[i // 2]
                            bk = ("PB", 4 + i // 2)
                            base = (i % 2) * 256
                            qh = qn[pr:pr + 64, i, cs]
                            if has_prev:
                                S.pe(r=[kpkey, ("qn", i)], w=[bk]).matmul(
                                    bank[:, base:base + Lc], lhsT=kpre[pr:pr + 64, :], rhs=qh, start=True, stop=True)
                            S.pe(r=["kbuf", ("qn", i)], w=[bk]).matmul(
                                bank[0:Lc, base + 128:base + 128 + Lc], lhsT=kown[pr:pr + 64, :], rhs=qh, start=True, stop=True)
                        for bi in range(2):
                            scv = sc_ps[bi][:, :].rearrange("p (i b q) -> p i b q", i=2, b=2)
                            if has_prev:
                                S.act(r=[("PB", 4 + bi)], w=[("pexp", bi)]).activation(out=pexp[:, 2 * bi:2 * bi + 2, 0, 0:Lc],
                                                                             in_=scv[:, :, 0, 0:Lc], func=AF.Exp)
                            S.act(r=[("PB", 4 + bi)], w=[("pexp", bi)]).activation(out=pexp[0:Lc, 2 * bi:2 * bi + 2, 1, 0:Lc],
                                                                         in_=scv[0:Lc, :, 1, 0:Lc], func=AF.Exp)
                        pkeys = [("pexp", 0), ("pexp", 1)]
                        if has_prev:
                            S.dve(r=pkeys + ["cf"], w=["PT"]).tensor_tensor(out=PT[:, :, 0, 0:Lc], in0=pexp[:, :, 0, 0:Lc],
                                                                   in1=ETAB[:, 4 * hg:4 * hg + 4, 0, 0:Lc], op=ALU.mult)
                        S.dve(r=pkeys + ["cf"], w=["PT"]).tensor_tensor(out=PT[0:Lc, :, 1, 0:Lc], in0=pexp[0:Lc, :, 1, 0:Lc],
                                                               in1=ETAB[0:Lc, 4 * hg:4 * hg + 4, 1, 0:Lc], op=ALU.mult)
                        for i in range(4):
                            oo = o_ps[pr:pr + 64, i * 128:i * 128 + Lc]
                            dd = den_ps[pr:pr + 64, i * 128:i * 128 + Lc]
                            if has_prev:
                                S.pe(r=[vpkey, "PT"], w=[("PB", 6)]).matmul(
                                    oo, lhsT=vprev[:, pr:pr + 64], rhs=PT[:, i, 0, 0:Lc], start=True, stop=False)
                            S.pe(r=[vkey, "PT"], w=[("PB", 6)]).matmul(
                                oo, lhsT=vown[0:Lc, pr:pr + 64], rhs=PT[0:Lc, i, 1, 0:Lc], start=(not has_prev), stop=True)
                            if has_prev:
                                S.pe(r=["cb", "PT"], w=[("PB", 7)]).matmul(dd, lhsT=ONESB[:, 0:64], rhs=PT[:, i, 0, 0:Lc],
                                                                    start=True, stop=False)
                            S.pe(r=["cb", "PT"], w=[("PB", 7)]).matmul(dd, lhsT=ONESB[0:Lc, 0:64], rhs=PT[0:Lc, i, 1, 0:Lc],
                                                                start=(not has_prev), stop=True)
                    S.dve(r=[("PB", 7), "esink"], w=["den"]).tensor_tensor(out=den[:, :, 0:Lc],
                                                    in0=den_ps[:, :].rearrange("p (i q) -> p i q", i=4)[:, :, 0:Lc],
                                                    in1=esink[:, :].unsqueeze(2).to_broadcast([128, 4, Lc]), op=ALU.add)
                    S.dve(r=["den"], w=["rden"]).reciprocal(out=rden[:, :, 0:Lc], in_=den[:, :, 0:Lc])
                    S.dve(r=[("PB", 6), "rden"], w=[("catt", g)]).tensor_tensor(out=catt[:, :, cs],
                                                           in0=o_ps[:, :].rearrange("p (i q) -> p i q", i=4)[:, :, 0:Lc],
                                                           in1=rden[:, :, 0:Lc], op=ALU.mult)
            ng = nseq * nj
            gl = [gchunk[0] - ng + i for i in range(ng)]
            ygk = [("yg", g) for g in gl]
            cak = [("catt", g) for g in gl]
            S.act(r=ygk, w=["sq"]).activation(out=sq[:, 0:4, 0:TTt], in_=yg[:, :, 0:TTt], func=AF.Square)
            for gi in range(2):
                pbt, pbk = next_pb()
                for cc in range(2):
                    S.pe(r=["sq", "cb"], w=[pbk]).matmul(pbt[:, 0:TTt], lhsT=ONESB, rhs=sq[:, 2 * gi + cc, 0:TTt],
                                                                   start=(cc == 0), stop=(cc == 1))
                S.act(r=[pbk, "eps"], w=["std"]).activation(out=std[:, 0:TTt], in_=pbt[:, 0:TTt], func=AF.Sqrt, bias=eps_t[:, 0:1],
                                                      scale=1.0 / 256)
                S.dve(r=["std"], w=["rstd"]).reciprocal(out=rstd[:, 0:TTt], in_=std[:, 0:TTt])
                for cc in range(2):
                    c = 2 * gi + cc
                    S.dve(r=ygk + ["pp", "rstd"], w=[("cssd", c)]).scalar_tensor_tensor(out=cssd[:, c, 0:TTt], in0=yg[:, c, 0:TTt],
                                                                scalar=pp[:, P_SNW + c:P_SNW + c + 1], in1=rstd[:, 0:TTt],
                                                                op0=ALU.mult, op1=ALU.mult)
            for m in range(8):
                pbt, pbk = next_pb()
                for c in range(8):
                    src = cssd[:, c, 0:TTt] if c < 4 else catt[:, c - 4, 0:TTt]
                    rk = [("cssd", c)] if c < 4 else cak
                    S.pe(r=["Wout"] + rk, w=[pbk]).matmul(pbt[:, 0:TTt], lhsT=Wout[:, c, m * 128:(m + 1) * 128],
                                                                        rhs=src, start=(c == 0), stop=(c == 7))
                S.dve(r=[pbk, xkey], w=[xkey]).tensor_tensor(out=X[:, m, 0:TTt], in0=pbt[:, 0:TTt], in1=X[:, m, 0:TTt],
                                                                   op=ALU.add)
            S.dma(("xst", slot), r=[xkey]).dma_start(out=kp(xdst)[:, :, col0:col0 + TTt], in_=X[:, :, 0:TTt])
        S.analyze().emit(nc)


def interleave(*gens):
    gens = [g for g in gens if g is not None]
    if os.environ.get("KSEQ"):
        for g in gens:
            for _ in g:
                pass
        return
    while gens:
        for g in list(gens):
            try:
                next(g)
            except StopIteration:
                gens.remove(g)


def rms_rstd(S, eps_t, ps_ap, pkey, ln_ap, lnkey, out_ap, okey, inv_n):
    if os.environ.get("KRSQ"):
        S.act(r=[pkey, "eps"], w=[lnkey]).activation(out=ln_ap, in_=ps_ap, func=AF.Sqrt, bias=eps_t[:, 0:1], scale=inv_n)
        S.dve(r=[lnkey], w=[okey]).reciprocal(out=out_ap, in_=ln_ap)
        return
    S.act(r=[pkey, "eps"], w=[lnkey]).activation(out=ln_ap, in_=ps_ap, func=AF.Ln, bias=eps_t[:, 0:1], scale=inv_n)
    S.act(r=[lnkey], w=[okey]).activation(out=out_ap, in_=ln_ap, func=AF.Exp, scale=-0.5)


def mixer_stage(nc, l, tiles, xsrc, xdst, D):
    _UID[0] += 1
    with contextlib.ExitStack() as st:
        sb = lambda name, shape, dt: st.enter_context(nc.sbuf_tensor(_uname(name), shape, dt))
        S = Sched()
        C = load_common(nc, S, st, l, D)
        pp, cb = C["pp"], C["cb"]
        cf = sb("cf", [128, NCF], F32)
        S.dma("cf", w=["cf"]).dma_start(out=cf[:], in_=D["cstf"])
        TRI = cf[:, C_TRI:C_TRI + 128]
        ONESF = cf[:, C_ONES:C_ONES + 128]
        ETAB = cf[:, C_E:C_E + 2048].rearrange("p (h b q) -> p h b q", h=8, b=2)
        IDB = cb[:, B_ID:B_ID + 128]
        ONESB = cb[:, B_ONES:B_ONES + 128]
        BDB = cb[:, B_BD:B_BD + 128]
        eps_t = sb("eps_t", [128, 1], F32)
        S.dve(w=["eps"]).memset(eps_t[:], EPS)

        Win = sb("Win", [128, 8, IN_DIM], BF16)
        Wout = sb("Wout", [128, 8, D_MODEL], BF16)
        for k in range(8):
            S.dma("win", w=["Win"], eng="pool", group=True).dma_start(out=Win[:, k, :], in_=D["w_in"][l, k * 128:(k + 1) * 128, :])
        for k in range(8):
            S.dma("wout", w=["Wout"], eng="pool", group=True).dma_start(out=Wout[:, k, :], in_=D["w_out"][l, k * 128:(k + 1) * 128, :])

        a_row = sb("a_row", [128, 8], F32)
        esink = sb("esink", [128, 4], F32)
        wq8 = sb("wq8", [128, 1], F32)
        S.act(r=["pp"], w=["a_row"]).activation(out=a_row[:], in_=pp[:, P_ALOG:P_ALOG + 8], func=AF.Exp)
        S.dve(r=["a_row"], w=["a_row"]).tensor_scalar(out=a_row[:], in0=a_row[:], scalar1=-1.0, scalar2=None, op0=ALU.mult)
        S.act(r=["pp"], w=["esink"]).activation(out=esink[:], in_=pp[:, P_SINK:P_SINK + 4], func=AF.Exp)
        S.dve(r=["pp"], w=["wq8"]).tensor_scalar(out=wq8[:], in0=pp[:, P_QN:P_QN + 1], scalar1=0.125, scalar2=None, op0=ALU.mult)

        xt = [sb(f"xt{i}", [128, 8, TT], F32) for i in range(2)]
        sq = sb("sq", [128, 8, TT], BF16)
        hn = sb("hn", [128, 8, TT], BF16)
        lnb0 = sb("lnb0", [128, TT], F32)
        rstd0 = sb("rstd0", [128, TT], F32)
        lnb = sb("lnb", [128, 2 * TT], F32)
        rstd = sb("rstd", [128, 2 * TT], F32)
        sq2 = sb("sq2", [128, 4, TT], BF16)
        lnb2 = sb("lnb2", [128, TT], F32)
        rstd2 = sb("rstd2", [128, TT], F32)
        sz = sb("sz", [128, 4, TT], F32)
        pcb = sb("pcb", [128, 8, TT + 4], F32)
        pcar = sb("pcar", [128, 8, 3], F32)
        cv = [sb(f"cv{i}", [128, TT], F32) for i in range(2)]
        xbc = sb("xbc", [128, 8, TT], BF16)
        qk32 = sb("qk32", [128, 5, TT], F32)
        qn = sb("qn", [128, 4, TT], BF16)
        kn32 = sb("kn32", [128, TT], F32)
        kbuf = sb("kbuf", [128, 128 + TT], BF16)
        kcar = sb("kcar", [128, 128], BF16)
        kcache = [sb(f"kcache{i}", [128, 128], BF16) for i in range(2)]
        vcache = [sb(f"vcache{i}", [128, 128], BF16) for i in range(2)]
        yg = sb("yg", [128, 4, TT], F32)
        cssd = sb("cssd", [128, 4, TT], BF16)
        catt = sb("catt", [128, 4, TT], BF16)
        dtb = sb("dtb", [128, 8], F32)
        e1 = sb("e1", [128, 8], F32)
        dt = sb("dt", [128, 8], F32)
        dA = sb("dA", [128, 8], F32)
        acum = sb("acum", [128, 8], F32)
        diff = sb("diff", [128, 8], F32)
        tail = sb("tail", [128, 8], F32)
        cd = sb("cd", [128, 8], F32)
        dtt = sb("dtt", [128, 8], F32)
        dA_rep = sb("dA_rep", [128, 8, 128], F32)
        xtm = sb("xtm", [128, 768], BF16)
        xdt = sb("xdt", [128, 8, 64], BF16)
        xst = sb("xst", [128, 8, 64], BF16)
        cbm = sb("cbm", [128, 2, 128], F32)
        seg = sb("seg", [128, 8, 128], F32)
        MT = sb("MT", [128, 8, 128], BF16)
        eb = sb("eb", [128, 8, 128], F32)
        Cp = sb("Cp", [128, 8, 128], BF16)
        ygt = sb("ygt", [128, 4, 128], F32)
        Sst = [sb(f"Sst{i}", [128, 8, 64], F32) for i in range(2)]
        Sbf = sb("Sbf", [128, 8, 64], BF16)
        vb = [sb(f"vb{i}", [128, 128], BF16) for i in range(3)]
        v32 = sb("v32", [128, 128], F32)
        pexp = [sb(f"pexp{i}", [128, 2, 2, 128], F32) for i in range(2)]
        PT = [sb(f"PT{i}", [128, 2, 2, 128], BF16) for i in range(2)]
        den = sb("den", [128, 4, 128], F32)
        lnd = sb("lnd", [128, 4, 128], F32)
        rden = sb("rden", [128, 4, 128], F32)

        PB = [st.enter_context(nc.psum_tensor(_uname(f"pb{i}"), [128, 512], F32)) for i in range(8)]
        small = PB[2]
        tr_ps = PB[3][:].bitcast(BF16)
        ns_ps = PB[3]
        y_ps = PB[3]
        acb_ps = PB[4]
        sc_ps = PB[5]
        o_ps = PB[6]
        den_ps = PB[7]
        K3, K4, K5, K6, K7 = ("PB", 3), ("PB", 4), ("PB", 5), ("PB", 6), ("PB", 7)
        pbrot = [0]

        def next_pb():
            i = pbrot[0] % 2
            pbrot[0] += 1
            return PB[i], ("PB", i)

        S.dve(w=["pcar"]).memset(pcar[:], 0.0)
        S.dve(w=[("S", 0)]).memset(Sst[0][:], 0.0)
        S.pool(w=["Sbf"]).memset(Sbf[:], 0.0)
        S.pool(w=["kcar"]).memset(kcar[:], 0.0)

        gchunk = [0]
        hnk = [("hn", k) for k in range(8)]
        rndc = [0]

        def front(ti):
            tl = tiles[ti]
            col0, nseq, L, sample = tl["col0"], tl["nseq"], tl["L"], tl["sample"]
            TTt = nseq * L
            slot = ti % 2
            X = xt[slot]
            xkey = ("xt", slot)
            xap = X[:, :, 0:TTt]
            S.dma(("xld", slot), w=[xkey]).dma_start(out=X[:, :, 0:TTt], in_=kp(xsrc)[:, :, col0:col0 + TTt])
            yield
            S.pool(r=[xkey], w=["sq"]).tensor_tensor(out=sq[:, :, 0:TTt], in0=xap, in1=xap, op=ALU.mult)
            yield
            pbt, pbk = next_pb()
            for k in range(8):
                S.pe(r=["sq", "cb"], w=[pbk]).matmul(pbt[:, 0:TTt], lhsT=ONESB, rhs=sq[:, k, 0:TTt], start=(k == 0), stop=(k == 7))
            yield
            rms_rstd(S, eps_t, pbt[:, 0:TTt], pbk, lnb0[:, 0:TTt], "lnb0", rstd0[:, 0:TTt], "rstd0", 1.0 / D_MODEL)
            yield
            for k in range(8):
                S.dve(r=[xkey, "rstd0", "pp"], w=[("hn", k)]).scalar_tensor_tensor(
                    out=hn[:, k, 0:TTt], in0=xap[:, k, :], scalar=pp[:, P_N1 + k:P_N1 + k + 1], in1=rstd0[:, 0:TTt],
                    op0=ALU.mult, op1=ALU.mult)
                if k % 2 == 1:
                    yield
            pcv = pcb[:, :, 0:nseq * (3 + L)].rearrange("p c (s t) -> p c s t", s=nseq)
            if sample:
                for c in range(8):
                    S.dma("sconv", r=[], w=["pcb_prev"], group=True).dma_start(out=pcv[:, c, :, 0:3], in_=D["s_conv"][l, :, c])
            else:
                S.pool(r=["pcar"], w=["pcb_prev"]).tensor_copy(out=pcv[:, :, 0, 0:3], in_=pcar[:])
                S.pool(r=["kcar"], w=["kbuf_prev"]).tensor_copy(out=kbuf[:, 0:128], in_=kcar[:])
            yield

            def proj(kind, c, wc):
                pbt, pbk = next_pb()
                for k in range(8):
                    S.pe(r=["Win"] + hnk, w=[pbk]).matmul(pbt[:, 0:TTt], lhsT=Win[:, k, wc:wc + 128], rhs=hn[:, k, 0:TTt],
                                                          start=(k == 0), stop=(k == 7))
                if kind == "z":
                    S.act(r=[pbk], w=[("sz", c)]).activation(out=sz[:, c, 0:TTt], in_=pbt[:, 0:TTt], func=AF.Silu)
                elif kind == "x":
                    S.act(r=[pbk], w=[("pcb", c)]).activation(out=pcv[:, c, :, 3:3 + L],
                                                             in_=pbt[:, 0:TTt].rearrange("p (s t) -> p s t", s=nseq), func=AF.Copy)
                else:
                    S.dve(r=[pbk], w=[("qk32", c)]).tensor_copy(out=qk32[:, c, 0:TTt], in_=pbt[:, 0:TTt])

            for c in range(4):
                proj("q", c, Q0 + c * 128)
                yield
            proj("q", 4, K0)
            yield
            S.pool(r=[("qk32", c) for c in range(5)], w=["sq"]).tensor_tensor(out=sq[:, 0:5, 0:TTt], in0=qk32[:, :, 0:TTt],
                                                                             in1=qk32[:, :, 0:TTt], op=ALU.mult)
            for c in range(8):
                proj("x", c, XBC0 + c * 128)
                yield
            for grp in [(0, 1), (2, 3), (4,)]:
                pbt, pbk = next_pb()
                n = len(grp)
                for i, c in enumerate(grp):
                    S.pe(r=["sq", "cb"], w=[pbk]).matmul(pbt[:, i * TTt:(i + 1) * TTt], lhsT=BDB, rhs=sq[:, c, 0:TTt], start=True, stop=True)
                rms_rstd(S, eps_t, pbt[:, 0:n * TTt], pbk, lnb[:, 0:n * TTt], "lnb", rstd[:, 0:n * TTt], "rstd", 1.0 / 64)
                for i, c in enumerate(grp):
                    rs = rstd[:, i * TTt:(i + 1) * TTt]
                    if c < 4:
                        S.dve(r=[("qk32", c), "wq8", "rstd"], w=[("qn", c)]).scalar_tensor_tensor(
                            out=qn[:, c, 0:TTt], in0=qk32[:, c, 0:TTt], scalar=wq8[:, 0:1], in1=rs, op0=ALU.mult, op1=ALU.mult)
                    else:
                        S.dve(r=[("qk32", 4), "pp", "rstd"], w=["kn32"]).scalar_tensor_tensor(
                            out=kn32[:, 0:TTt], in0=qk32[:, 4, 0:TTt], scalar=pp[:, P_KN:P_KN + 1], in1=rs, op0=ALU.mult, op1=ALU.mult)
                        S.act(r=["kn32"], w=["kbuf"]).activation(out=kbuf[:, 128:128 + TTt], in_=kn32[:, 0:TTt], func=AF.Copy)
                yield
            if not sample:
                S.pool(r=["kbuf", "kbuf_prev"], w=["kcar"]).tensor_copy(out=kcar[:], in_=kbuf[:, TTt:TTt + 128])
                if tl["last"]:
                    S.dma("pk", r=["kn32"]).dma_start(out=D["p_kT"][l], in_=kn32[:, TTt - 128:TTt])
            else:
                for s in range(nseq):
                    S.dma("okc", group=True, w=["okc"]).dma_start(out=D["o_kT"][l, s, :, 0:124], in_=D["s_kT"][l, s, :, 4:128])
                    S.dma("okn", group=True, r=["kn32"], w=["okn"]).dma_start(out=D["o_kT"][l, s, :, 124:128], in_=kn32[:, s * LS:(s + 1) * LS])
                    S.dma("ovc", group=True, w=["ovc"]).dma_start(out=D["o_v"][l, s, 0:124, :], in_=D["s_v"][l, s, 4:128, :])
            yield
            for c in range(4):
                proj("z", c, Z0 + c * 128)
                yield
            for c in range(8):
                t = cv[c % 2]
                tk = ("cv", c % 2)
                t3 = t[:, 0:TTt].rearrange("p (s t) -> p s t", s=nseq)
                cw = lambda j, c=c: pp[:, P_CW + c * 4 + j:P_CW + c * 4 + j + 1]
                S.pool(r=[("pcb", c), "pp"], w=[tk]).tensor_scalar(out=t3, in0=pcv[:, c, :, 3:3 + L], scalar1=cw(3),
                                                                  scalar2=pp[:, P_CB + c:P_CB + c + 1], op0=ALU.mult, op1=ALU.add)
                for j in (2, 1, 0):
                    S.dve(r=[("pcb", c), "pcb_prev", "pp", tk], w=[tk]).scalar_tensor_tensor(
                        out=t3, in0=pcv[:, c, :, j:j + L], scalar=cw(j), in1=t3, op0=ALU.mult, op1=ALU.add)
                S.act(r=[tk], w=[("xbc", c)]).activation(out=xbc[:, c, 0:TTt], in_=t[:, 0:TTt], func=AF.Silu)
                yield
            allpcb = [("pcb", c) for c in range(8)]
            if sample:
                for c in range(8):
                    S.dma("oconv", r=allpcb + ["pcb_prev"], w=["oconv"], group=True).dma_start(out=D["o_conv"][l, :, c], in_=pcv[:, c, :, L:L + 3])
            else:
                S.pool(r=allpcb + ["pcb_prev"], w=["pcar"]).tensor_copy(out=pcar[:], in_=pcv[:, :, 0, L:L + 3])
                if tl["last"]:
                    S.dma("pconv", r=["pcar"]).dma_start(out=D["p_conv"][l], in_=pcar[:])
            yield

        def ssd_chunk(tl, s, j, g, Scur, skey):
            nseq, L, sample = tl["nseq"], tl["L"], tl["sample"]
            Lc = min(L, 128)
            nj = L // Lc
            c0 = s * L + j * Lc
            cs = slice(c0, c0 + Lc)
            for k in range(8):
                S.pe(r=["Win"] + hnk, w=[("PB", 2)]).matmul(small[0:Lc, 0:8], lhsT=hn[:, k, cs], rhs=Win[:, k, DT0:DT0 + 8],
                                                            start=(k == 0), stop=(k == 7))
            for gidx in range(2):
                S.pe(r=[("xbc", 4 + gidx), ("xbc", 6 + gidx)], w=[("PB", 2)]).matmul(
                    small[0:Lc, 160 + gidx * 128:160 + gidx * 128 + Lc], lhsT=xbc[:, 4 + gidx, cs], rhs=xbc[:, 6 + gidx, cs], start=True, stop=True)
            yield
            S.dve(r=[("PB", 2), "pp"], w=["dtb"]).tensor_tensor(out=dtb[0:Lc, :], in0=small[0:Lc, 0:8], in1=pp[0:Lc, P_DTB:P_DTB + 8], op=ALU.add)
            S.dve(r=[("PB", 2), "cf"], w=["cbm"]).tensor_tensor(out=cbm[0:Lc, :, 0:Lc],
                                                               in0=small[0:Lc, 160:416].rearrange("p (g q) -> p g q", g=2)[:, :, 0:Lc],
                                                               in1=TRI[0:Lc, 0:Lc].unsqueeze(1).to_broadcast([Lc, 2, Lc]), op=ALU.mult)
            yield
            S.act(r=["dtb"], w=["e1"]).activation(out=e1[0:Lc, :], in_=dtb[0:Lc, :], func=AF.Exp)
            S.act(r=["e1"], w=["dt"]).activation(out=dt[0:Lc, :], in_=e1[0:Lc, :], func=AF.Ln, bias=1.0)
            yield
            S.dve(r=["dt", "a_row"], w=["dA"]).tensor_tensor(out=dA[0:Lc, :], in0=dt[0:Lc, :], in1=a_row[0:Lc, :], op=ALU.mult)
            yield
            S.pe(r=["dA", "cf"], w=[("PB", 2)]).matmul(small[0:Lc, 144:152], lhsT=TRI[0:Lc, 0:Lc], rhs=dA[0:Lc, :], start=True, stop=True)
            S.pe(r=["dA", "cf"], w=[("PB", 2)]).matmul(small[:, 152:160], lhsT=ONESF[0:Lc, :], rhs=dA[0:Lc, :], start=True, stop=True)
            S.pool(r=["dA"], w=["dA_rep"]).tensor_copy(out=dA_rep[0:Lc, :, :], in_=dA[0:Lc, :].unsqueeze(2).to_broadcast([Lc, 8, 128]))
            yield
            S.dve(r=[("PB", 2)], w=["acum"]).tensor_copy(out=acum[0:Lc, :], in_=small[0:Lc, 144:152])
            S.dve(r=[("PB", 2), "acum"], w=["diff"]).tensor_tensor(out=diff[0:Lc, :], in0=small[0:Lc, 152:160], in1=acum[0:Lc, :], op=ALU.subtract)
            S.act(r=["diff"], w=["tail"]).activation(out=tail[0:Lc, :], in_=diff[0:Lc, :], func=AF.Exp)
            S.act(r=[("PB", 2)], w=["cd"]).activation(out=cd[:, :], in_=small[:, 152:160], func=AF.Exp)
            yield
            S.dve(r=["dt", "tail"], w=["dtt"]).tensor_tensor(out=dtt[0:Lc, :], in0=dt[0:Lc, :], in1=tail[0:Lc, :], op=ALU.mult)
            yield
            for hh in range(2):
                for r_ in range(4):
                    h = hh * 4 + r_
                    S.pe(r=["dA_rep", "cf", "acum", "diff", "cd", "tail", "dtt"], w=[K4]).matmul(
                        acb_ps[:, r_ * 128:r_ * 128 + Lc], lhsT=dA_rep[0:Lc, h, :], rhs=TRI[0:Lc, 0:Lc], start=True, stop=True)
                yield
                for r_ in range(4):
                    h = hh * 4 + r_
                    S.dve(r=[K4, "acum"], w=[("seg", hh)]).tensor_scalar(out=seg[0:Lc, h, 0:Lc], in0=acb_ps[0:Lc, r_ * 128:r_ * 128 + Lc],
                                                                         scalar1=acum[0:Lc, h:h + 1], scalar2=0.0, op0=ALU.subtract, op1=ALU.min)
                S.act(r=[K4], w=[("eb", hh)]).activation(out=eb[:, hh * 4:(hh + 1) * 4, 0:Lc],
                                                         in_=acb_ps[:, :].rearrange("p (r q) -> p r q", r=4)[:, :, 0:Lc], func=AF.Exp)
                yield
                S.act(r=[("seg", hh)], w=[("seg", hh)]).activation(out=seg[0:Lc, hh * 4:(hh + 1) * 4, 0:Lc], in_=seg[0:Lc, hh * 4:(hh + 1) * 4, 0:Lc], func=AF.Exp)
                S.pool(r=[("eb", hh), ("xbc", 6 + hh)], w=[("Cp", hh)]).tensor_tensor(
                    out=Cp[:, hh * 4:(hh + 1) * 4, 0:Lc], in0=eb[:, hh * 4:(hh + 1) * 4, 0:Lc],
                    in1=xbc[:, 6 + hh, cs].unsqueeze(1).to_broadcast([128, 4, Lc]), op=ALU.mult)
                yield
                S.dve(r=[("seg", hh), "cbm"], w=[("MT", hh)]).tensor_tensor(
                    out=MT[0:Lc, hh * 4:(hh + 1) * 4, 0:Lc], in0=seg[0:Lc, hh * 4:(hh + 1) * 4, 0:Lc],
                    in1=cbm[0:Lc, hh, 0:Lc].unsqueeze(1).to_broadcast([Lc, 4, Lc]), op=ALU.mult)
                yield
            for i in range(6):
                S.pe(r=[("xbc", i), "cb"], w=[K3]).transpose(tr_ps[0:Lc, i * 128:(i + 1) * 128], xbc[:, i, cs], IDB)
            yield
            S.act(r=[K3], w=["xtm"]).activation(out=xtm[0:Lc, :], in_=tr_ps[0:Lc, 0:768], func=AF.Copy)
            yield
            xtm3 = xtm[0:Lc, 0:512].rearrange("p (h d) -> p h d", h=8)
            S.pool(r=["xtm", "dt"], w=["xdt"]).tensor_tensor(out=xdt[0:Lc], in0=xtm3, in1=dt[0:Lc, :].unsqueeze(2).to_broadcast([Lc, 8, 64]), op=ALU.mult)
            S.pool(r=["xtm", "dtt"], w=["xst"]).tensor_tensor(out=xst[0:Lc], in0=xtm3, in1=dtt[0:Lc, :].unsqueeze(2).to_broadcast([Lc, 8, 64]), op=ALU.mult)
            yield
            for gidx in range(2):
                S.pe(r=["xtm", "xst"], w=[K3]).matmul(ns_ps[:, gidx * 256:(gidx + 1) * 256], lhsT=xtm[0:Lc, 512 + gidx * 128:512 + (gidx + 1) * 128],
                                                      rhs=xst[0:Lc, gidx * 4:(gidx + 1) * 4, :], start=True, stop=True)
            yield
            S.pool(r=[skey, "cd"], w=[skey]).tensor_tensor(out=Scur[:], in0=Scur[:], in1=cd[:, :].unsqueeze(2).to_broadcast([128, 8, 64]), op=ALU.mult)
            S.dve(r=[skey, K3], w=[skey]).tensor_tensor(out=Scur[:], in0=Scur[:], in1=ns_ps[:, :].rearrange("p (h d) -> p h d", h=8), op=ALU.add)
            yield
            for h in range(8):
                pr = (h % 2) * 64
                yo = y_ps[pr:pr + 64, (h // 2) * 128:(h // 2) * 128 + Lc]
                S.pe(r=["xdt", ("MT", h // 4)], w=[K3]).matmul(yo, lhsT=xdt[0:Lc, h, :], rhs=MT[0:Lc, h, 0:Lc], start=True, stop=False)
                S.pe(r=["Sbf", ("Cp", h // 4)], w=[K3]).matmul(yo, lhsT=Sbf[:, h, :], rhs=Cp[:, h, 0:Lc], start=False, stop=True)
                if h % 4 == 3:
                    yield
            for c in range(4):
                S.dve(r=[("xbc", c), "pp", K3], w=["ygt"]).scalar_tensor_tensor(
                    out=ygt[:, c, 0:Lc], in0=xbc[:, c, cs], scalar=pp[:, P_DSK + c:P_DSK + c + 1], in1=y_ps[:, c * 128:c * 128 + Lc],
                    op0=ALU.mult, op1=ALU.add)
            yield
            S.pool(r=["ygt"] + [("sz", c) for c in range(4)], w=[("yg", g)]).tensor_tensor(out=yg[:, :, cs], in0=ygt[:, :, 0:Lc], in1=sz[:, :, cs], op=ALU.mult)
            if sample:
                S.dma(("sst", s % 2), r=[skey]).dma_start(out=D["o_ssm"][l, s], in_=Scur[:].rearrange("p h d -> p (h d)"))
            else:
                S.act(r=[skey], w=["Sbf"]).activation(out=Sbf[:], in_=Scur[:], func=AF.Copy)
                if tl["last"] and j == nj - 1:
                    S.dma("pssm", r=[skey]).dma_start(out=D["p_ssm"][l], in_=Scur[:].rearrange("p h d -> p (h d)"))
            yield

        def att_chunk(tl, s, j, g):
            nseq, L, sample = tl["nseq"], tl["L"], tl["sample"]
            Lc = min(L, 128)
            nj = L // Lc
            c0 = s * L + j * Lc
            cs = slice(c0, c0 + Lc)
            has_prev = sample or not (tl["first"] and j == 0)
            for k in range(8):
                S.pe(r=["Win"] + hnk, w=[K7]).matmul(PB[7][0:Lc, 0:128], lhsT=hn[:, k, cs], rhs=Win[:, k, V0:V0 + 128],
                                                     start=(k == 0), stop=(k == 7))
            yield
            vown = vb[g % 3]
            vkey = ("vb", g % 3)
            S.act(r=[K7], w=[vkey]).activation(out=vown[0:Lc, :], in_=PB[7][0:Lc, 0:128], func=AF.Copy)
            if sample:
                S.dve(r=[K7], w=["v32"]).tensor_copy(out=v32[0:Lc, :], in_=PB[7][0:Lc, 0:128])
                S.dma("ovn", r=["v32"]).dma_start(out=D["o_v"][l, s, 124:128, :], in_=v32[0:Lc, :])
                kpre, kpkey = kcache[s % 2][:, :], ("kcache", s % 2)
                vprev, vpkey = vcache[s % 2], ("vcache", s % 2)
                kown = kbuf[:, 128 + c0:128 + c0 + Lc]
            else:
                if tl["last"] and j == nj - 1:
                    S.dve(r=[K7], w=["v32"]).tensor_copy(out=v32[0:Lc, :], in_=PB[7][0:Lc, 0:128])
                    S.dma("pv", r=["v32"]).dma_start(out=D["p_v"][l], in_=v32[:, :])
                kpre, kpkey = kbuf[:, j * 128:(j + 1) * 128], ("kbuf_prev" if j == 0 else "kbuf")
                kown = kbuf[:, 128 + j * 128:128 + (j + 1) * 128]
                vprev, vpkey = vb[(g - 1) % 3], ("vb", (g - 1) % 3)
            yield
            for hg in range(2):
                pr = hg * 64
                for rnd in range(2):
                    sl_ = rndc[0] % 2
                    rndc[0] += 1
                    PX, pxk = pexp[sl_], ("pexp", sl_)
                    PTt, ptk = PT[sl_], ("PT", sl_)
                    for ii in range(2):
                        i = 2 * rnd + ii
                        base = ii * 256
                        qh = qn[pr:pr + 64, i, cs]
                        if has_prev:
                            S.pe(r=[kpkey, ("qn", i)], w=[K5]).matmul(sc_ps[:, base:base + Lc], lhsT=kpre[pr:pr + 64, :], rhs=qh, start=True, stop=True)
                        S.pe(r=["kbuf", ("qn", i)], w=[K5]).matmul(sc_ps[0:Lc, base + 128:base + 128 + Lc], lhsT=kown[pr:pr + 64, :], rhs=qh,
                                                                   start=True, stop=True)
                    yield
                    scv = sc_ps[:, :].rearrange("p (i b q) -> p i b q", i=2, b=2)
                    h0 = 4 * hg + 2 * rnd
                    if has_prev:
                        S.act(r=[K5], w=[pxk]).activation(out=PX[:, :, 0, 0:Lc], in_=scv[:, :, 0, 0:Lc], func=AF.Exp)
                    S.act(r=[K5], w=[pxk]).activation(out=PX[0:Lc, :, 1, 0:Lc], in_=scv[0:Lc, :, 1, 0:Lc], func=AF.Exp)
                    yield
                    if has_prev:
                        S.dve(r=[pxk, "cf"], w=[ptk]).tensor_tensor(out=PTt[:, :, 0, 0:Lc], in0=PX[:, :, 0, 0:Lc],
                                                                    in1=ETAB[:, h0:h0 + 2, 0, 0:Lc], op=ALU.mult)
                    S.dve(r=[pxk, "cf"], w=[ptk]).tensor_tensor(out=PTt[0:Lc, :, 1, 0:Lc], in0=PX[0:Lc, :, 1, 0:Lc],
                                                                in1=ETAB[0:Lc, h0:h0 + 2, 1, 0:Lc], op=ALU.mult)
                    yield
                    for ii in range(2):
                        i = 2 * rnd + ii
                        oo = o_ps[pr:pr + 64, i * 128:i * 128 + Lc]
                        dd = den_ps[pr:pr + 64, i * 128:i * 128 + Lc]
                        if has_prev:
                            S.pe(r=[vpkey, ptk], w=[K6]).matmul(oo, lhsT=vprev[:, pr:pr + 64], rhs=PTt[:, ii, 0, 0:Lc], start=True, stop=False)
                        S.pe(r=[vkey, ptk], w=[K6]).matmul(oo, lhsT=vown[0:Lc, pr:pr + 64], rhs=PTt[0:Lc, ii, 1, 0:Lc], start=(not has_prev), stop=True)
                        if has_prev:
                            S.pe(r=["cb", ptk], w=[K7]).matmul(dd, lhsT=ONESB[:, 0:64], rhs=PTt[:, ii, 0, 0:Lc], start=True, stop=False)
                        S.pe(r=["cb", ptk], w=[K7]).matmul(dd, lhsT=ONESB[0:Lc, 0:64], rhs=PTt[0:Lc, ii, 1, 0:Lc], start=(not has_prev), stop=True)
                    yield
            S.dve(r=[K7, "esink"], w=["den"]).tensor_tensor(out=den[:, :, 0:Lc], in0=den_ps[:, :].rearrange("p (i q) -> p i q", i=4)[:, :, 0:Lc],
                                                            in1=esink[:, :].unsqueeze(2).to_broadcast([128, 4, Lc]), op=ALU.add)
            yield
            if os.environ.get("KRSQ"):
                S.dve(r=["den"], w=["rden"]).reciprocal(out=rden[:, :, 0:Lc], in_=den[:, :, 0:Lc])
            else:
                S.act(r=["den"], w=["lnd"]).activation(out=lnd[:, :, 0:Lc], in_=den[:, :, 0:Lc], func=AF.Ln)
                S.act(r=["lnd"], w=["rden"]).activation(out=rden[:, :, 0:Lc], in_=lnd[:, :, 0:Lc], func=AF.Exp, scale=-1.0)
            yield
            S.dve(r=[K6, "rden"], w=[("catt", g)]).tensor_tensor(out=catt[:, :, cs], in0=o_ps[:, :].rearrange("p (i q) -> p i q", i=4)[:, :, 0:Lc],
                                                                 in1=rden[:, :, 0:Lc], op=ALU.mult)
            yield

        def back(ti, gl):
            tl = tiles[ti]
            col0, nseq, L = tl["col0"], tl["nseq"], tl["L"]
            TTt = nseq * L
            slot = ti % 2
            X = xt[slot]
            xkey = ("xt", slot)
            ygk = [("yg", g) for g in gl]
            cak = [("catt", g) for g in gl]
            S.pool(r=ygk, w=["sq2"]).tensor_tensor(out=sq2[:, :, 0:TTt], in0=yg[:, :, 0:TTt], in1=yg[:, :, 0:TTt], op=ALU.mult)
            yield
            for gi in range(2):
                bi = 6 + gi
                pbt, pbk = PB[bi], ("PB", bi)
                for cc in range(2):
                    S.pe(r=["sq2", "cb"], w=[pbk]).matmul(pbt[:, 0:TTt], lhsT=ONESB, rhs=sq2[:, 2 * gi + cc, 0:TTt], start=(cc == 0), stop=(cc == 1))
                rms_rstd(S, eps_t, pbt[:, 0:TTt], pbk, lnb2[:, 0:TTt], "lnb2", rstd2[:, 0:TTt], "rstd2", 1.0 / 256)
                for cc in range(2):
                    c = 2 * gi + cc
                    S.dve(r=ygk + ["pp", "rstd2"], w=[("cssd", c)]).scalar_tensor_tensor(
                        out=cssd[:, c, 0:TTt], in0=yg[:, c, 0:TTt], scalar=pp[:, P_SNW + c:P_SNW + c + 1], in1=rstd2[:, 0:TTt],
                        op0=ALU.mult, op1=ALU.mult)
                yield
            for m in range(8):
                bi = 6 + m % 2
                pbt, pbk = PB[bi], ("PB", bi)
                for c in range(8):
                    src = cssd[:, c, 0:TTt] if c < 4 else catt[:, c - 4, 0:TTt]
                    rk = [("cssd", c)] if c < 4 else cak
                    S.pe(r=["Wout"] + rk, w=[pbk]).matmul(pbt[:, 0:TTt], lhsT=Wout[:, c, m * 128:(m + 1) * 128], rhs=src, start=(c == 0), stop=(c == 7))
                S.dve(r=[pbk, xkey], w=[xkey]).tensor_tensor(out=X[:, m, 0:TTt], in0=pbt[:, 0:TTt], in1=X[:, m, 0:TTt], op=ALU.add)
                yield
            S.dma(("xst", slot), r=[xkey]).dma_start(out=kp(xdst)[:, :, col0:col0 + TTt], in_=X[:, :, 0:TTt])
            yield

        interleave(front(0))
        for ti, tl in enumerate(tiles):
            nseq, L, sample = tl["nseq"], tl["L"], tl["sample"]
            Lc = min(L, 128)
            nj = L // Lc
            gl = []
            for s in range(nseq):
                if sample:
                    sslot = s % 2
                    Scur, skey = Sst[sslot], ("S", sslot)
                    S.dma(("sld", sslot), w=[skey]).dma_start(out=Scur[:].rearrange("p h d -> p (h d)"), in_=D["s_ssm"][l, s])
                    S.act(r=[skey], w=["Sbf"]).activation(out=Sbf[:], in_=Scur[:], func=AF.Copy)
                    S.dma(("kcl", sslot), w=[("kcache", sslot)], eng="pool").dma_start(out=kcache[sslot][:], in_=D["s_kT"][l, s])
                    S.dma(("vcl", sslot), w=[("vcache", sslot)], eng="pool").dma_start(out=vcache[sslot][:], in_=D["s_v"][l, s])
                else:
                    Scur, skey = Sst[0], ("S", 0)
                for j in range(nj):
                    g = gchunk[0]
                    gchunk[0] += 1
                    gl.append(g)
                    if os.environ.get("KLOG"):
                        print("KLOG chunk start", ti, s, j, "nrec", getattr(S, "nrec", 0))
                        interleave(ssd_chunk(tl, s, j, g, Scur, skey))
                        print("KLOG  after ssd nrec", getattr(S, "nrec", 0))
                        interleave(att_chunk(tl, s, j, g))
                        print("KLOG  after att nrec", getattr(S, "nrec", 0))
                    else:
                        interleave(ssd_chunk(tl, s, j, g, Scur, skey), att_chunk(tl, s, j, g))
            if os.environ.get("KLOG"):
                print("KLOG before back", ti, "nrec", getattr(S, "nrec", 0))
            interleave(back(ti, gl), front(ti + 1) if ti + 1 < len(tiles) else None)
            if os.environ.get("KLOG"):
                print("KLOG after back+front", ti, "nrec", getattr(S, "nrec", 0))
        S.analyze().emit(nc)


def ffn_stage(nc, l, tiles, xsrc, xdst, D):
    _UID[0] += 1
    with contextlib.ExitStack() as st:
        sb = lambda name, shape, dt: st.enter_context(nc.sbuf_tensor(_uname(name), shape, dt))
        S = Sched()
        C = load_common(nc, S, st, l, D)
        pp, cb = C["pp"], C["cb"]
        ONESB = cb[:, B_ONES:B_ONES + 128]
        eps_t = sb("eps_t", [128, 1], F32)
        S.dve(w=["eps"]).memset(eps_t[:], EPS)
        Wup = sb("Wup", [128, 8, 2 * D_FF], BF16)
        Wdn = sb("Wdn", [128, NPAIR, D_MODEL], BF16)
        GCOLS = 11 * 128
        for gq in (0, 2, 1, 3):
            for k in range(8):
                S.dma(("wup", gq), w=[("Wup", gq)], eng="pool", group=True).dma_start(
                    out=Wup[:, k, gq * GCOLS:(gq + 1) * GCOLS], in_=D["w_up"][l, k * 128:(k + 1) * 128, gq * GCOLS:(gq + 1) * GCOLS])
        for j in range(NPAIR):
            S.dma("wdn", w=["Wdn"], eng="pool", group=True).dma_start(out=Wdn[:, j, :], in_=D["w_down"][l, j * 128:(j + 1) * 128, :])
        xt = [sb(f"xt{i}", [128, 8, TT], F32) for i in range(2)]
        sq = sb("sq", [128, 8, TT], BF16)
        hn = [sb(f"hn{i}", [128, 8, TT], BF16) for i in range(2)]
        lnb = sb("lnb", [128, TT], F32)
        rstd = sb("rstd", [128, TT], F32)
        ub = [sb(f"ub{i}", [128, TT + 2], F32) for i in range(2)]
        t0 = [sb(f"t0{i}", [128, TT], F32) for i in range(2)]
        sg = [sb(f"sg{i}", [128, TT], F32) for i in range(2)]
        gb = sb("gb", [128, NPAIR, TT], BF16)
        car = sb("car", [128, 44, NSQ, 2], F32)
        PB = [st.enter_context(nc.psum_tensor(_uname(f"pb{i}"), [128, 512], F32)) for i in range(8)]
        cark = [("car", ch) for ch in range(44)]
        S.dve(w=cark).memset(car[:], 0.0)
        rot = [0]
        uct = [0]

        def front(ti):
            tl = tiles[ti]
            col0, nseq, L = tl["col0"], tl["nseq"], tl["L"]
            TTt = nseq * L
            slot = ti % 2
            X, xkey = xt[slot], ("xt", slot)
            H = hn[slot]
            xap = X[:, :, 0:TTt]
            S.dma(("xld", slot), w=[xkey]).dma_start(out=X[:, :, 0:TTt], in_=kp(xsrc)[:, :, col0:col0 + TTt])
            yield
            S.pool(r=[xkey], w=["sq"]).tensor_tensor(out=sq[:, :, 0:TTt], in0=xap, in1=xap, op=ALU.mult)
            yield
            for k in range(8):
                S.pe(r=["sq", "cb"], w=[("PB", 0)]).matmul(PB[0][:, 0:TTt], lhsT=ONESB, rhs=sq[:, k, 0:TTt], start=(k == 0), stop=(k == 7))
            yield
            rms_rstd(S, eps_t, PB[0][:, 0:TTt], ("PB", 0), lnb[:, 0:TTt], "lnb", rstd[:, 0:TTt], "rstd", 1.0 / D_MODEL)
            yield
            for k in range(8):
                S.dve(r=[xkey, "rstd", "pp"], w=[("hn", slot, k)]).scalar_tensor_tensor(
                    out=H[:, k, 0:TTt], in0=xap[:, k, :], scalar=pp[:, P_N2 + k:P_N2 + k + 1], in1=rstd[:, 0:TTt], op0=ALU.mult, op1=ALU.mult)
                yield

        def up(ti):
            tl = tiles[ti]
            nseq, L, sample = tl["nseq"], tl["L"], tl["sample"]
            TTt = nseq * L
            slot = ti % 2
            H = hn[slot]
            hnk = [("hn", slot, k) for k in range(8)]
            if sample:
                S.dma("sffn", w=cark).dma_start(out=car[:], in_=D["s_ffn"][l])
            for j in range(NPAIR):
                for half in range(2):
                    ch = j + half * NPAIR
                    bi = 1 + rot[0] % 3
                    rot[0] += 1
                    pbt, pbk = PB[bi], ("PB", bi)
                    us = uct[0] % 2
                    uct[0] += 1
                    U, ukey = ub[us], ("ub", us)
                    T0, tkey = t0[us], ("t0", us)
                    uv = U[:, 0:nseq * (2 + L)].rearrange("p (s t) -> p s t", s=nseq)
                    t3 = T0[:, 0:TTt].rearrange("p (s t) -> p s t", s=nseq)
                    p3 = pbt[:, 0:TTt].rearrange("p (s t) -> p s t", s=nseq)
                    for k in range(8):
                        S.pe(r=[("Wup", ch // 11)] + hnk, w=[pbk]).matmul(pbt[:, 0:TTt], lhsT=Wup[:, k, ch * 128:(ch + 1) * 128], rhs=H[:, k, 0:TTt],
                                                              start=(k == 0), stop=(k == 7))
                    S.pool(r=[("car", ch)], w=[ukey]).tensor_copy(out=uv[:, :, 0:2], in_=car[:, ch, 0:nseq, :])
                    S.act(r=[pbk, ukey], w=[ukey]).activation(out=uv[:, :, 2:2 + L], in_=p3, func=AF.Copy)
                    fw = lambda jj, ch=ch: pp[:, P_FW + ch * 3 + jj:P_FW + ch * 3 + jj + 1]
                    S.act(r=[pbk, "pp"], w=[tkey]).activation(out=t3, in_=p3, func=AF.Identity, scale=fw(2), bias=pp[:, P_FB + ch:P_FB + ch + 1])
                    S.pool(r=[ukey], w=[("car", ch)]).tensor_copy(out=car[:, ch, 0:nseq, :], in_=uv[:, :, L:L + 2])
                    for jj in (1, 0):
                        S.dve(r=[ukey, "pp", tkey], w=[tkey]).scalar_tensor_tensor(out=t3, in0=uv[:, :, jj:jj + L], scalar=fw(jj), in1=t3,
                                                                                   op0=ALU.mult, op1=ALU.add)
                    if half == 0:
                        SG, sgk = sg[j % 2], ("sg", j % 2)
                        S.act(r=[tkey], w=[sgk]).activation(out=SG[:, 0:TTt], in_=T0[:, 0:TTt], func=AF.Silu)
                    else:
                        S.dve(r=[sgk, tkey], w=[("gb", j)]).tensor_tensor(out=gb[:, j, 0:TTt], in0=SG[:, 0:TTt], in1=T0[:, 0:TTt], op=ALU.mult)
                    yield
            if sample:
                S.dma("offn", r=cark).dma_start(out=D["o_ffn"][l], in_=car[:])
            elif tl["last"]:
                S.dma("pffn", r=cark).dma_start(out=D["p_ffn"][l], in_=car[:, :, 0, :])
            yield

        def down(ti):
            tl = tiles[ti]
            col0, nseq, L = tl["col0"], tl["nseq"], tl["L"]
            TTt = nseq * L
            slot = ti % 2
            X, xkey = xt[slot], ("xt", slot)
            gbk = [("gb", j) for j in range(NPAIR)]
            for m in range(8):
                bi = 4 + m % 4
                pbt, pbk = PB[bi], ("PB", bi)
                for j in range(NPAIR):
                    S.pe(r=["Wdn"] + gbk, w=[pbk]).matmul(pbt[:, 0:TTt], lhsT=Wdn[:, j, m * 128:(m + 1) * 128], rhs=gb[:, j, 0:TTt],
                                                          start=(j == 0), stop=(j == NPAIR - 1))
                    if j % 6 == 5:
                        yield
                S.dve(r=[pbk, xkey], w=[xkey]).tensor_tensor(out=X[:, m, 0:TTt], in0=pbt[:, 0:TTt], in1=X[:, m, 0:TTt], op=ALU.add)
                yield
            S.dma(("xst", slot), r=[xkey]).dma_start(out=kp(xdst)[:, :, col0:col0 + TTt], in_=X[:, :, 0:TTt])
            yield

        interleave(front(0))
        for ti in range(len(tiles)):
            interleave(up(ti))
            interleave(down(ti), front(ti + 1) if ti + 1 < len(tiles) else None)
        S.analyze().emit(nc)


def _consts():
    s = np.arange(128)[:, None]
    q = np.arange(128)[None, :]
    tri = (q >= s).astype(np.float32)
    ones = np.ones((128, 128), np.float32)
    slopes = 2.0 ** (-8.0 * np.arange(1, 9) / 8)
    E = np.zeros((128, 8, 2, 128), np.float64)
    for h in range(8):
        rel_prev = (q + 128 - s).astype(np.float64)
        E[:, h, 0, :] = np.where(s > q, np.exp(-slopes[h] * rel_prev), 0.0)
        rel_own = (q - s).astype(np.float64)
        E[:, h, 1, :] = np.where(s <= q, np.exp(-slopes[h] * rel_own), 0.0)
    cstf = np.concatenate([tri, ones, E.reshape(128, 2048).astype(np.float32)], axis=1)
    ident = np.eye(128, dtype=np.float32)
    bd = np.zeros((128, 128), np.float32)
    bd[:64, :64] = 1.0
    bd[64:, 64:] = 1.0
    cstb = np.concatenate([ident, ones, bd], axis=1)
    return np.ascontiguousarray(cstf), np.ascontiguousarray(cstb)


def _pack_params(inp):
    pp = np.zeros((DEPTH, 128, NPP), np.float32)
    p = np.arange(128)
    for l in range(DEPTH):
        pp[l, :, P_N1:P_N1 + 8] = inp["norm1_w"][l].reshape(8, 128).T
        pp[l, :, P_N2:P_N2 + 8] = inp["norm2_w"][l].reshape(8, 128).T
        cw = inp["ssd_conv_w"][l].reshape(4, 8, 128)
        pp[l, :, P_CW:P_CW + 32] = cw.transpose(2, 1, 0).reshape(128, 32)
        pp[l, :, P_CB:P_CB + 8] = inp["ssd_conv_b"][l].reshape(8, 128).T
        pp[l, :, P_DTB:P_DTB + 8] = inp["dt_bias"][l][None, :]
        pp[l, :, P_ALOG:P_ALOG + 8] = inp["a_log"][l][None, :]
        pp[l, :, P_DSK:P_DSK + 4] = inp["d_skip"][l].reshape(4, 2)[:, (p >= 64).astype(int)].T
        pp[l, :, P_SNW:P_SNW + 4] = inp["ssd_norm_w"][l].reshape(4, 128).T
        pp[l, :, P_QN] = inp["q_norm_w"][l][p % 64]
        pp[l, :, P_KN] = inp["k_norm_w"][l][p % 64]
        sk = inp["attn_sinks"][l]
        pp[l, :, P_SINK:P_SINK + 4] = np.stack([np.where(p < 64, sk[c], sk[4 + c]) for c in range(4)], axis=1)
        fw = inp["ffn_conv_w"][l].reshape(3, 44, 128)
        pp[l, :, P_FW:P_FW + 132] = fw.transpose(2, 1, 0).reshape(128, 132)
        pp[l, :, P_FB:P_FB + 44] = inp["ffn_conv_b"][l].reshape(44, 128).T
    return pp


def _perm_weights(inp):
    w_in = inp["w_in"]
    qcols = []
    for c in range(4):
        qcols += list(range(1544 + c * 64, 1544 + (c + 1) * 64))
        qcols += list(range(1544 + (4 + c) * 64, 1544 + (5 + c) * 64))
    cols = list(range(0, 1536)) + qcols + list(range(2056, 2312)) + list(range(1536, 1544))
    w_in_p = np.ascontiguousarray(w_in[:, :, cols])
    rows = list(range(512))
    for c in range(4):
        rows += list(range(512 + c * 64, 512 + (c + 1) * 64))
        rows += list(range(512 + (4 + c) * 64, 512 + (5 + c) * 64))
    w_out_p = np.ascontiguousarray(inp["w_out"][:, rows, :])
    return w_in_p, w_out_p


_NC_CACHE = {}


def kernel(**inp):
    inp = {k: np.asarray(v) for k, v in inp.items()}
    xp = inp["x_prompt"]
    B, TP, _ = xp.shape
    xs = inp["x_sample"]
    n_stages = int(inp.pop("_n_stages", 4)) if "_n_stages" in inp else 4
    debug = bool(inp.pop("_debug", False)) if "_debug" in inp else False
    key = (TP, n_stages, debug)
    if key not in _NC_CACHE:
        _NC_CACHE[key] = build_nc(TP, n_stages, debug)
    nc = _NC_CACHE[key]
    cstf, cstb = _consts()
    pp = _pack_params(inp)
    w_in_p, w_out_p = _perm_weights(inp)
    w_up = np.ascontiguousarray(inp["w_up"])
    w_down = np.ascontiguousarray(inp["w_down"])
    in_maps = []
    for c in range(NCORES):
        b = c % B
        sl = slice(c * NSQ, (c + 1) * NSQ)
        xT = np.concatenate([xp[b].T, xs[sl].reshape(TS, D_MODEL).T], axis=1)
        m = {
            "xT": np.ascontiguousarray(xT, dtype=np.float32),
            "w_in": w_in_p, "w_out": w_out_p, "w_up": w_up, "w_down": w_down,
            "pp": pp, "cstf": cstf, "cstb": cstb,
            "s_ssm": np.ascontiguousarray(inp["state_ssm"][:, sl].transpose(0, 1, 4, 2, 3).reshape(DEPTH, NSQ, 128, 512)),
            "s_conv": np.ascontiguousarray(inp["state_ssd_conv"][:, sl].reshape(DEPTH, NSQ, 3, 8, 128).transpose(0, 4, 3, 1, 2)),
            "s_kT": np.ascontiguousarray(inp["cache_swa_k"][:, sl].reshape(DEPTH, NSQ, 128, 128).transpose(0, 1, 3, 2)),
            "s_v": np.ascontiguousarray(inp["cache_swa_v"][:, sl].reshape(DEPTH, NSQ, 128, 128)),
            "s_ffn": np.ascontiguousarray(inp["state_ffn_conv"][:, sl].reshape(DEPTH, NSQ, 2, 44, 128).transpose(0, 4, 3, 1, 2)),
        }
        in_maps.append(m)
    res = run_bass_kernel_spmd(nc, in_maps, core_ids=list(range(NCORES)))
    R = res.results
    f32 = np.float32
    y_prompt = np.stack([R[b]["yT"][:, :TP].T for b in range(B)]).astype(f32)
    y_sample = np.concatenate([R[c]["yT"][:, TP:].T.reshape(NSQ, LS, D_MODEL) for c in range(NCORES)]).astype(f32)
    p_ssm = np.stack([R[b]["p_ssm"].reshape(DEPTH, 128, 8, 64).transpose(0, 2, 3, 1) for b in range(B)], axis=1)
    p_conv = np.stack([R[b]["p_conv"].transpose(0, 3, 2, 1).reshape(DEPTH, 3, 1024) for b in range(B)], axis=1)
    p_k = np.stack([R[b]["p_kT"].transpose(0, 2, 1).reshape(DEPTH, 128, 2, 64) for b in range(B)], axis=1)
    p_v = np.stack([R[b]["p_v"].reshape(DEPTH, 128, 2, 64) for b in range(B)], axis=1)
    p_ffn = np.stack([R[b]["p_ffn"].transpose(0, 3, 2, 1).reshape(DEPTH, 2, 2 * D_FF) for b in range(B)], axis=1)
    o_ssm = np.concatenate([R[c]["o_ssm"].reshape(DEPTH, NSQ, 128, 8, 64).transpose(0, 1, 3, 4, 2) for c in range(NCORES)], axis=1)
    o_conv = np.concatenate([R[c]["o_conv"].transpose(0, 3, 4, 2, 1).reshape(DEPTH, NSQ, 3, 1024) for c in range(NCORES)], axis=1)
    o_k = np.concatenate([R[c]["o_kT"].transpose(0, 1, 3, 2).reshape(DEPTH, NSQ, 128, 2, 64) for c in range(NCORES)], axis=1)
    o_v = np.concatenate([R[c]["o_v"].reshape(DEPTH, NSQ, 128, 2, 64) for c in range(NCORES)], axis=1)
    o_ffn = np.concatenate([R[c]["o_ffn"].transpose(0, 3, 4, 2, 1).reshape(DEPTH, NSQ, 2, 2 * D_FF) for c in range(NCORES)], axis=1)
    outs = (y_prompt, y_sample, p_ssm, p_conv, p_k, p_v, p_ffn, o_ssm, o_conv, o_k, o_v, o_ffn)
    outs = tuple(np.ascontiguousarray(o, dtype=f32) for o in outs)
    if debug:
        kernel._dbg = R
    return outs
```
